# Optimizing a Trainium2 kernel written in Bass

```python
import math
import jax, jax.numpy as jnp
from jax import lax
import numpy as np

D_MODEL = 1024
BATCH = 8
SEQ = 2048
DEPTH = 4

D_FF = 2816
N_NORMS = 6
EPS = 1e-6
HEAD_DIM = 64
CONV_CH = 512
CONV_K = 31
SWA_HQ = 8
SWA_HKV = 2
SWA_G = SWA_HQ // SWA_HKV
WINDOW = 128
BLK = 128
FOX_H = 16
AB_IN = 2 * CONV_CH + SWA_HQ * HEAD_DIM + 2 * SWA_HKV * HEAD_DIM
AB_OUT = CONV_CH + SWA_HQ * HEAD_DIM
FOX_IN = 3 * FOX_H * HEAD_DIM + FOX_H
FOX_OUT = FOX_H * HEAD_DIM
NEG = -1e30

kernel_name = "hybrid_conv_swa_fox_macaron"


def _rmsnorm(x, g):
    xf = x.astype(jnp.float32)
    y = xf * lax.rsqrt(jnp.mean(xf * xf, axis=-1, keepdims=True) + EPS)
    return (y * g.astype(jnp.float32)).astype(x.dtype)


def _swiglu(x, wg, wu, wd):
    return (jax.nn.silu(x @ wg) * (x @ wu)) @ wd


def _alibi_slopes(n):
    return jnp.asarray([2.0 ** (-8.0 * (i + 1) / n) for i in range(n)], dtype=jnp.float32)


def _conv_module(u, gate, w, b, ln_g, ln_b):
    a = u * jax.nn.sigmoid(gate)
    y = lax.conv_general_dilated(
        a, w.reshape(CONV_K, 1, CONV_CH).astype(a.dtype),
        window_strides=(1,), padding=[(CONV_K - 1, 0)],
        dimension_numbers=("NWC", "WIO", "NWC"),
        feature_group_count=CONV_CH) + b
    yf = y.astype(jnp.float32)
    mu = jnp.mean(yf, axis=-1, keepdims=True)
    var = jnp.mean(jnp.square(yf - mu), axis=-1, keepdims=True)
    yf = (yf - mu) * lax.rsqrt(var + EPS) * ln_g.astype(jnp.float32) + ln_b.astype(jnp.float32)
    return jax.nn.silu(yf).astype(u.dtype)


def _swa_sinks(q, k, v, sinks):
    b, s, _, dh = q.shape
    nb = s // BLK
    qb = q.reshape(b, nb, BLK, SWA_HKV, SWA_G, dh)
    kb = k.reshape(b, nb, BLK, SWA_HKV, dh)
    vb = v.reshape(b, nb, BLK, SWA_HKV, dh)
    pad = ((0, 0), (1, 0), (0, 0), (0, 0), (0, 0))
    kk = jnp.concatenate([jnp.pad(kb, pad)[:, :-1], kb], axis=2)
    vv = jnp.concatenate([jnp.pad(vb, pad)[:, :-1], vb], axis=2)
    sc = jnp.einsum("bnqhgd,bnkhd->bnhgqk", qb, kk).astype(jnp.float32) / math.sqrt(dh)
    t_loc = jnp.arange(BLK)[:, None]
    s_loc = jnp.arange(2 * BLK)[None, :]
    dist = t_loc + BLK - s_loc
    valid = (dist >= 0) & (dist < WINDOW)
    mask = valid[None] & ((jnp.arange(nb)[:, None, None] > 0) | (s_loc >= BLK)[None])
    slopes = _alibi_slopes(SWA_HQ).reshape(SWA_HKV, SWA_G)
    sc = sc - slopes[:, :, None, None] * dist.astype(jnp.float32)
    sc = jnp.where(mask[None, :, None, None], sc, NEG)
    sink = jnp.broadcast_to(sinks.astype(jnp.float32).reshape(SWA_HKV, SWA_G)[None, None, :, :, None, None],
                            sc.shape[:-1] + (1,))
    p = jax.nn.softmax(jnp.concatenate([sc, sink], axis=-1), axis=-1)[..., :-1]
    o = jnp.einsum("bnhgqk,bnkhd->bnqhgd", p.astype(v.dtype), vv)
    return o.reshape(b, s, SWA_HQ * dh)


def _fox_attention(q, k, v, zf, b_f):
    b, s, h, dh = q.shape
    nb = s // BLK
    qh = q.transpose(0, 2, 1, 3)
    kh = k.transpose(0, 2, 1, 3)
    vh = v.transpose(0, 2, 1, 3)
    logf = jax.nn.log_sigmoid(zf.astype(jnp.float32) + b_f.astype(jnp.float32))
    c = jnp.cumsum(logf.transpose(0, 2, 1), axis=-1)
    key_pos = jnp.arange(s)

    def one_block(i):
        start = i * BLK
        qs = lax.dynamic_slice_in_dim(qh, start, BLK, axis=2)
        cs = lax.dynamic_slice_in_dim(c, start, BLK, axis=2)
        sc = jnp.einsum("bhqd,bhkd->bhqk", qs, kh).astype(jnp.float32) / math.sqrt(dh)
        sc = sc + cs[..., :, None] - c[..., None, :]
        tpos = start + jnp.arange(BLK)
        sc = jnp.where(tpos[:, None] >= key_pos[None, :], sc, NEG)
        p = jax.nn.softmax(sc, axis=-1)
        return jnp.einsum("bhqk,bhkd->bhqd", p.astype(vh.dtype), vh)

    o = lax.map(one_block, jnp.arange(nb))
    return o.transpose(1, 0, 3, 2, 4).reshape(b, s, h * dh)


def _mixer_even(h, w_in, conv_w, conv_b, ln_g, ln_b, sinks, w_out):
    b, s, _ = h.shape
    z = h @ w_in
    o0 = CONV_CH
    o1 = o0 + CONV_CH
    o2 = o1 + SWA_HQ * HEAD_DIM
    o3 = o2 + SWA_HKV * HEAD_DIM
    u, gate = z[..., :o0], z[..., o0:o1]
    q = z[..., o1:o2].reshape(b, s, SWA_HQ, HEAD_DIM)
    k = z[..., o2:o3].reshape(b, s, SWA_HKV, HEAD_DIM)
    v = z[..., o3:].reshape(b, s, SWA_HKV, HEAD_DIM)
    a_out = _conv_module(u, gate, conv_w, conv_b, ln_g, ln_b)
    b_out = _swa_sinks(q, k, v, sinks)
    return jnp.concatenate([a_out, b_out], axis=-1) @ w_out


def _mixer_odd(h, w_in, b_f, w_out):
    b, s, _ = h.shape
    z = h @ w_in
    hd = FOX_H * HEAD_DIM
    q = z[..., :hd].reshape(b, s, FOX_H, HEAD_DIM)
    k = z[..., hd:2 * hd].reshape(b, s, FOX_H, HEAD_DIM)
    v = z[..., 2 * hd:3 * hd].reshape(b, s, FOX_H, HEAD_DIM)
    zf = z[..., 3 * hd:]
    return _fox_attention(q, k, v, zf, b_f) @ w_out


def setup_inputs(seed: int = 0) -> dict:
    key = jax.random.key(seed)
    ks = jax.random.split(key, 16)
    n_even = (DEPTH + 1) // 2
    n_odd = DEPTH // 2
    f32 = jnp.float32
    nrm = lambda k, shp, fan: jax.random.normal(k, shp, f32) * fan ** -0.5
    return {
        "x": jax.random.normal(ks[0], (BATCH, SEQ, D_MODEL), f32),
        "norm_g": 1.0 + 0.05 * jax.random.normal(ks[1], (DEPTH, N_NORMS, D_MODEL), f32),
        "ffn_w_gate": nrm(ks[2], (DEPTH, 2, D_MODEL, D_FF), D_MODEL),
        "ffn_w_up": nrm(ks[3], (DEPTH, 2, D_MODEL, D_FF), D_MODEL),
        "ffn_w_down": nrm(ks[4], (DEPTH, 2, D_FF, D_MODEL), D_FF),
        "ab_w_in": nrm(ks[5], (n_even, D_MODEL, AB_IN), D_MODEL),
        "conv_w": nrm(ks[6], (n_even, CONV_K, CONV_CH), CONV_K),
        "conv_b": 0.02 * jax.random.normal(ks[7], (n_even, CONV_CH), f32),
        "conv_ln_g": 1.0 + 0.05 * jax.random.normal(ks[8], (n_even, CONV_CH), f32),
        "conv_ln_b": 0.02 * jax.random.normal(ks[9], (n_even, CONV_CH), f32),
        "swa_sinks": 0.5 * jax.random.normal(ks[10], (n_even, SWA_HQ), f32),
        "ab_w_out": nrm(ks[11], (n_even, AB_OUT, D_MODEL), AB_OUT),
        "fox_w_in": nrm(ks[12], (n_odd, D_MODEL, FOX_IN), D_MODEL),
        "fox_b_f": jax.random.uniform(ks[13], (n_odd, FOX_H), f32, 1.0, 6.0),
        "fox_w_out": nrm(ks[14], (n_odd, FOX_OUT, D_MODEL), FOX_OUT),
    }


def reference(x, norm_g, ffn_w_gate, ffn_w_up, ffn_w_down, ab_w_in, conv_w, conv_b,
              conv_ln_g, conv_ln_b, swa_sinks, ab_w_out, fox_w_in, fox_b_f, fox_w_out):
    for l in range(DEPTH):
        g = norm_g[l]
        h = _swiglu(_rmsnorm(x, g[0]), ffn_w_gate[l, 0], ffn_w_up[l, 0], ffn_w_down[l, 0])
        x = x + 0.5 * _rmsnorm(h, g[1])
        h = _rmsnorm(x, g[2])
        if l % 2 == 0:
            i = l // 2
            h = _mixer_even(h, ab_w_in[i], conv_w[i], conv_b[i], conv_ln_g[i], conv_ln_b[i],
                            swa_sinks[i], ab_w_out[i])
        else:
            i = l // 2
            h = _mixer_odd(h, fox_w_in[i], fox_b_f[i], fox_w_out[i])
        x = x + _rmsnorm(h, g[3])
        h = _swiglu(_rmsnorm(x, g[4]), ffn_w_gate[l, 1], ffn_w_up[l, 1], ffn_w_down[l, 1])
        x = x + 0.5 * _rmsnorm(h, g[5])
    return x
```

```python
import contextlib
import numpy as np
import concourse.bass as bass
import concourse.mybir as mybir
from concourse.bass_utils import run_bass_kernel_spmd

F32 = mybir.dt.float32
BF16 = mybir.dt.bfloat16
AF = mybir.ActivationFunctionType
ALU = mybir.AluOpType
ESZ = {F32: 4, BF16: 2}

D = 1024
S = 2048
DFF = 2816
KC = D // 128
FCH = DFF // 128
DEPTH = 4
EPS = 1e-6
NCORES = 8
TT = 512
TG = 1024
SLOT = 6144
NSLOT = 3
ARENA = 96 * 1024
NCST = 474
HD = 64
CONV_K = 31
NEGBIG = -30000.0


def _esz(dt):
    return ESZ[dt]


class Sched:
    def __init__(self, nc, es):
        self.nc = nc
        self.es = es
        self.eng = {}
        for name, obj in (("pe", nc.tensor), ("act", nc.scalar), ("dve", nc.vector),
                          ("pool", nc.gpsimd), ("sp", nc.sync)):
            sem = es.enter_context(nc.semaphore("sem_" + name))
            self.eng[name] = dict(obj=obj, sem=sem, cnt=0, waited={})
        self.recs = {}
        self.nwaits = 0
        self.nops = 0

    @staticmethod
    def region(ap):
        name = ap.tensor.name
        a = ap.ap
        esz = _esz(ap.dtype)
        pstep, pcnt = a[0]
        off = ap.offset
        if pstep > 0:
            p0 = off // pstep
            lo = off % pstep
        else:
            p0 = 0
            lo = off
        hi = lo + 1
        for st, c in a[1:]:
            hi += (c - 1) * abs(st)
        return (name, p0, p0 + pcnt, lo * esz, hi * esz)

    @staticmethod
    def _ov(r, q):
        return r[1] < q[2] and q[1] < r[2] and r[3] < q[4] and q[3] < r[4]

    @staticmethod
    def _contains(outer, inner):
        return (outer[1] <= inner[1] and inner[2] <= outer[2]
                and outer[3] <= inner[3] and inner[4] <= outer[4])

    def _collect(self, reads, writes):
        deps = {}

        def add(tok):
            h, v = tok[0], tok[1]
            k = h.name
            if k not in deps or deps[k][1] < v:
                deps[k] = (h, v)

        rregs = [self.region(a) for a in reads]
        wregs = [self.region(a) for a in writes]
        for r in rregs:
            rec = self.recs.get(r[0])
            if rec:
                for q, tok in rec["w"]:
                    if self._ov(r, q):
                        add(tok)
        for r in wregs:
            rec = self.recs.get(r[0])
            if rec:
                for q, tok in rec["w"]:
                    if self._ov(r, q):
                        add(tok)
                for q, tok in rec["r"]:
                    if self._ov(r, q):
                        add(tok)
        return deps, rregs, wregs

    def _emit_waits(self, E, deps, skip_self):
        for k, (h, v) in deps.items():
            if skip_self and h is E["sem"]:
                continue
            if E["waited"].get(k, 0) >= v:
                continue
            E["obj"].wait_ge(h, v)
            E["waited"][k] = v
            self.nwaits += 1

    def _record(self, rregs, wregs, tok):
        for r in rregs:
            rec = self.recs.setdefault(r[0], {"w": [], "r": []})
            found = False
            for i, (q, t) in enumerate(rec["r"]):
                if q == r and t[0] is tok[0]:
                    if t[1] < tok[1]:
                        rec["r"][i] = (q, tok)
                    found = True
                    break
            if not found:
                rec["r"].append((r, tok))
        for r in wregs:
            rec = self.recs.setdefault(r[0], {"w": [], "r": []})
            rec["w"] = [(q, t) for (q, t) in rec["w"] if not self._contains(r, q)]
            rec["r"] = [(q, t) for (q, t) in rec["r"] if not self._contains(r, q)]
            rec["w"].append((r, tok))

    def op(self, eng, fn, reads=(), writes=(), signal=True, extra=()):
        E = self.eng[eng]
        deps, rregs, wregs = self._collect(reads, writes)
        for tok in extra:
            k = tok[0].name
            if k not in deps or deps[k][1] < tok[1]:
                deps[k] = (tok[0], tok[1])
        self._emit_waits(E, deps, skip_self=(eng == "pe"))
        ins = fn(E["obj"])
        self.nops += 1
        if signal:
            E["cnt"] += 1
            ins.then_inc(E["sem"], 1)
            tok = [E["sem"], E["cnt"]]
        else:
            tok = [E["sem"], E["cnt"] + 1]
        self._record(rregs, wregs, tok)
        return tok

    def new_dsem(self, name):
        sem = self.es.enter_context(self.nc.semaphore(name))
        return dict(sem=sem, cnt=0)

    def dma(self, queue, ds, out, in_, tok, sb_reads=(), sb_writes=(), extra=()):
        E = self.eng[queue]
        deps, rregs, wregs = self._collect(sb_reads, sb_writes)
        for t in extra:
            k = t[0].name
            if k not in deps or deps[k][1] < t[1]:
                deps[k] = (t[0], t[1])
        self._emit_waits(E, deps, skip_self=False)
        E["obj"].dma_start(out=out, in_=in_).then_inc(ds["sem"], 16)
        ds["cnt"] += 16
        tok[0] = ds["sem"]
        tok[1] = ds["cnt"]
        self._record(rregs, wregs, tok)
        self.nops += 1


class WeightPool:
    def __init__(self, sch, ring, nslot, slot_elems):
        self.sch = sch
        self.ring = ring
        self.nslot = nslot
        self.slot = slot_elems
        self.plan = []
        self.next_load = 0
        self.next_get = 0
        self.free = list(range(nslot))
        self.slot_of = {}
        self.dsems = [sch.new_dsem(f"wsem{i}") for i in range(nslot)]

    def add(self, tile):
        self.plan.append(tile)

    def view(self, slot, off, shape):
        n = 1
        for s in shape:
            n *= s
        base = slot * self.slot + off
        v = self.ring[:, base:base + n]
        if len(shape) == 2:
            v = v.rearrange("p (a b) -> p a b", a=shape[0])
        elif len(shape) == 3:
            v = v.rearrange("p (a b c) -> p a b c", a=shape[0], b=shape[1])
        return v

    def prefetch(self):
        while self.free and self.next_load < len(self.plan):
            slot = self.free.pop(0)
            idx = self.next_load
            self.next_load += 1
            self.slot_of[idx] = slot
            tok = [None, 0]
            for (dst_fn, src) in self.plan[idx]:
                dst = dst_fn(slot)
                self.sch.dma("pool", self.dsems[slot], dst, src, tok, sb_writes=[dst])

    def get(self):
        self.prefetch()
        idx = self.next_get
        self.next_get += 1
        if idx not in self.slot_of:
            self.prefetch()
        assert idx in self.slot_of, "weight pool starved (release missing?)"
        return idx, self.slot_of[idx]

    def release(self, idx):
        self.free.append(self.slot_of[idx])
        self.prefetch()


class Prog:
    def __init__(self, layers, subs=("f1", "mix", "f2")):
        self.layers = list(layers)
        self.subs = subs
        self.nc = bass.Bass("TRN2", target_bir_lowering=False)
        self.es = contextlib.ExitStack()
        self.pp = 0
        import os
        self.stop = int(os.environ.get('KSTOP', '99'))
        self.kswa = int(os.environ.get('KSWA', '99'))
        self.kn = int(os.environ.get('KN', '99'))

    def dram_in(self, name, shape):
        return self.nc.dram_tensor(name, list(shape), F32, kind="ExternalInput").ap()

    def build(self):
        nc = self.nc
        with self.es as es:
            self.sch = Sched(nc, es)
            sch = self.sch
            self.xT = self.dram_in("xT", [D, S])
            self.cst_d = self.dram_in("cst", [128, NCST])
            self.mswa_d = self.dram_in("mswa", [128, 2048])
            self.mc_d = self.dram_in("mc", [128, 128])
            self.sel_d = self.dram_in("sel", [16, 2048])
            self.ident_d = self.dram_in("ident", [16, 16])
            self.win, self.wout = {}, {}
            if "mix" in self.subs:
                for l in self.layers:
                    self.win[l] = self.dram_in(f"win{l}", [D, 1792 if l % 2 == 0 else 3088])
                    self.wout[l] = self.dram_in(f"wout{l}", [D, D])
            self.wg, self.wu, self.wd = {}, {}, {}
            for l in self.layers:
                for j, sub in ((0, "f1"), (1, "f2")):
                    if sub in self.subs:
                        self.wg[l, j] = self.dram_in(f"wg{l}{j}", [D, DFF])
                        self.wu[l, j] = self.dram_in(f"wu{l}{j}", [D, DFF])
                        self.wd[l, j] = self.dram_in(f"wd{l}{j}", [DFF, D])
            self.yT = nc.dram_tensor("yT", [D, S], F32, kind="ExternalOutput").ap()
            sb = lambda n, shp, dt: es.enter_context(nc.sbuf_tensor(n, shp, dt))
            self.xs = sb("xs", [128, KC, S], F32)
            self.ring = sb("ring", [128, NSLOT * SLOT], BF16)
            self.cst = sb("cst_sb", [128, NCST], F32)
            self.gT = self.cst[:, 0:DEPTH * 6 * KC]
            self.mswa = sb("mswa_sb", [128, 2048], BF16)
            self.mc = sb("mc_sb", [128, 128], F32)
            self.sel = sb("sel_sb", [16, 2048], BF16)
            self.ident = sb("ident_sb", [16, 16], F32)
            self.g32 = sb("g32", [128, DEPTH * 6 * KC], F32)
            self.ones = sb("ones", [128, 128], BF16)
            self.arena = sb("arena", [128, ARENA], mybir.dt.uint8)
            self.ps = [es.enter_context(nc.psum_tensor(f"ps{i}", [128, TT], F32)) for i in range(8)]
            self.wp = WeightPool(sch, self.ring, NSLOT, SLOT)
            self.ds_x = sch.new_dsem("ds_x")
            self.ds_c = sch.new_dsem("ds_c")
            self.ds_c2 = sch.new_dsem("ds_c2")
            self.ds_o = sch.new_dsem("ds_o")

            for l in self.layers:
                for sub in self.subs:
                    if sub == "f1":
                        self.plan_ffn(l, 0)
                    elif sub == "f2":
                        self.plan_ffn(l, 1)
                    elif sub == "mix":
                        (self.plan_even if l % 2 == 0 else self.plan_odd)(l)

            tok = [None, 0]
            for k in range(KC):
                sch.dma("sp", self.ds_x, self.xs[:, k, :], self.xT[k * 128:(k + 1) * 128, :], tok,
                        sb_writes=[self.xs[:, k, :]])
            tok = [None, 0]
            sch.dma("sp", self.ds_c, self.cst[:], self.cst_d, tok, sb_writes=[self.cst[:]])
            sch.dma("sp", self.ds_c, self.mc[:], self.mc_d, tok, sb_writes=[self.mc[:]])
            sch.dma("sp", self.ds_c, self.ident[:], self.ident_d, tok, sb_writes=[self.ident[:]])
            tok = [None, 0]
            sch.dma("pool", self.ds_c2, self.mswa[:], self.mswa_d, tok, sb_writes=[self.mswa[:]])
            sch.dma("pool", self.ds_c2, self.sel[:], self.sel_d, tok, sb_writes=[self.sel[:]])
            sch.op("dve", lambda e: e.memset(self.ones[:], 1.0), writes=[self.ones[:]])
            self.wp.prefetch()
            gv = self.gT.rearrange("p (l n k) -> p l n k", l=DEPTH, n=6)
            g32v = self.g32[:].rearrange("p (l n k) -> p l n k", l=DEPTH, n=6)
            for n in range(6):
                f = 16.0 if n in (1, 5) else 32.0
                sch.op("dve", lambda e, n=n, f=f: e.tensor_scalar(
                    out=g32v[:, :, n, :], in0=gv[:, :, n, :], scalar1=f, scalar2=None, op0=ALU.mult),
                    reads=[gv[:, :, n, :]], writes=[g32v[:, :, n, :]])

            for l in self.layers:
                for sub in self.subs:
                    if sub == "f1":
                        self.ffn(l, 0)
                    elif sub == "f2":
                        self.ffn(l, 1)
                    elif sub == "mix":
                        (self.mixer_even if l % 2 == 0 else self.mixer_odd)(l)

            tok = [None, 0]
            for k in range(KC):
                sch.dma("sp", self.ds_o, self.yT[k * 128:(k + 1) * 128, :], self.xs[:, k, :], tok,
                        sb_reads=[self.xs[:, k, :]])
            nc.sync.wait_ge(self.ds_o["sem"], self.ds_o["cnt"])
        return nc

    def gcol(self, l, n, k):
        c = (l * 6 + n) * KC + k
        return self.g32[:, c:c + 1]

    def carve(self, off, shape, dt):
        n = 1
        for s in shape:
            n *= s
        nb = n * _esz(dt)
        v = self.arena[:, off:off + nb].bitcast(dt)
        if len(shape) == 2:
            v = v.rearrange("p (a b) -> p a b", a=shape[0])
        elif len(shape) == 3:
            v = v.rearrange("p (a b c) -> p a b c", a=shape[0], b=shape[1])
        return v, off + nb

    GU_STAGES = [(0, 3), (3, 6), (6, 9), (9, 12), (12, 15), (15, 18), (18, 21), (21, 22)]

    def plan_ffn(self, l, j):
        wg = self.wg[l, j].rearrange("(k p) c -> p k c", p=128)
        wu = self.wu[l, j].rearrange("(k p) c -> p k c", p=128)
        wd = self.wd[l, j].rearrange("(f p) c -> p f c", p=128)
        for g in range(S // TG):
            for (c0, c1) in self.GU_STAGES:
                n = (c1 - c0) * 128
                self.wp.add([(lambda s_, n=n: self.wp.view(s_, 0, [KC, n]), wg[:, :, c0 * 128:c1 * 128]),
                             (lambda s_, n=n: self.wp.view(s_, KC * n, [KC, n]), wu[:, :, c0 * 128:c1 * 128])])
            for tt in range(TG // TT):
                for dp in range(4):
                    self.wp.add([(lambda s_: self.wp.view(s_, 0, [FCH, 256]), wd[:, :, dp * 256:(dp + 1) * 256])])

    def rstd_from_stats(self, st_ps, rs):
        self.sch.op("act", lambda e: e.activation(out=rs, in_=st_ps, func=AF.Sqrt, bias=float(D * EPS), scale=1.0),
                    reads=[st_ps], writes=[rs])
        self.sch.op("dve", lambda e: e.reciprocal(out=rs, in_=rs), reads=[rs], writes=[rs])

    def prenorm_tile(self, l, n, t0, hdst, sqb, rs, st_ps):
        sch = self.sch
        xin = self.xs[:, :, t0:t0 + TT]
        sch.op("act", lambda e: e.activation(out=sqb, in_=xin, func=AF.Square), reads=[xin], writes=[sqb])
        for k in range(KC):
            sch.op("pe", lambda e, k=k: e.matmul(st_ps, lhsT=self.ones[:], rhs=sqb[:, k, :],
                                                 start=(k == 0), stop=(k == KC - 1)),
                   reads=[self.ones[:], sqb[:, k, :]], writes=[st_ps], signal=(k == KC - 1))
        self.rstd_from_stats(st_ps, rs)
        for k in range(KC):
            xi = self.xs[:, k, t0:t0 + TT]
            sch.op("dve", lambda e, k=k, xi=xi: e.scalar_tensor_tensor(
                out=hdst[:, k, :], in0=xi, scalar=self.gcol(l, n, k), in1=rs, op0=ALU.mult, op1=ALU.mult),
                reads=[xi, self.gcol(l, n, k), rs], writes=[hdst[:, k, :]])

    def postnorm_update(self, l, n, t0, hout, rs):
        sch = self.sch
        for k in range(KC):
            xi = self.xs[:, k, t0:t0 + TT]
            hk = hout[:, k, :]
            sch.op("dve", lambda e, k=k, hk=hk: e.scalar_tensor_tensor(
                out=hk, in0=hk, scalar=self.gcol(l, n, k), in1=rs, op0=ALU.mult, op1=ALU.mult),
                reads=[hk, self.gcol(l, n, k), rs], writes=[hk])
            sch.op("dve", lambda e, xi=xi, hk=hk: e.tensor_tensor(out=xi, in0=xi, in1=hk, op=ALU.add),
                   reads=[xi, hk], writes=[xi])

    def ffn(self, l, j):
        sch = self.sch
        wp = self.wp
        n_pre, n_post = (0, 1) if j == 0 else (4, 5)
        ntt = TG // TT
        off = 0
        hT, off = self.carve(off, [KC, TG], BF16)
        hout = self.arena[:, 0:KC * TT * 4].bitcast(F32).rearrange("p (a b) -> p a b", a=KC)
        actT, off = self.carve(off, [FCH, TG], BF16)
        sqb, off = self.carve(off, [KC, TT], BF16)
        rs, off = self.carve(off, [TT], F32)
        rs2, off = self.carve(off, [TT], F32)
        sgt = []
        for i in range(2):
            v, off = self.carve(off, [TT], BF16)
            sgt.append(v)
        sqh = []
        for i in range(2):
            v, off = self.carve(off, [TT], BF16)
            sqh.append(v)
        assert off <= ARENA
        psG = [self.ps[0][:], self.ps[1][:]]
        psU = [self.ps[2][:], self.ps[3][:]]
        psD = [self.ps[4][:], self.ps[5][:]]
        psS = [self.ps[6][:], self.ps[7][:]]
        it = 0
        for g in range(S // TG):
            g0 = g * TG
            for tt in range(ntt):
                self.prenorm_tile(l, n_pre, g0 + tt * TT, hT[:, :, tt * TT:(tt + 1) * TT], sqb, rs,
                                  psS[tt % 2])
            for (c0, c1) in self.GU_STAGES:
                idx, slot = wp.get()
                n = (c1 - c0) * 128
                wgv = wp.view(slot, 0, [KC, n])
                wuv = wp.view(slot, KC * n, [KC, n])
                for c in range(c0, c1):
                    cl = (c - c0) * 128
                    for tt in range(ntt):
                        b = it % 2
                        it += 1
                        hsl = hT[:, :, tt * TT:(tt + 1) * TT]
                        for (wv, pst) in ((wgv, psG[b]), (wuv, psU[b])):
                            for k in range(KC):
                                sch.op("pe", lambda e, k=k, wv=wv, pst=pst: e.matmul(
                                    pst, lhsT=wv[:, k, cl:cl + 128], rhs=hsl[:, k, :],
                                    start=(k == 0), stop=(k == KC - 1)),
                                    reads=[wv[:, k, cl:cl + 128], hsl[:, k, :]], writes=[pst],
                                    signal=(k == KC - 1))
                        sch.op("act", lambda e, b=b: e.activation(out=sgt[b], in_=psG[b], func=AF.Silu),
                               reads=[psG[b]], writes=[sgt[b]])
                        dst = actT[:, c, tt * TT:(tt + 1) * TT]
                        sch.op("dve", lambda e, b=b, dst=dst: e.tensor_tensor(
                            out=dst, in0=psU[b], in1=sgt[b], op=ALU.mult),
                            reads=[psU[b], sgt[b]], writes=[dst])
                wp.release(idx)
            for tt in range(ntt):
                t0 = g0 + tt * TT
                asl = actT[:, :, tt * TT:(tt + 1) * TT]
                stp = psS[tt % 2]
                for dp in range(4):
                    idx, slot = wp.get()
                    wdv = wp.view(slot, 0, [FCH, 256])
                    for dd in range(2):
                        dc = dp * 2 + dd
                        b = dc % 2
                        for f in range(FCH):
                            sch.op("pe", lambda e, f=f, b=b, dd=dd: e.matmul(
                                psD[b], lhsT=wdv[:, f, dd * 128:(dd + 1) * 128], rhs=asl[:, f, :],
                                start=(f == 0), stop=(f == FCH - 1)),
                                reads=[wdv[:, f, dd * 128:(dd + 1) * 128], asl[:, f, :]], writes=[psD[b]],
                                signal=(f == FCH - 1))
                        hk = hout[:, dc, :]
                        sch.op("act", lambda e, b=b, hk=hk: e.activation(out=hk, in_=psD[b], func=AF.Copy),
                               reads=[psD[b]], writes=[hk])
                        sch.op("act", lambda e, b=b: e.activation(out=sqh[b], in_=psD[b], func=AF.Square),
                               reads=[psD[b]], writes=[sqh[b]])
                        sch.op("pe", lambda e, b=b, dc=dc: e.matmul(
                            stp, lhsT=self.ones[:], rhs=sqh[b], start=(dc == 0), stop=(dc == KC - 1)),
                            reads=[self.ones[:], sqh[b]], writes=[stp], signal=True)
                    wp.release(idx)
                self.rstd_from_stats(stp, rs2)
                self.postnorm_update(l, n_post, t0, hout, rs2)


    def plan_proj_out(self, w):
        wv = w.rearrange("(k p) c -> p k c", p=128)
        for hh in range(2):
            self.wp.add([(lambda s_: self.wp.view(s_, 0, [KC, 512]), wv[:, :, hh * 512:(hh + 1) * 512])])

    def fm_chunk(self, wv, hT, evac):
        sch = self.sch
        for tt in range(S // TT):
            pst = self.ps[self.pp % 2][:]
            self.pp += 1
            for k in range(KC):
                rhs = hT[:, k, tt * TT:(tt + 1) * TT]
                sch.op("pe", lambda e, k=k, rhs=rhs, pst=pst: e.matmul(
                    pst, lhsT=wv[:, k, :], rhs=rhs, start=(k == 0), stop=(k == KC - 1)),
                    reads=[wv[:, k, :], rhs], writes=[pst], signal=(k == KC - 1))
            evac(tt, pst)

    def prenorm_full(self, l, n, hT, sqb, rs):
        for tt in range(S // TT):
            self.prenorm_tile(l, n, tt * TT, hT[:, :, tt * TT:(tt + 1) * TT], sqb, rs, self.ps[6 + tt % 2][:])

    def proj_out_post(self, l, rhs_fn, hout, sqh, rs2):
        sch, wp = self.sch, self.wp
        t1 = wp.get()
        t2 = wp.get()
        psD = [self.ps[0][:], self.ps[1][:]]
        stp = [self.ps[2][:], self.ps[3][:]]
        for tt in range(S // TT):
            st = stp[tt % 2]
            for dc in range(KC):
                tile = t1 if dc < 4 else t2
                wv = wp.view(tile[1], 0, [KC, 512])
                b = dc % 2
                c0 = (dc % 4) * 128
                for k in range(KC):
                    rhs = rhs_fn(k, tt)
                    sch.op("pe", lambda e, k=k, rhs=rhs, wv=wv, b=b, c0=c0: e.matmul(
                        psD[b], lhsT=wv[:, k, c0:c0 + 128], rhs=rhs, start=(k == 0), stop=(k == KC - 1)),
                        reads=[wv[:, k, c0:c0 + 128], rhs], writes=[psD[b]], signal=(k == KC - 1))
                hk = hout[:, dc, :]
                sch.op("act", lambda e, b=b, hk=hk: e.activation(out=hk, in_=psD[b], func=AF.Copy),
                       reads=[psD[b]], writes=[hk])
                sch.op("act", lambda e, b=b: e.activation(out=sqh[b], in_=psD[b], func=AF.Square),
                       reads=[psD[b]], writes=[sqh[b]])
                sch.op("pe", lambda e, b=b, dc=dc, st=st: e.matmul(
                    st, lhsT=self.ones[:], rhs=sqh[b], start=(dc == 0), stop=(dc == KC - 1)),
                    reads=[self.ones[:], sqh[b]], writes=[st], signal=True)
            self.rstd_from_stats(st, rs2)
            self.postnorm_update(l, 3, tt * TT, hout, rs2)
        wp.release(t1[0])
        wp.release(t2[0])

    def plan_even(self, l):
        w = self.win[l].rearrange("(k p) c -> p k c", p=128)
        V = self.wp.view
        self.wp.add([(lambda s_: V(s_, 0, [KC, 768]), w[:, :, 0:768])])
        self.wp.add([(lambda s_: V(s_, 0, [KC, 768]), w[:, :, 768:1536])])
        t3 = []
        for g in range(2):
            for hh in range(2):
                t3.append((lambda s_, g=g, hh=hh: V(s_, g * 1024, [KC, 128])[:, :, hh * 64:(hh + 1) * 64],
                           w[:, :, 1536 + g * 64:1536 + (g + 1) * 64]))
        t3.append((lambda s_: V(s_, 2048, [KC, 128]), w[:, :, 1664:1792]))
        self.wp.add(t3)
        self.plan_proj_out(self.wout[l])

    def mixer_even(self, l):
        sch, wp = self.sch, self.wp
        i = l // 2
        cb0 = DEPTH * 6 * KC + i * 140
        cst = self.cst
        cwv = cst[:, cb0:cb0 + 124].rearrange("p (c j) -> p c j", c=4)
        cbv = cst[:, cb0 + 124:cb0 + 128]
        lgv = cst[:, cb0 + 128:cb0 + 132]
        lbv = cst[:, cb0 + 132:cb0 + 136]
        skv = cst[:, cb0 + 136:cb0 + 140]
        K32 = 32 * 1024
        hT = self.carve(0, [KC, S], BF16)[0]
        y = self.carve(0, [4, S], F32)[0]
        boT = self.carve(0, [4, S], BF16)[0]
        hout = self.carve(16 * 1024, [KC, TT], F32)[0]
        aT, off = self.carve(K32, [4, S + 30], BF16)
        coT = self.carve(K32, [4, S], BF16)[0]
        off = (off + 63) // 64 * 64
        qT, off = self.carve(off, [4, S], BF16)
        kdT, off = self.carve(off, [2, S], BF16)
        vT, off = self.carve(off, [16, 128], BF16)
        T0 = off
        assert T0 + 12 * 1024 <= ARENA, T0
        sqb, o = self.carve(T0, [KC, TT], BF16)
        rs, o = self.carve(o, [TT], F32)
        self.prenorm_full(l, 2, hT, sqb, rs)
        if self.stop <= 1:
            return
        sgs = []
        o = T0
        for _ in range(2):
            v, o = self.carve(o, [S], BF16)
            sgs.append(v)
        t1 = wp.get()
        t2 = wp.get()

        def wview(col):
            tile = t1 if col < 768 else t2
            v = wp.view(tile[1], 0, [KC, 768])
            lc = col % 768
            return v[:, :, lc:lc + 128]

        for c in range(4):
            sch.op("dve", lambda e, c=c: e.memset(aT[:, c, 0:30], 0.0), writes=[aT[:, c, 0:30]])
        for c in range(4):
            sg = sgs[c % 2]
            self.fm_chunk(wview(512 + 128 * c), hT, lambda tt, p, sg=sg: sch.op(
                "act", lambda e: e.activation(out=sg[:, tt * TT:(tt + 1) * TT], in_=p, func=AF.Sigmoid),
                reads=[p], writes=[sg[:, tt * TT:(tt + 1) * TT]]))
            self.fm_chunk(wview(128 * c), hT, lambda tt, p, sg=sg, c=c: sch.op(
                "dve", lambda e: e.tensor_tensor(out=aT[:, c, 30 + tt * TT:30 + (tt + 1) * TT], in0=p,
                                                 in1=sg[:, tt * TT:(tt + 1) * TT], op=ALU.mult),
                reads=[p, sg[:, tt * TT:(tt + 1) * TT]], writes=[aT[:, c, 30 + tt * TT:30 + (tt + 1) * TT]]))
        wp.release(t1[0])
        for pr in range(4):
            self.fm_chunk(wview(1024 + 128 * pr), hT, lambda tt, p, pr=pr: sch.op(
                "act", lambda e: e.activation(out=qT[:, pr, tt * TT:(tt + 1) * TT], in_=p, func=AF.Copy),
                reads=[p], writes=[qT[:, pr, tt * TT:(tt + 1) * TT]]))
        wp.release(t2[0])
        t3 = wp.get()
        for g in range(2):
            wv = wp.view(t3[1], g * 1024, [KC, 128])
            self.fm_chunk(wv, hT, lambda tt, p, g=g: sch.op(
                "dve", lambda e: e.tensor_copy(out=kdT[:, g, tt * TT:(tt + 1) * TT], in_=p),
                reads=[p], writes=[kdT[:, g, tt * TT:(tt + 1) * TT]]))
        wvv = wp.view(t3[1], 2048, [KC, 128])
        for tb4 in range(4):
            pst = self.ps[self.pp % 2][:]
            self.pp += 1
            for jb in range(4):
                tb = tb4 * 4 + jb
                po = pst[:, jb * 128:(jb + 1) * 128]
                for k in range(KC):
                    lh = hT[:, k, tb * 128:(tb + 1) * 128]
                    sch.op("pe", lambda e, k=k, lh=lh, po=po: e.matmul(
                        po, lhsT=lh, rhs=wvv[:, k, :], start=(k == 0), stop=(k == KC - 1)),
                        reads=[lh, wvv[:, k, :]], writes=[po], signal=(k == KC - 1))
            dst = vT[:, tb4 * 4:(tb4 + 1) * 4, :]
            src = pst.rearrange("p (a b) -> p a b", a=4)
            sch.op("act", lambda e, dst=dst, src=src: e.activation(out=dst, in_=src, func=AF.Copy),
                   reads=[pst], writes=[dst])
        wp.release(t3[0])
        if self.stop <= 2:
            return
        for j in range(CONV_K):
            for c in range(4):
                src = aT[:, c, j:j + S]
                yc = y[:, c, :]
                if j == 0:
                    sch.op("dve", lambda e, c=c, src=src, yc=yc: e.tensor_scalar(
                        out=yc, in0=src, scalar1=cwv[:, c, 0:1], scalar2=cbv[:, c:c + 1],
                        op0=ALU.mult, op1=ALU.add),
                        reads=[src, cwv[:, c, 0:1], cbv[:, c:c + 1]], writes=[yc])
                else:
                    sch.op("dve", lambda e, c=c, j=j, src=src, yc=yc: e.scalar_tensor_tensor(
                        out=yc, in0=src, scalar=cwv[:, c, j:j + 1], in1=yc, op0=ALU.mult, op1=ALU.add),
                        reads=[src, cwv[:, c, j:j + 1], yc], writes=[yc])
        if self.stop <= 3:
            return
        o = T0
        yb, o = self.carve(o, [4, TT], BF16)
        ysq, o = self.carve(o, [4, TT], BF16)
        mu, o = self.carve(o, [TT], F32)
        var, o = self.carve(o, [TT], F32)
        for tt in range(S // TT):
            ysl = y[:, :, tt * TT:(tt + 1) * TT]
            s0 = self.ps[2 + 2 * (tt % 2)][:]
            s1 = self.ps[3 + 2 * (tt % 2)][:]
            sch.op("act", lambda e, ysl=ysl: e.activation(out=yb, in_=ysl, func=AF.Copy), reads=[ysl], writes=[yb])
            sch.op("act", lambda e, ysl=ysl: e.activation(out=ysq, in_=ysl, func=AF.Square), reads=[ysl], writes=[ysq])
            for (src, st) in ((yb, s0), (ysq, s1)):
                for c in range(4):
                    sch.op("pe", lambda e, c=c, src=src, st=st: e.matmul(
                        st, lhsT=self.ones[:], rhs=src[:, c, :], start=(c == 0), stop=(c == 3)),
                        reads=[self.ones[:], src[:, c, :]], writes=[st], signal=(c == 3))
            sch.op("act", lambda e, s0=s0: e.activation(out=mu, in_=s0, func=AF.Copy, scale=1.0 / 512.0),
                   reads=[s0], writes=[mu])
            sch.op("dve", lambda e: e.tensor_tensor(out=var, in0=mu, in1=mu, op=ALU.mult), reads=[mu], writes=[var])
            sch.op("dve", lambda e, s1=s1: e.scalar_tensor_tensor(
                out=var, in0=s1, scalar=1.0 / 512.0, in1=var, op0=ALU.mult, op1=ALU.subtract),
                reads=[s1, var], writes=[var])
            sch.op("act", lambda e: e.activation(out=var, in_=var, func=AF.Sqrt, bias=float(EPS), scale=1.0),
                   reads=[var], writes=[var])
            sch.op("dve", lambda e: e.reciprocal(out=var, in_=var), reads=[var], writes=[var])
            for c in range(4):
                yc = y[:, c, tt * TT:(tt + 1) * TT]
                sch.op("dve", lambda e, yc=yc: e.tensor_tensor(out=yc, in0=yc, in1=mu, op=ALU.subtract),
                       reads=[yc, mu], writes=[yc])
                sch.op("dve", lambda e, yc=yc: e.tensor_tensor(out=yc, in0=yc, in1=var, op=ALU.mult),
                       reads=[yc, var], writes=[yc])
                dst = coT[:, c, tt * TT:(tt + 1) * TT]
                sch.op("act", lambda e, yc=yc, dst=dst, c=c: e.activation(
                    out=dst, in_=yc, func=AF.Silu, bias=lbv[:, c:c + 1], scale=lgv[:, c:c + 1]),
                    reads=[yc, lbv[:, c:c + 1], lgv[:, c:c + 1]], writes=[dst])
        if self.stop <= 4:
            return
        o = T0
        ets, pTs = [], []
        for _ in range(4):
            v, o = self.carve(o, [TT], F32)
            ets.append(v)
        for _ in range(4):
            v, o = self.carve(o, [TT], BF16)
            pTs.append(v)
        den, o = self.carve(o, [4, 128], F32)
        esk, o = self.carve(o, [4], F32)
        assert o <= ARENA
        sch.op("act", lambda e: e.activation(out=esk, in_=skv, func=AF.Exp), reads=[skv], writes=[esk])
        Mv = self.mswa[:].rearrange("p (r h b q) -> p r h b q", r=4, h=2, b=2)
        it = 0
        for n in range(min(S // 128, self.kn)):
            A = self.ps[4 + (n % 2) * 2][:]
            B = self.ps[5 + (n % 2) * 2][:]
            kbs = [1] if n == 0 else [0, 1]
            b0 = kbs[0]
            for g in range(2):
                par = it % 2
                it += 1
                Sb = [self.ps[2 * par + hs][:].rearrange("p (r b q) -> p r b q", r=2, b=2) for hs in range(2)]
                etv = [ets[2 * par + hs].rearrange("p (r b q) -> p r b q", r=2, b=2) for hs in range(2)]
                pTv = [pTs[2 * par + hs].rearrange("p (r b q) -> p r b q", r=2, b=2) for hs in range(2)]
                for prl in range(2):
                    pr = 2 * g + prl
                    for bs in kbs:
                        kb = n - 1 + bs
                        for hs in range(2):
                            r0 = hs * 64
                            lh = kdT[r0:r0 + 64, g, kb * 128:(kb + 1) * 128]
                            rh = qT[r0:r0 + 64, pr, n * 128:(n + 1) * 128]
                            po = Sb[hs][:, prl, bs, :]
                            sch.op("pe", lambda e, lh=lh, rh=rh, po=po: e.matmul(po, lhsT=lh, rhs=rh, start=True, stop=True),
                                   reads=[lh, rh], writes=[po], signal=(prl == 1 and bs == kbs[-1]))
                if self.kswa <= 1:
                    continue
                for hs in range(2):
                    si, eo, po_ = Sb[hs][:, :, b0:, :], etv[hs][:, :, b0:, :], pTv[hs][:, :, b0:, :]
                    mi = Mv[:, 2 * g:2 * g + 2, hs, b0:, :]
                    sch.op("act", lambda e, si=si, eo=eo: e.activation(out=eo, in_=si, func=AF.Exp, scale=0.125),
                           reads=[si], writes=[eo])
                    if self.kswa <= 2:
                        continue
                    sch.op("dve", lambda e, eo=eo, po_=po_, mi=mi: e.tensor_tensor(out=po_, in0=eo, in1=mi, op=ALU.mult),
                           reads=[eo, mi], writes=[po_])
                if self.kswa <= 3:
                    continue
                for prl in range(2):
                    pr = 2 * g + prl
                    for hs in range(2):
                        r0 = hs * 64
                        for bs in kbs:
                            kb = n - 1 + bs
                            rh = pTv[hs][:, prl, bs, :]
                            lv = vT[:, kb, g * 64:(g + 1) * 64]
                            pa = A[r0:r0 + 64, pr * 128:(pr + 1) * 128]
                            pb = B[r0:r0 + 64, pr * 128:(pr + 1) * 128]
                            last = (prl == 1 and hs == 1 and bs == kbs[-1])
                            sch.op("pe", lambda e, lv=lv, rh=rh, pa=pa, bs=bs: e.matmul(
                                pa, lhsT=lv, rhs=rh, start=(bs == kbs[0]), stop=(bs == kbs[-1])),
                                reads=[lv, rh], writes=[pa], signal=False)
                            sch.op("pe", lambda e, rh=rh, pb=pb, bs=bs: e.matmul(
                                pb, lhsT=self.ones[:, 0:64], rhs=rh, start=(bs == kbs[0]), stop=(bs == kbs[-1])),
                                reads=[self.ones[:, 0:64], rh], writes=[pb], signal=last)
            if self.kswa <= 4:
                continue
            Bv = B.rearrange("p (r q) -> p r q", r=4)
            Av = A.rearrange("p (r q) -> p r q", r=4)
            ebc = esk.unsqueeze(2).to_broadcast([128, 4, 128])
            sch.op("dve", lambda e, Bv=Bv, ebc=ebc: e.tensor_tensor(out=den, in0=Bv, in1=ebc, op=ALU.add),
                   reads=[B, esk], writes=[den])
            sch.op("dve", lambda e: e.reciprocal(out=den, in_=den), reads=[den], writes=[den])
            dst = boT[:, :, n * 128:(n + 1) * 128]
            sch.op("dve", lambda e, Av=Av, dst=dst: e.tensor_tensor(out=dst, in0=Av, in1=den, op=ALU.mult),
                   reads=[A, den], writes=[dst])
        if self.stop <= 5:
            return
        o = T0
        sqh = []
        for _ in range(2):
            v, o = self.carve(o, [TT], BF16)
            sqh.append(v)
        rs2, o = self.carve(o, [TT], F32)
        self.proj_out_post(l, lambda k, tt: (coT if k < 4 else boT)[:, k % 4, tt * TT:(tt + 1) * TT],
                           hout, sqh, rs2)

    def plan_odd(self, l):
        w = self.win[l].rearrange("(k p) c -> p k c", p=128)
        V = self.wp.view
        self.wp.add([(lambda s_: V(s_, 0, [KC, 16]), w[:, :, 3072:3088])])
        for pr in range(8):
            self.wp.add([
                (lambda s_: V(s_, 0, [KC, 128]), w[:, :, pr * 128:(pr + 1) * 128]),
                (lambda s_: V(s_, 1024, [KC, 128]), w[:, :, 1024 + pr * 128:1024 + (pr + 1) * 128]),
                (lambda s_: V(s_, 2048, [KC, 128]), w[:, :, 2048 + pr * 128:2048 + (pr + 1) * 128])])
        self.plan_proj_out(self.wout[l])

    def mixer_odd(self, l):
        sch, wp = self.sch, self.wp
        i = l // 2
        bfv = self.cst[0:16, DEPTH * 6 * KC + 280 + i:DEPTH * 6 * KC + 280 + i + 1]
        K32 = 32 * 1024
        hT = self.carve(0, [KC, S], BF16)[0]
        hout = self.carve(0, [KC, TT], F32)[0]
        OT, off = self.carve(K32, [KC, S], BF16)
        P0 = off
        qTp, off = self.carve(off, [S], BF16)
        kTp, off = self.carve(off, [S], BF16)
        vaug, off = self.carve(off, [16, 192], BF16)
        Cb, off = self.carve(off, [S], BF16)
        negc, off = self.carve(off, [16, 16], F32)
        cTb, off = self.carve(off, [S], BF16)
        tmps, pTs = [], []
        for _ in range(2):
            v, off = self.carve(off, [TT], F32)
            tmps.append(v)
        for _ in range(2):
            v, off = self.carve(off, [TT], BF16)
            pTs.append(v)
        rec, off = self.carve(off, [TT], F32)
        assert off <= ARENA, off
        sqb, o = self.carve(P0, [KC, TT], BF16)
        rs, o = self.carve(o, [TT], F32)
        lf = self.carve(P0, [S], F32)[0]
        cpT = self.carve(tmps[0].offset * 0 + (off - 0), [1], F32)[0] if False else None
        cpT = self.carve(K32, [S], F32)[0]
        nbf, _ = self.carve(K32 + S * 4, [1], F32)
        self.prenorm_full(l, 2, hT, sqb, rs)
        tF = wp.get()
        wf = wp.view(tF[1], 0, [KC, 16])
        sch.op("dve", lambda e: e.tensor_scalar(out=nbf[0:16, :], in0=bfv, scalar1=-1.0, scalar2=None, op0=ALU.mult),
               reads=[bfv], writes=[nbf[0:16, :]])
        for tt in range(S // TT):
            pst = self.ps[self.pp % 2][:]
            self.pp += 1
            for k in range(KC):
                rhs = hT[:, k, tt * TT:(tt + 1) * TT]
                sch.op("pe", lambda e, k=k, rhs=rhs, pst=pst: e.matmul(
                    pst[0:16, :], lhsT=wf[:, k, :], rhs=rhs, start=(k == 0), stop=(k == KC - 1)),
                    reads=[wf[:, k, :], rhs], writes=[pst[0:16, :]], signal=(k == KC - 1))
            dst = lf[0:16, tt * TT:(tt + 1) * TT]
            sch.op("act", lambda e, pst=pst, dst=dst: e.activation(
                out=dst, in_=pst[0:16, :], func=AF.Exp, bias=nbf[0:16, :], scale=-1.0),
                reads=[pst[0:16, :], nbf[0:16, :]], writes=[dst])
        wp.release(tF[0])
        sch.op("act", lambda e: e.activation(out=lf[0:16, :], in_=lf[0:16, :], func=AF.Ln, bias=1.0, scale=1.0),
               reads=[lf[0:16, :]], writes=[lf[0:16, :]])
        sch.op("dve", lambda e: e.tensor_tensor_scan(out=cpT[0:16, :], data0=lf[0:16, :], data1=lf[0:16, :],
                                                     initial=0.0, op0=ALU.add, op1=ALU.bypass),
               reads=[lf[0:16, :]], writes=[cpT[0:16, :]])
        pst = self.ps[7][:]
        for kb in range(16):
            po = pst[:, kb * 16:(kb + 1) * 16]
            src = cpT[0:16, kb * 128:(kb + 1) * 128]
            sch.op("pe", lambda e, po=po, src=src: e.transpose(po, src, self.ident[:]),
                   reads=[src, self.ident[:]], writes=[po], signal=(kb == 15))
        sch.op("act", lambda e: e.activation(out=negc, in_=pst[:, 0:256].rearrange("p (a b) -> p a b", a=16), func=AF.Copy),
               reads=[pst[:, 0:256]], writes=[negc])
        sch.op("dve", lambda e: e.tensor_scalar(out=cTb[0:16, :], in0=cpT[0:16, :], scalar1=-1.0, scalar2=None, op0=ALU.mult),
               reads=[cpT[0:16, :]], writes=[cTb[0:16, :]])
        sch.op("dve", lambda e: e.memset(vaug[:, :, 64:128], 1.0), writes=[vaug[:, :, 64:128]])
        selv = self.sel[:].rearrange("p (h c) -> p h c", h=16)
        it = 0
        for pr in range(8):
            tp = wp.get()
            wq = wp.view(tp[1], 0, [KC, 128])
            wk = wp.view(tp[1], 1024, [KC, 128])
            wv_ = wp.view(tp[1], 2048, [KC, 128])
            self.fm_chunk(wq, hT, lambda tt, p: sch.op(
                "act", lambda e: e.activation(out=qTp[:, tt * TT:(tt + 1) * TT], in_=p, func=AF.Copy),
                reads=[p], writes=[qTp[:, tt * TT:(tt + 1) * TT]]))
            self.fm_chunk(wk, hT, lambda tt, p: sch.op(
                "dve", lambda e: e.tensor_copy(out=kTp[:, tt * TT:(tt + 1) * TT], in_=p),
                reads=[p], writes=[kTp[:, tt * TT:(tt + 1) * TT]]))
            for tb4 in range(4):
                pst = self.ps[self.pp % 2][:]
                self.pp += 1
                for jb in range(4):
                    tb = tb4 * 4 + jb
                    po = pst[:, jb * 128:(jb + 1) * 128]
                    for k in range(KC):
                        lh = hT[:, k, tb * 128:(tb + 1) * 128]
                        sch.op("pe", lambda e, k=k, lh=lh, po=po: e.matmul(
                            po, lhsT=lh, rhs=wv_[:, k, :], start=(k == 0), stop=(k == KC - 1)),
                            reads=[lh, wv_[:, k, :]], writes=[po], signal=(k == KC - 1))
                srcv = pst.rearrange("p (a b) -> p a b", a=4)
                d0 = vaug[:, tb4 * 4:(tb4 + 1) * 4, 0:64]
                d1 = vaug[:, tb4 * 4:(tb4 + 1) * 4, 128:192]
                sch.op("act", lambda e, d0=d0, srcv=srcv: e.activation(out=d0, in_=srcv[:, :, 0:64], func=AF.Copy),
                       reads=[pst], writes=[d0])
                sch.op("dve", lambda e, d1=d1, srcv=srcv: e.tensor_copy(out=d1, in_=srcv[:, :, 64:128]),
                       reads=[pst], writes=[d1])
            wp.release(tp[0])
            for hs in range(2):
                h = 2 * pr + hs
                r0 = hs * 64
                for tt in range(S // TT):
                    pst = self.ps[7][:]
                    rhs = cTb[0:16, tt * TT:(tt + 1) * TT]
                    sch.op("pe", lambda e, rhs=rhs, pst=pst, h=h: e.matmul(pst, lhsT=selv[0:16, h, :], rhs=rhs,
                                                                          start=True, stop=True),
                           reads=[selv[0:16, h, :], rhs], writes=[pst])
                    dst = Cb[:, tt * TT:(tt + 1) * TT]
                    sch.op("act", lambda e, dst=dst, pst=pst: e.activation(out=dst, in_=pst, func=AF.Copy),
                           reads=[pst], writes=[dst])
                for qc in range(4):
                    O = self.ps[5 + qc % 2][:]
                    nkb = 4 * qc + 4
                    for kb in range(nkb):
                        j = kb - 4 * qc
                        q0 = max(0, j) * 128
                        Sb = self.ps[2 + it % 3][:]
                        tmp = tmps[it % 2]
                        pT = pTs[it % 2]
                        it += 1
                        lh = kTp[r0:r0 + 64, kb * 128:(kb + 1) * 128]
                        rh = qTp[r0:r0 + 64, qc * TT + q0:(qc + 1) * TT]
                        sch.op("pe", lambda e, lh=lh, rh=rh, Sb=Sb, q0=q0: e.matmul(
                            Sb[:, q0:TT], lhsT=lh, rhs=rh, start=True, stop=True),
                            reads=[lh, rh], writes=[Sb[:, q0:TT]])
                        cbs = Cb[:, qc * TT + q0:(qc + 1) * TT]
                        sch.op("dve", lambda e, Sb=Sb, tmp=tmp, cbs=cbs, q0=q0: e.scalar_tensor_tensor(
                            out=tmp[:, q0:TT], in0=Sb[:, q0:TT], scalar=0.125, in1=cbs, op0=ALU.mult, op1=ALU.add),
                            reads=[Sb[:, q0:TT], cbs], writes=[tmp[:, q0:TT]])
                        if j >= 0:
                            dg = tmp[:, q0:q0 + 128]
                            sch.op("dve", lambda e, dg=dg: e.tensor_tensor(out=dg, in0=dg, in1=self.mc[:], op=ALU.add),
                                   reads=[dg, self.mc[:]], writes=[dg])
                        nb = negc[:, kb, h:h + 1]
                        sch.op("act", lambda e, tmp=tmp, pT=pT, nb=nb, q0=q0: e.activation(
                            out=pT[:, q0:TT], in_=tmp[:, q0:TT], func=AF.Exp, bias=nb, scale=1.0),
                            reads=[tmp[:, q0:TT], nb], writes=[pT[:, q0:TT]])
                        lv = vaug[:, kb, hs * 64:hs * 64 + 128]
                        sch.op("pe", lambda e, lv=lv, pT=pT, O=O, q0=q0, kb=kb, nkb=nkb: e.matmul(
                            O[:, q0:TT], lhsT=lv, rhs=pT[:, q0:TT], start=(kb == 0), stop=(kb == nkb - 1)),
                            reads=[lv, pT[:, q0:TT]], writes=[O[:, q0:TT]], signal=(kb == nkb - 1))
                    nr = hs * 64
                    dr = 64 - nr
                    sch.op("dve", lambda e, O=O, nr=nr, dr=dr: e.reciprocal(out=rec[nr:nr + 64, :], in_=O[dr:dr + 64, :]),
                           reads=[O[dr:dr + 64, :]], writes=[rec[nr:nr + 64, :]])
                    dst = OT[nr:nr + 64, pr, qc * TT:(qc + 1) * TT]
                    sch.op("dve", lambda e, O=O, nr=nr, dst=dst: e.tensor_tensor(
                        out=dst, in0=O[nr:nr + 64, :], in1=rec[nr:nr + 64, :], op=ALU.mult),
                        reads=[O[nr:nr + 64, :], rec[nr:nr + 64, :]], writes=[dst])
        sqh = []
        o = P0
        for _ in range(2):
            v, o = self.carve(o, [TT], BF16)
            sqh.append(v)
        rs2, o = self.carve(o, [TT], F32)
        self.proj_out_post(l, lambda k, tt: OT[:, k, tt * TT:(tt + 1) * TT], hout, sqh, rs2)


_CACHE = {}


def _get_prog(layers, subs):
    key = (tuple(layers), tuple(subs))
    if key not in _CACHE:
        p = Prog(layers, subs)
        p.build()
        _CACHE[key] = p
    return _CACHE[key]


def _consts(inputs):
    ng = np.asarray(inputs["norm_g"], dtype=np.float32)
    cst = np.zeros((128, NCST), np.float32)
    cst[:, 0:DEPTH * 6 * KC] = ng.reshape(DEPTH, 6, KC, 128).transpose(3, 0, 1, 2).reshape(128, -1)
    base = DEPTH * 6 * KC
    for i in range(2):
        b0 = base + i * 140
        cw = np.asarray(inputs["conv_w"][i], np.float32)
        cst[:, b0:b0 + 124] = cw.reshape(CONV_K, 4, 128).transpose(2, 1, 0).reshape(128, 124)
        cst[:, b0 + 124:b0 + 128] = np.asarray(inputs["conv_b"][i], np.float32).reshape(4, 128).T
        cst[:, b0 + 128:b0 + 132] = np.asarray(inputs["conv_ln_g"][i], np.float32).reshape(4, 128).T
        cst[:, b0 + 132:b0 + 136] = np.asarray(inputs["conv_ln_b"][i], np.float32).reshape(4, 128).T
        sk = np.asarray(inputs["swa_sinks"][i], np.float32)
        cst[0:64, b0 + 136:b0 + 140] = sk[0::2][None, :]
        cst[64:128, b0 + 136:b0 + 140] = sk[1::2][None, :]
        cst[0:16, base + 280 + i] = np.asarray(inputs["fox_b_f"][i], np.float32)
    k = np.arange(128)[:, None].astype(np.float64)
    q = np.arange(128)[None, :].astype(np.float64)
    mswa = np.zeros((128, 8, 2, 128), np.float64)
    for h in range(8):
        slope = 2.0 ** (-(h + 1))
        d0 = 128 + q - k
        mswa[:, h, 0, :] = np.exp(-slope * d0) * (q < k)
        d1 = q - k
        mswa[:, h, 1, :] = np.exp(-slope * d1) * (q >= k)
    mc = np.where(q >= k, 0.0, NEGBIG).astype(np.float32)
    sel = np.zeros((16, 16, 128), np.float32)
    for h in range(16):
        sel[h, h, :] = 1.0
    return {"cst": cst, "mswa": mswa.reshape(128, 2048).astype(np.float32), "mc": mc,
            "sel": sel.reshape(16, 2048), "ident": np.eye(16, dtype=np.float32)}


def _run(inputs, layers, subs=("f1", "mix", "f2"), x_override=None):
    x = np.asarray(inputs["x"], dtype=np.float32) if x_override is None else x_override
    shared = _consts(inputs)
    f32c = lambda a: np.ascontiguousarray(a, dtype=np.float32)
    for l in layers:
        for j, sub in ((0, "f1"), (1, "f2")):
            if sub in subs:
                shared[f"wg{l}{j}"] = f32c(inputs["ffn_w_gate"][l, j])
                shared[f"wu{l}{j}"] = f32c(inputs["ffn_w_up"][l, j])
                shared[f"wd{l}{j}"] = f32c(inputs["ffn_w_down"][l, j])
        if "mix" in subs:
            if l % 2 == 0:
                shared[f"win{l}"] = f32c(inputs["ab_w_in"][l // 2])
                shared[f"wout{l}"] = f32c(inputs["ab_w_out"][l // 2])
            else:
                shared[f"win{l}"] = f32c(inputs["fox_w_in"][l // 2])
                shared[f"wout{l}"] = f32c(inputs["fox_w_out"][l // 2])
    p = _get_prog(layers, subs)
    in_maps = []
    for b in range(NCORES):
        m = dict(shared)
        m["xT"] = np.ascontiguousarray(x[b].T)
        in_maps.append(m)
    import time as _t
    _t0 = _t.time()
    res = run_bass_kernel_spmd(p.nc, in_maps, core_ids=list(range(NCORES)))
    print("[kernel] launch wall s", round(_t.time() - _t0, 1), flush=True)
    out = np.stack([np.ascontiguousarray(r["yT"].T) for r in res.results], axis=0)
    return out.astype(np.float32)


LAUNCH_GROUPS = [[0], [1], [2], [3]]


def kernel(**inputs):
    x = None
    for grp in LAUNCH_GROUPS:
        x = _run(inputs, grp, x_override=x)
    return x
```

```python
import contextlib
import numpy as np
import concourse.bass as bass
import concourse.mybir as mybir
from concourse.bass_utils import run_bass_kernel_spmd

F32 = mybir.dt.float32
BF16 = mybir.dt.bfloat16
AF = mybir.ActivationFunctionType
ALU = mybir.AluOpType
ESZ = {F32: 4, BF16: 2}

D = 1024
S = 2048
DFF = 2816
KC = D // 128
FCH = DFF // 128
DEPTH = 4
EPS = 1e-6
NCORES = 8
TT = 512
TG = 1024
SLOT = 6144
NSLOT = 3
ARENA = 96 * 1024
NCST = 474
HD = 64
CONV_K = 31
NEGBIG = -30000.0


def _esz(dt):
    return ESZ[dt]


class Sched:
    def __init__(self, nc, es):
        self.nc = nc
        self.es = es
        self.eng = {}
        for name, obj in (("pe", nc.tensor), ("act", nc.scalar), ("dve", nc.vector),
                          ("pool", nc.gpsimd), ("sp", nc.sync)):
            sem = es.enter_context(nc.semaphore("sem_" + name))
            self.eng[name] = dict(obj=obj, sem=sem, cnt=0, waited={})
        self.recs = {}
        self.nwaits = 0
        self.nops = 0

    @staticmethod
    def region(ap):
        name = ap.tensor.name
        a = ap.ap
        esz = _esz(ap.dtype)
        pstep, pcnt = a[0]
        off = ap.offset
        if pstep > 0:
            p0 = off // pstep
            lo = off % pstep
        else:
            p0 = 0
            lo = off
        hi = lo + 1
        for st, c in a[1:]:
            hi += (c - 1) * abs(st)
        return (name, p0, p0 + pcnt, lo * esz, hi * esz)

    @staticmethod
    def _ov(r, q):
        return r[1] < q[2] and q[1] < r[2] and r[3] < q[4] and q[3] < r[4]

    @staticmethod
    def _contains(outer, inner):
        return (outer[1] <= inner[1] and inner[2] <= outer[2]
                and outer[3] <= inner[3] and inner[4] <= outer[4])

    def _collect(self, reads, writes):
        deps = {}

        def add(tok):
            h, v = tok[0], tok[1]
            k = h.name
            if k not in deps or deps[k][1] < v:
                deps[k] = (h, v)

        rregs = [self.region(a) for a in reads]
        wregs = [self.region(a) for a in writes]
        for r in rregs:
            rec = self.recs.get(r[0])
            if rec:
                for q, tok in rec["w"]:
                    if self._ov(r, q):
                        add(tok)
        for r in wregs:
            rec = self.recs.get(r[0])
            if rec:
                for q, tok in rec["w"]:
                    if self._ov(r, q):
                        add(tok)
                for q, tok in rec["r"]:
                    if self._ov(r, q):
                        add(tok)
        return deps, rregs, wregs

    def _emit_waits(self, E, deps, skip_self):
        need = []
        for k, (h, v) in deps.items():
            if skip_self and h is E["sem"]:
                continue
            if E["waited"].get(k, 0) >= v:
                continue
            need.append((k, h, v))
        for (k, h, v) in need[:-1]:
            E["obj"].wait_ge(h, v)
            E["waited"][k] = v
            self.nwaits += 1
        if need:
            k, h, v = need[-1]
            E["waited"][k] = v
            return (h, v)
        return None

    def _record(self, rregs, wregs, tok):
        for r in rregs:
            rec = self.recs.setdefault(r[0], {"w": [], "r": []})
            found = False
            for i, (q, t) in enumerate(rec["r"]):
                if q == r and t[0] is tok[0]:
                    if t[1] < tok[1]:
                        rec["r"][i] = (q, tok)
                    found = True
                    break
            if not found:
                rec["r"].append((r, tok))
        for r in wregs:
            rec = self.recs.setdefault(r[0], {"w": [], "r": []})
            rec["w"] = [(q, t) for (q, t) in rec["w"] if not self._contains(r, q)]
            rec["r"] = [(q, t) for (q, t) in rec["r"] if not self._contains(r, q)]
            rec["w"].append((r, tok))

    def op(self, eng, fn, reads=(), writes=(), signal=True, extra=()):
        E = self.eng[eng]
        deps, rregs, wregs = self._collect(reads, writes)
        for tok in extra:
            k = tok[0].name
            if k not in deps or deps[k][1] < tok[1]:
                deps[k] = (tok[0], tok[1])
        w = self._emit_waits(E, deps, skip_self=(eng == "pe"))
        ins = fn(E["obj"])
        if w is not None:
            ins._wait_ge(w[0], w[1])
        self.nops += 1
        if signal:
            E["cnt"] += 1
            ins.then_inc(E["sem"], 1)
            tok = [E["sem"], E["cnt"]]
        else:
            tok = [E["sem"], E["cnt"] + 1]
        self._record(rregs, wregs, tok)
        return tok

    def new_dsem(self, name):
        sem = self.es.enter_context(self.nc.semaphore(name))
        return dict(sem=sem, cnt=0)

    def dma(self, queue, ds, out, in_, tok, sb_reads=(), sb_writes=(), extra=()):
        E = self.eng[queue]
        deps, rregs, wregs = self._collect(sb_reads, sb_writes)
        for t in extra:
            k = t[0].name
            if k not in deps or deps[k][1] < t[1]:
                deps[k] = (t[0], t[1])
        w = self._emit_waits(E, deps, skip_self=False)
        ins = E["obj"].dma_start(out=out, in_=in_)
        if w is not None:
            ins._wait_ge(w[0], w[1])
        ins.then_inc(ds["sem"], 16)
        ds["cnt"] += 16
        tok[0] = ds["sem"]
        tok[1] = ds["cnt"]
        self._record(rregs, wregs, tok)
        self.nops += 1


class WeightPool:
    def __init__(self, sch, ring, nslot, slot_elems):
        self.sch = sch
        self.ring = ring
        self.nslot = nslot
        self.slot = slot_elems
        self.plan = []
        self.next_load = 0
        self.next_get = 0
        self.free = list(range(nslot))
        self.slot_of = {}
        self.dsems = [sch.new_dsem(f"wsem{i}") for i in range(nslot)]

    def add(self, tile):
        self.plan.append(tile)

    def view(self, slot, off, shape):
        n = 1
        for s in shape:
            n *= s
        base = slot * self.slot + off
        v = self.ring[:, base:base + n]
        if len(shape) == 2:
            v = v.rearrange("p (a b) -> p a b", a=shape[0])
        elif len(shape) == 3:
            v = v.rearrange("p (a b c) -> p a b c", a=shape[0], b=shape[1])
        return v

    def prefetch(self):
        while self.free and self.next_load < len(self.plan):
            slot = self.free.pop(0)
            idx = self.next_load
            self.next_load += 1
            self.slot_of[idx] = slot
            tok = [None, 0]
            for (dst_fn, src) in self.plan[idx]:
                dst = dst_fn(slot)
                self.sch.dma("pool", self.dsems[slot], dst, src, tok, sb_writes=[dst])

    def get(self):
        self.prefetch()
        idx = self.next_get
        self.next_get += 1
        if idx not in self.slot_of:
            self.prefetch()
        assert idx in self.slot_of, "weight pool starved (release missing?)"
        return idx, self.slot_of[idx]

    def release(self, idx):
        self.free.append(self.slot_of[idx])
        self.prefetch()


class Prog:
    def __init__(self, layers, subs=("f1", "mix", "f2")):
        self.layers = list(layers)
        self.subs = subs
        self.nc = bass.Bass("TRN2", target_bir_lowering=False)
        self.es = contextlib.ExitStack()
        self.pp = 0
        import os
        self.stop = int(os.environ.get('KSTOP', '99'))
        self.kswa = int(os.environ.get('KSWA', '99'))
        self.kn = int(os.environ.get('KN', '99'))

    def dram_in(self, name, shape):
        return self.nc.dram_tensor(name, list(shape), F32, kind="ExternalInput").ap()

    def build(self):
        nc = self.nc
        with self.es as es:
            self.sch = Sched(nc, es)
            sch = self.sch
            self.xT = self.dram_in("xT", [D, S])
            self.cst_d = self.dram_in("cst", [128, NCST])
            self.mswa_d = self.dram_in("mswa", [128, 2048])
            self.mc_d = self.dram_in("mc", [128, 128])
            self.sel_d = self.dram_in("sel", [16, 2048])
            self.ident_d = self.dram_in("ident", [16, 16])
            self.win, self.wout = {}, {}
            if "mix" in self.subs:
                for l in self.layers:
                    self.win[l] = self.dram_in(f"win{l}", [D, 1792 if l % 2 == 0 else 3088])
                    self.wout[l] = self.dram_in(f"wout{l}", [D, D])
            self.wg, self.wu, self.wd = {}, {}, {}
            for l in self.layers:
                for j, sub in ((0, "f1"), (1, "f2")):
                    if sub in self.subs:
                        self.wg[l, j] = self.dram_in(f"wg{l}{j}", [D, DFF])
                        self.wu[l, j] = self.dram_in(f"wu{l}{j}", [D, DFF])
                        self.wd[l, j] = self.dram_in(f"wd{l}{j}", [DFF, D])
            self.yT = nc.dram_tensor("yT", [D, S], F32, kind="ExternalOutput").ap()
            sb = lambda n, shp, dt: es.enter_context(nc.sbuf_tensor(n, shp, dt))
            self.xs = sb("xs", [128, KC, S], F32)
            self.ring = sb("ring", [128, NSLOT * SLOT], BF16)
            self.cst = sb("cst_sb", [128, NCST], F32)
            self.gT = self.cst[:, 0:DEPTH * 6 * KC]
            self.mswa = sb("mswa_sb", [128, 2048], BF16)
            self.mc = sb("mc_sb", [128, 128], F32)
            self.sel = sb("sel_sb", [16, 2048], BF16)
            self.ident = sb("ident_sb", [16, 16], F32)
            self.g32 = sb("g32", [128, DEPTH * 6 * KC], F32)
            self.ones = sb("ones", [128, 128], BF16)
            self.arena = sb("arena", [128, ARENA], mybir.dt.uint8)
            self.ps = [es.enter_context(nc.psum_tensor(f"ps{i}", [128, TT], F32)) for i in range(8)]
            self.wp = WeightPool(sch, self.ring, NSLOT, SLOT)
            self.ds_x = sch.new_dsem("ds_x")
            self.ds_c = sch.new_dsem("ds_c")
            self.ds_c2 = sch.new_dsem("ds_c2")
            self.ds_o = sch.new_dsem("ds_o")

            for l in self.layers:
                for sub in self.subs:
                    if sub == "f1":
                        self.plan_ffn(l, 0)
                    elif sub == "f2":
                        self.plan_ffn(l, 1)
                    elif sub == "mix":
                        (self.plan_even if l % 2 == 0 else self.plan_odd)(l)

            tok = [None, 0]
            for k in range(KC):
                sch.dma("sp", self.ds_x, self.xs[:, k, :], self.xT[k * 128:(k + 1) * 128, :], tok,
                        sb_writes=[self.xs[:, k, :]])
            tok = [None, 0]
            sch.dma("sp", self.ds_c, self.cst[:], self.cst_d, tok, sb_writes=[self.cst[:]])
            sch.dma("sp", self.ds_c, self.mc[:], self.mc_d, tok, sb_writes=[self.mc[:]])
            sch.dma("sp", self.ds_c, self.ident[:], self.ident_d, tok, sb_writes=[self.ident[:]])
            tok = [None, 0]
            sch.dma("pool", self.ds_c2, self.mswa[:], self.mswa_d, tok, sb_writes=[self.mswa[:]])
            sch.dma("pool", self.ds_c2, self.sel[:], self.sel_d, tok, sb_writes=[self.sel[:]])
            sch.op("dve", lambda e: e.memset(self.ones[:], 1.0), writes=[self.ones[:]])
            self.wp.prefetch()
            gv = self.gT.rearrange("p (l n k) -> p l n k", l=DEPTH, n=6)
            g32v = self.g32[:].rearrange("p (l n k) -> p l n k", l=DEPTH, n=6)
            for n in range(6):
                f = 16.0 if n in (1, 5) else 32.0
                sch.op("dve", lambda e, n=n, f=f: e.tensor_scalar(
                    out=g32v[:, :, n, :], in0=gv[:, :, n, :], scalar1=f, scalar2=None, op0=ALU.mult),
                    reads=[gv[:, :, n, :]], writes=[g32v[:, :, n, :]])

            for l in self.layers:
                for sub in self.subs:
                    if sub == "f1":
                        self.ffn(l, 0)
                    elif sub == "f2":
                        self.ffn(l, 1)
                    elif sub == "mix":
                        (self.mixer_even if l % 2 == 0 else self.mixer_odd)(l)

            tok = [None, 0]
            for k in range(KC):
                sch.dma("sp", self.ds_o, self.yT[k * 128:(k + 1) * 128, :], self.xs[:, k, :], tok,
                        sb_reads=[self.xs[:, k, :]])
            nc.sync.wait_ge(self.ds_o["sem"], self.ds_o["cnt"])
        return nc

    def gcol(self, l, n, k):
        c = (l * 6 + n) * KC + k
        return self.g32[:, c:c + 1]

    def carve(self, off, shape, dt):
        n = 1
        for s in shape:
            n *= s
        nb = n * _esz(dt)
        v = self.arena[:, off:off + nb].bitcast(dt)
        if len(shape) == 2:
            v = v.rearrange("p (a b) -> p a b", a=shape[0])
        elif len(shape) == 3:
            v = v.rearrange("p (a b c) -> p a b c", a=shape[0], b=shape[1])
        return v, off + nb

    GU_STAGES = [(0, 3), (3, 6), (6, 9), (9, 12), (12, 15), (15, 18), (18, 21), (21, 22)]

    def plan_ffn(self, l, j):
        wg = self.wg[l, j].rearrange("(k p) c -> p k c", p=128)
        wu = self.wu[l, j].rearrange("(k p) c -> p k c", p=128)
        wd = self.wd[l, j].rearrange("(f p) c -> p f c", p=128)
        for g in range(S // TG):
            for (c0, c1) in self.GU_STAGES:
                n = (c1 - c0) * 128
                self.wp.add([(lambda s_, n=n: self.wp.view(s_, 0, [KC, n]), wg[:, :, c0 * 128:c1 * 128]),
                             (lambda s_, n=n: self.wp.view(s_, KC * n, [KC, n]), wu[:, :, c0 * 128:c1 * 128])])
            for tt in range(TG // TT):
                for dp in range(4):
                    self.wp.add([(lambda s_: self.wp.view(s_, 0, [FCH, 256]), wd[:, :, dp * 256:(dp + 1) * 256])])

    def rstd_from_stats(self, st_ps, rs):
        self.sch.op("act", lambda e: e.activation(out=rs, in_=st_ps, func=AF.Sqrt, bias=float(D * EPS), scale=1.0),
                    reads=[st_ps], writes=[rs])
        self.sch.op("dve", lambda e: e.reciprocal(out=rs, in_=rs), reads=[rs], writes=[rs])

    def prenorm_tile(self, l, n, t0, hdst, sqb, rs, st_ps):
        sch = self.sch
        xin = self.xs[:, :, t0:t0 + TT]
        sch.op("act", lambda e: e.activation(out=sqb, in_=xin, func=AF.Square), reads=[xin], writes=[sqb])
        for k in range(KC):
            sch.op("pe", lambda e, k=k: e.matmul(st_ps, lhsT=self.ones[:], rhs=sqb[:, k, :],
                                                 start=(k == 0), stop=(k == KC - 1)),
                   reads=[self.ones[:], sqb[:, k, :]], writes=[st_ps], signal=(k == KC - 1))
        self.rstd_from_stats(st_ps, rs)
        for k in range(KC):
            xi = self.xs[:, k, t0:t0 + TT]
            sch.op("dve", lambda e, k=k, xi=xi: e.scalar_tensor_tensor(
                out=hdst[:, k, :], in0=xi, scalar=self.gcol(l, n, k), in1=rs, op0=ALU.mult, op1=ALU.mult),
                reads=[xi, self.gcol(l, n, k), rs], writes=[hdst[:, k, :]])

    def postnorm_update(self, l, n, t0, hout, rs):
        sch = self.sch
        for k in range(KC):
            xi = self.xs[:, k, t0:t0 + TT]
            hk = hout[:, k, :]
            sch.op("dve", lambda e, k=k, hk=hk: e.scalar_tensor_tensor(
                out=hk, in0=hk, scalar=self.gcol(l, n, k), in1=rs, op0=ALU.mult, op1=ALU.mult),
                reads=[hk, self.gcol(l, n, k), rs], writes=[hk])
            sch.op("dve", lambda e, xi=xi, hk=hk: e.tensor_tensor(out=xi, in0=xi, in1=hk, op=ALU.add),
                   reads=[xi, hk], writes=[xi])

    def ffn(self, l, j):
        sch = self.sch
        wp = self.wp
        n_pre, n_post = (0, 1) if j == 0 else (4, 5)
        ntt = TG // TT
        off = 0
        hT, off = self.carve(off, [KC, TG], BF16)
        hout = self.arena[:, 0:KC * TT * 4].bitcast(F32).rearrange("p (a b) -> p a b", a=KC)
        actT, off = self.carve(off, [FCH, TG], BF16)
        sqb, off = self.carve(off, [KC, TT], BF16)
        rs, off = self.carve(off, [TT], F32)
        rs2, off = self.carve(off, [TT], F32)
        sgt = []
        for i in range(2):
            v, off = self.carve(off, [TT], BF16)
            sgt.append(v)
        sqh = []
        for i in range(2):
            v, off = self.carve(off, [TT], BF16)
            sqh.append(v)
        assert off <= ARENA
        psG = [self.ps[0][:], self.ps[1][:]]
        psU = [self.ps[2][:], self.ps[3][:]]
        psD = [self.ps[4][:], self.ps[5][:]]
        psS = [self.ps[6][:], self.ps[7][:]]
        it = 0
        for g in range(S // TG):
            g0 = g * TG
            for tt in range(ntt):
                self.prenorm_tile(l, n_pre, g0 + tt * TT, hT[:, :, tt * TT:(tt + 1) * TT], sqb, rs,
                                  psS[tt % 2])
            for (c0, c1) in self.GU_STAGES:
                idx, slot = wp.get()
                n = (c1 - c0) * 128
                wgv = wp.view(slot, 0, [KC, n])
                wuv = wp.view(slot, KC * n, [KC, n])
                for c in range(c0, c1):
                    cl = (c - c0) * 128
                    for tt in range(ntt):
                        b = it % 2
                        it += 1
                        hsl = hT[:, :, tt * TT:(tt + 1) * TT]
                        for (wv, pst) in ((wgv, psG[b]), (wuv, psU[b])):
                            for k in range(KC):
                                sch.op("pe", lambda e, k=k, wv=wv, pst=pst: e.matmul(
                                    pst, lhsT=wv[:, k, cl:cl + 128], rhs=hsl[:, k, :],
                                    start=(k == 0), stop=(k == KC - 1)),
                                    reads=[wv[:, k, cl:cl + 128], hsl[:, k, :]], writes=[pst],
                                    signal=(k == KC - 1))
                        sch.op("act", lambda e, b=b: e.activation(out=sgt[b], in_=psG[b], func=AF.Silu),
                               reads=[psG[b]], writes=[sgt[b]])
                        dst = actT[:, c, tt * TT:(tt + 1) * TT]
                        sch.op("dve", lambda e, b=b, dst=dst: e.tensor_tensor(
                            out=dst, in0=psU[b], in1=sgt[b], op=ALU.mult),
                            reads=[psU[b], sgt[b]], writes=[dst])
                wp.release(idx)
            for tt in range(ntt):
                t0 = g0 + tt * TT
                asl = actT[:, :, tt * TT:(tt + 1) * TT]
                stp = psS[tt % 2]
                for dp in range(4):
                    idx, slot = wp.get()
                    wdv = wp.view(slot, 0, [FCH, 256])
                    for dd in range(2):
                        dc = dp * 2 + dd
                        b = dc % 2
                        for f in range(FCH):
                            sch.op("pe", lambda e, f=f, b=b, dd=dd: e.matmul(
                                psD[b], lhsT=wdv[:, f, dd * 128:(dd + 1) * 128], rhs=asl[:, f, :],
                                start=(f == 0), stop=(f == FCH - 1)),
                                reads=[wdv[:, f, dd * 128:(dd + 1) * 128], asl[:, f, :]], writes=[psD[b]],
                                signal=(f == FCH - 1))
                        hk = hout[:, dc, :]
                        sch.op("act", lambda e, b=b, hk=hk: e.activation(out=hk, in_=psD[b], func=AF.Copy),
                               reads=[psD[b]], writes=[hk])
                        sch.op("act", lambda e, b=b: e.activation(out=sqh[b], in_=psD[b], func=AF.Square),
                               reads=[psD[b]], writes=[sqh[b]])
                        sch.op("pe", lambda e, b=b, dc=dc: e.matmul(
                            stp, lhsT=self.ones[:], rhs=sqh[b], start=(dc == 0), stop=(dc == KC - 1)),
                            reads=[self.ones[:], sqh[b]], writes=[stp], signal=True)
                    wp.release(idx)
                self.rstd_from_stats(stp, rs2)
                self.postnorm_update(l, n_post, t0, hout, rs2)


    def plan_proj_out(self, w):
        wv = w.rearrange("(k p) c -> p k c", p=128)
        for hh in range(2):
            self.wp.add([(lambda s_: self.wp.view(s_, 0, [KC, 512]), wv[:, :, hh * 512:(hh + 1) * 512])])

    def fm_chunk(self, wv, hT, evac):
        sch = self.sch
        for tt in range(S // TT):
            pst = self.ps[self.pp % 2][:]
            self.pp += 1
            for k in range(KC):
                rhs = hT[:, k, tt * TT:(tt + 1) * TT]
                sch.op("pe", lambda e, k=k, rhs=rhs, pst=pst: e.matmul(
                    pst, lhsT=wv[:, k, :], rhs=rhs, start=(k == 0), stop=(k == KC - 1)),
                    reads=[wv[:, k, :], rhs], writes=[pst], signal=(k == KC - 1))
            evac(tt, pst)

    def prenorm_full(self, l, n, hT, sqb, rs):
        for tt in range(S // TT):
            self.prenorm_tile(l, n, tt * TT, hT[:, :, tt * TT:(tt + 1) * TT], sqb, rs, self.ps[6 + tt % 2][:])

    def proj_out_post(self, l, rhs_fn, hout, sqh, rs2):
        sch, wp = self.sch, self.wp
        t1 = wp.get()
        t2 = wp.get()
        psD = [self.ps[0][:], self.ps[1][:]]
        stp = [self.ps[2][:], self.ps[3][:]]
        for tt in range(S // TT):
            st = stp[tt % 2]
            for dc in range(KC):
                tile = t1 if dc < 4 else t2
                wv = wp.view(tile[1], 0, [KC, 512])
                b = dc % 2
                c0 = (dc % 4) * 128
                for k in range(KC):
                    rhs = rhs_fn(k, tt)
                    sch.op("pe", lambda e, k=k, rhs=rhs, wv=wv, b=b, c0=c0: e.matmul(
                        psD[b], lhsT=wv[:, k, c0:c0 + 128], rhs=rhs, start=(k == 0), stop=(k == KC - 1)),
                        reads=[wv[:, k, c0:c0 + 128], rhs], writes=[psD[b]], signal=(k == KC - 1))
                hk = hout[:, dc, :]
                sch.op("act", lambda e, b=b, hk=hk: e.activation(out=hk, in_=psD[b], func=AF.Copy),
                       reads=[psD[b]], writes=[hk])
                sch.op("act", lambda e, b=b: e.activation(out=sqh[b], in_=psD[b], func=AF.Square),
                       reads=[psD[b]], writes=[sqh[b]])
                sch.op("pe", lambda e, b=b, dc=dc, st=st: e.matmul(
                    st, lhsT=self.ones[:], rhs=sqh[b], start=(dc == 0), stop=(dc == KC - 1)),
                    reads=[self.ones[:], sqh[b]], writes=[st], signal=True)
            self.rstd_from_stats(st, rs2)
            self.postnorm_update(l, 3, tt * TT, hout, rs2)
        wp.release(t1[0])
        wp.release(t2[0])

    def plan_even(self, l):
        w = self.win[l].rearrange("(k p) c -> p k c", p=128)
        V = self.wp.view
        self.wp.add([(lambda s_: V(s_, 0, [KC, 768]), w[:, :, 0:768])])
        self.wp.add([(lambda s_: V(s_, 0, [KC, 768]), w[:, :, 768:1536])])
        t3 = []
        for g in range(2):
            for hh in range(2):
                t3.append((lambda s_, g=g, hh=hh: V(s_, g * 1024, [KC, 128])[:, :, hh * 64:(hh + 1) * 64],
                           w[:, :, 1536 + g * 64:1536 + (g + 1) * 64]))
        t3.append((lambda s_: V(s_, 2048, [KC, 128]), w[:, :, 1664:1792]))
        self.wp.add(t3)
        self.plan_proj_out(self.wout[l])

    def mixer_even(self, l):
        sch, wp = self.sch, self.wp
        i = l // 2
        cb0 = DEPTH * 6 * KC + i * 140
        cst = self.cst
        cwv = cst[:, cb0:cb0 + 124].rearrange("p (c j) -> p c j", c=4)
        cbv = cst[:, cb0 + 124:cb0 + 128]
        lgv = cst[:, cb0 + 128:cb0 + 132]
        lbv = cst[:, cb0 + 132:cb0 + 136]
        skv = cst[:, cb0 + 136:cb0 + 140]
        K32 = 32 * 1024
        hT = self.carve(0, [KC, S], BF16)[0]
        y = self.carve(0, [4, S], F32)[0]
        boT = self.carve(0, [4, S], BF16)[0]
        hout = self.carve(16 * 1024, [KC, TT], F32)[0]
        aT, off = self.carve(K32, [4, S + 30], BF16)
        coT = self.carve(K32, [4, S], BF16)[0]
        off = (off + 63) // 64 * 64
        qT, off = self.carve(off, [4, S], BF16)
        kdT, off = self.carve(off, [2, S], BF16)
        vT, off = self.carve(off, [16, 128], BF16)
        T0 = off
        assert T0 + 12 * 1024 <= ARENA, T0
        sqb, o = self.carve(T0, [KC, TT], BF16)
        rs, o = self.carve(o, [TT], F32)
        self.prenorm_full(l, 2, hT, sqb, rs)
        if self.stop <= 1:
            return
        sgs = []
        o = T0
        for _ in range(2):
            v, o = self.carve(o, [S], BF16)
            sgs.append(v)
        t1 = wp.get()
        t2 = wp.get()

        def wview(col):
            tile = t1 if col < 768 else t2
            v = wp.view(tile[1], 0, [KC, 768])
            lc = col % 768
            return v[:, :, lc:lc + 128]

        for c in range(4):
            sch.op("dve", lambda e, c=c: e.memset(aT[:, c, 0:30], 0.0), writes=[aT[:, c, 0:30]])
        for c in range(4):
            sg = sgs[c % 2]
            self.fm_chunk(wview(512 + 128 * c), hT, lambda tt, p, sg=sg: sch.op(
                "act", lambda e: e.activation(out=sg[:, tt * TT:(tt + 1) * TT], in_=p, func=AF.Sigmoid),
                reads=[p], writes=[sg[:, tt * TT:(tt + 1) * TT]]))
            self.fm_chunk(wview(128 * c), hT, lambda tt, p, sg=sg, c=c: sch.op(
                "dve", lambda e: e.tensor_tensor(out=aT[:, c, 30 + tt * TT:30 + (tt + 1) * TT], in0=p,
                                                 in1=sg[:, tt * TT:(tt + 1) * TT], op=ALU.mult),
                reads=[p, sg[:, tt * TT:(tt + 1) * TT]], writes=[aT[:, c, 30 + tt * TT:30 + (tt + 1) * TT]]))
        wp.release(t1[0])
        for pr in range(4):
            self.fm_chunk(wview(1024 + 128 * pr), hT, lambda tt, p, pr=pr: sch.op(
                "act", lambda e: e.activation(out=qT[:, pr, tt * TT:(tt + 1) * TT], in_=p, func=AF.Copy),
                reads=[p], writes=[qT[:, pr, tt * TT:(tt + 1) * TT]]))
        wp.release(t2[0])
        t3 = wp.get()
        for g in range(2):
            wv = wp.view(t3[1], g * 1024, [KC, 128])
            self.fm_chunk(wv, hT, lambda tt, p, g=g: sch.op(
                "dve", lambda e: e.tensor_copy(out=kdT[:, g, tt * TT:(tt + 1) * TT], in_=p),
                reads=[p], writes=[kdT[:, g, tt * TT:(tt + 1) * TT]]))
        wvv = wp.view(t3[1], 2048, [KC, 128])
        for tb4 in range(4):
            pst = self.ps[self.pp % 2][:]
            self.pp += 1
            for jb in range(4):
                tb = tb4 * 4 + jb
                po = pst[:, jb * 128:(jb + 1) * 128]
                for k in range(KC):
                    lh = hT[:, k, tb * 128:(tb + 1) * 128]
                    sch.op("pe", lambda e, k=k, lh=lh, po=po: e.matmul(
                        po, lhsT=lh, rhs=wvv[:, k, :], start=(k == 0), stop=(k == KC - 1)),
                        reads=[lh, wvv[:, k, :]], writes=[po], signal=(k == KC - 1))
            dst = vT[:, tb4 * 4:(tb4 + 1) * 4, :]
            src = pst.rearrange("p (a b) -> p a b", a=4)
            sch.op("act", lambda e, dst=dst, src=src: e.activation(out=dst, in_=src, func=AF.Copy),
                   reads=[pst], writes=[dst])
        wp.release(t3[0])
        if self.stop <= 2:
            return
        for j in range(CONV_K):
            for c in range(4):
                src = aT[:, c, j:j + S]
                yc = y[:, c, :]
                if j == 0:
                    sch.op("dve", lambda e, c=c, src=src, yc=yc: e.tensor_scalar(
                        out=yc, in0=src, scalar1=cwv[:, c, 0:1], scalar2=cbv[:, c:c + 1],
                        op0=ALU.mult, op1=ALU.add),
                        reads=[src, cwv[:, c, 0:1], cbv[:, c:c + 1]], writes=[yc])
                else:
                    sch.op("dve", lambda e, c=c, j=j, src=src, yc=yc: e.scalar_tensor_tensor(
                        out=yc, in0=src, scalar=cwv[:, c, j:j + 1], in1=yc, op0=ALU.mult, op1=ALU.add),
                        reads=[src, cwv[:, c, j:j + 1], yc], writes=[yc])
        if self.stop <= 3:
            return
        o = T0
        yb, o = self.carve(o, [4, TT], BF16)
        ysq, o = self.carve(o, [4, TT], BF16)
        mu, o = self.carve(o, [TT], F32)
        var, o = self.carve(o, [TT], F32)
        for tt in range(S // TT):
            ysl = y[:, :, tt * TT:(tt + 1) * TT]
            s0 = self.ps[2 + 2 * (tt % 2)][:]
            s1 = self.ps[3 + 2 * (tt % 2)][:]
            sch.op("act", lambda e, ysl=ysl: e.activation(out=yb, in_=ysl, func=AF.Copy), reads=[ysl], writes=[yb])
            sch.op("act", lambda e, ysl=ysl: e.activation(out=ysq, in_=ysl, func=AF.Square), reads=[ysl], writes=[ysq])
            for (src, st) in ((yb, s0), (ysq, s1)):
                for c in range(4):
                    sch.op("pe", lambda e, c=c, src=src, st=st: e.matmul(
                        st, lhsT=self.ones[:], rhs=src[:, c, :], start=(c == 0), stop=(c == 3)),
                        reads=[self.ones[:], src[:, c, :]], writes=[st], signal=(c == 3))
            sch.op("act", lambda e, s0=s0: e.activation(out=mu, in_=s0, func=AF.Copy, scale=1.0 / 512.0),
                   reads=[s0], writes=[mu])
            sch.op("dve", lambda e: e.tensor_tensor(out=var, in0=mu, in1=mu, op=ALU.mult), reads=[mu], writes=[var])
            sch.op("dve", lambda e, s1=s1: e.scalar_tensor_tensor(
                out=var, in0=s1, scalar=1.0 / 512.0, in1=var, op0=ALU.mult, op1=ALU.subtract),
                reads=[s1, var], writes=[var])
            sch.op("act", lambda e: e.activation(out=var, in_=var, func=AF.Sqrt, bias=float(EPS), scale=1.0),
                   reads=[var], writes=[var])
            sch.op("dve", lambda e: e.reciprocal(out=var, in_=var), reads=[var], writes=[var])
            for c in range(4):
                yc = y[:, c, tt * TT:(tt + 1) * TT]
                sch.op("dve", lambda e, yc=yc: e.tensor_tensor(out=yc, in0=yc, in1=mu, op=ALU.subtract),
                       reads=[yc, mu], writes=[yc])
                sch.op("dve", lambda e, yc=yc: e.tensor_tensor(out=yc, in0=yc, in1=var, op=ALU.mult),
                       reads=[yc, var], writes=[yc])
                dst = coT[:, c, tt * TT:(tt + 1) * TT]
                sch.op("act", lambda e, yc=yc, dst=dst, c=c: e.activation(
                    out=dst, in_=yc, func=AF.Silu, bias=lbv[:, c:c + 1], scale=lgv[:, c:c + 1]),
                    reads=[yc, lbv[:, c:c + 1], lgv[:, c:c + 1]], writes=[dst])
        if self.stop <= 4:
            return
        o = T0
        ets, pTs = [], []
        for _ in range(4):
            v, o = self.carve(o, [TT], F32)
            ets.append(v)
        for _ in range(4):
            v, o = self.carve(o, [TT], BF16)
            pTs.append(v)
        den, o = self.carve(o, [4, 128], F32)
        esk, o = self.carve(o, [4], F32)
        assert o <= ARENA
        sch.op("act", lambda e: e.activation(out=esk, in_=skv, func=AF.Exp), reads=[skv], writes=[esk])
        Mv = self.mswa[:].rearrange("p (r h b q) -> p r h b q", r=4, h=2, b=2)
        it = 0
        for n in range(min(S // 128, self.kn)):
            A = self.ps[4 + (n % 2) * 2][:]
            B = self.ps[5 + (n % 2) * 2][:]
            kbs = [1] if n == 0 else [0, 1]
            b0 = kbs[0]
            for g in range(2):
                par = it % 2
                it += 1
                Sb = [self.ps[2 * par + hs][:].rearrange("p (r b q) -> p r b q", r=2, b=2) for hs in range(2)]
                etv = [ets[2 * par + hs].rearrange("p (r b q) -> p r b q", r=2, b=2) for hs in range(2)]
                pTv = [pTs[2 * par + hs].rearrange("p (r b q) -> p r b q", r=2, b=2) for hs in range(2)]
                for prl in range(2):
                    pr = 2 * g + prl
                    for bs in kbs:
                        kb = n - 1 + bs
                        for hs in range(2):
                            r0 = hs * 64
                            lh = kdT[r0:r0 + 64, g, kb * 128:(kb + 1) * 128]
                            rh = qT[r0:r0 + 64, pr, n * 128:(n + 1) * 128]
                            po = Sb[hs][:, prl, bs, :]
                            sch.op("pe", lambda e, lh=lh, rh=rh, po=po: e.matmul(po, lhsT=lh, rhs=rh, start=True, stop=True),
                                   reads=[lh, rh], writes=[po], signal=(prl == 1 and bs == kbs[-1]))
                if self.kswa <= 1:
                    continue
                for hs in range(2):
                    si, eo, po_ = Sb[hs][:, :, b0:, :], etv[hs][:, :, b0:, :], pTv[hs][:, :, b0:, :]
                    mi = Mv[:, 2 * g:2 * g + 2, hs, b0:, :]
                    sch.op("act", lambda e, si=si, eo=eo: e.activation(out=eo, in_=si, func=AF.Exp, scale=0.125),
                           reads=[si], writes=[eo])
                    if self.kswa <= 2:
                        continue
                    sch.op("dve", lambda e, eo=eo, po_=po_, mi=mi: e.tensor_tensor(out=po_, in0=eo, in1=mi, op=ALU.mult),
                           reads=[eo, mi], writes=[po_])
                if self.kswa <= 3:
                    continue
                for prl in range(2):
                    pr = 2 * g + prl
                    for hs in range(2):
                        r0 = hs * 64
                        for bs in kbs:
                            kb = n - 1 + bs
                            rh = pTv[hs][:, prl, bs, :]
                            lv = vT[:, kb, g * 64:(g + 1) * 64]
                            pa = A[r0:r0 + 64, pr * 128:(pr + 1) * 128]
                            pb = B[r0:r0 + 64, pr * 128:(pr + 1) * 128]
                            last = (prl == 1 and hs == 1 and bs == kbs[-1])
                            sch.op("pe", lambda e, lv=lv, rh=rh, pa=pa, bs=bs: e.matmul(
                                pa, lhsT=lv, rhs=rh, start=(bs == kbs[0]), stop=(bs == kbs[-1])),
                                reads=[lv, rh], writes=[pa], signal=False)
                            sch.op("pe", lambda e, rh=rh, pb=pb, bs=bs: e.matmul(
                                pb, lhsT=self.ones[:, 0:64], rhs=rh, start=(bs == kbs[0]), stop=(bs == kbs[-1])),
                                reads=[self.ones[:, 0:64], rh], writes=[pb], signal=last)
            if self.kswa <= 4:
                continue
            Bv = B.rearrange("p (r q) -> p r q", r=4)
            Av = A.rearrange("p (r q) -> p r q", r=4)
            ebc = esk.unsqueeze(2).to_broadcast([128, 4, 128])
            sch.op("dve", lambda e, Bv=Bv, ebc=ebc: e.tensor_tensor(out=den, in0=Bv, in1=ebc, op=ALU.add),
                   reads=[B, esk], writes=[den])
            sch.op("dve", lambda e: e.reciprocal(out=den, in_=den), reads=[den], writes=[den])
            dst = boT[:, :, n * 128:(n + 1) * 128]
            sch.op("dve", lambda e, Av=Av, dst=dst: e.tensor_tensor(out=dst, in0=Av, in1=den, op=ALU.mult),
                   reads=[A, den], writes=[dst])
        if self.stop <= 5:
            return
        o = T0
        sqh = []
        for _ in range(2):
            v, o = self.carve(o, [TT], BF16)
            sqh.append(v)
        rs2, o = self.carve(o, [TT], F32)
        self.proj_out_post(l, lambda k, tt: (coT if k < 4 else boT)[:, k % 4, tt * TT:(tt + 1) * TT],
                           hout, sqh, rs2)

    def plan_odd(self, l):
        w = self.win[l].rearrange("(k p) c -> p k c", p=128)
        V = self.wp.view
        self.wp.add([(lambda s_: V(s_, 0, [KC, 16]), w[:, :, 3072:3088])])
        for pr in range(8):
            self.wp.add([
                (lambda s_: V(s_, 0, [KC, 128]), w[:, :, pr * 128:(pr + 1) * 128]),
                (lambda s_: V(s_, 1024, [KC, 128]), w[:, :, 1024 + pr * 128:1024 + (pr + 1) * 128]),
                (lambda s_: V(s_, 2048, [KC, 128]), w[:, :, 2048 + pr * 128:2048 + (pr + 1) * 128])])
        self.plan_proj_out(self.wout[l])

    def mixer_odd(self, l):
        sch, wp = self.sch, self.wp
        i = l // 2
        bfv = self.cst[0:16, DEPTH * 6 * KC + 280 + i:DEPTH * 6 * KC + 280 + i + 1]
        K32 = 32 * 1024
        hT = self.carve(0, [KC, S], BF16)[0]
        hout = self.carve(0, [KC, TT], F32)[0]
        OT, off = self.carve(K32, [KC, S], BF16)
        P0 = off
        qTp, off = self.carve(off, [S], BF16)
        kTp, off = self.carve(off, [S], BF16)
        vaug, off = self.carve(off, [16, 192], BF16)
        Cb, off = self.carve(off, [S], BF16)
        negc, off = self.carve(off, [16, 16], F32)
        cTb, off = self.carve(off, [S], BF16)
        tmps, pTs = [], []
        for _ in range(2):
            v, off = self.carve(off, [TT], F32)
            tmps.append(v)
        for _ in range(2):
            v, off = self.carve(off, [TT], BF16)
            pTs.append(v)
        rec, off = self.carve(off, [TT], F32)
        assert off <= ARENA, off
        sqb, o = self.carve(P0, [KC, TT], BF16)
        rs, o = self.carve(o, [TT], F32)
        lf = self.carve(P0, [S], F32)[0]
        cpT = self.carve(tmps[0].offset * 0 + (off - 0), [1], F32)[0] if False else None
        cpT = self.carve(K32, [S], F32)[0]
        nbf, _ = self.carve(K32 + S * 4, [1], F32)
        self.prenorm_full(l, 2, hT, sqb, rs)
        tF = wp.get()
        wf = wp.view(tF[1], 0, [KC, 16])
        sch.op("dve", lambda e: e.tensor_scalar(out=nbf[0:16, :], in0=bfv, scalar1=-1.0, scalar2=None, op0=ALU.mult),
               reads=[bfv], writes=[nbf[0:16, :]])
        for tt in range(S // TT):
            pst = self.ps[self.pp % 2][:]
            self.pp += 1
            for k in range(KC):
                rhs = hT[:, k, tt * TT:(tt + 1) * TT]
                sch.op("pe", lambda e, k=k, rhs=rhs, pst=pst: e.matmul(
                    pst[0:16, :], lhsT=wf[:, k, :], rhs=rhs, start=(k == 0), stop=(k == KC - 1)),
                    reads=[wf[:, k, :], rhs], writes=[pst[0:16, :]], signal=(k == KC - 1))
            dst = lf[0:16, tt * TT:(tt + 1) * TT]
            sch.op("act", lambda e, pst=pst, dst=dst: e.activation(
                out=dst, in_=pst[0:16, :], func=AF.Exp, bias=nbf[0:16, :], scale=-1.0),
                reads=[pst[0:16, :], nbf[0:16, :]], writes=[dst])
        wp.release(tF[0])
        sch.op("act", lambda e: e.activation(out=lf[0:16, :], in_=lf[0:16, :], func=AF.Ln, bias=1.0, scale=1.0),
               reads=[lf[0:16, :]], writes=[lf[0:16, :]])
        sch.op("dve", lambda e: e.tensor_tensor_scan(out=cpT[0:16, :], data0=lf[0:16, :], data1=lf[0:16, :],
                                                     initial=0.0, op0=ALU.add, op1=ALU.bypass),
               reads=[lf[0:16, :]], writes=[cpT[0:16, :]])
        pst = self.ps[7][:]
        for kb in range(16):
            po = pst[:, kb * 16:(kb + 1) * 16]
            src = cpT[0:16, kb * 128:(kb + 1) * 128]
            sch.op("pe", lambda e, po=po, src=src: e.transpose(po, src, self.ident[:]),
                   reads=[src, self.ident[:]], writes=[po], signal=(kb == 15))
        sch.op("act", lambda e: e.activation(out=negc, in_=pst[:, 0:256].rearrange("p (a b) -> p a b", a=16), func=AF.Copy),
               reads=[pst[:, 0:256]], writes=[negc])
        sch.op("dve", lambda e: e.tensor_scalar(out=cTb[0:16, :], in0=cpT[0:16, :], scalar1=-1.0, scalar2=None, op0=ALU.mult),
               reads=[cpT[0:16, :]], writes=[cTb[0:16, :]])
        sch.op("dve", lambda e: e.memset(vaug[:, :, 64:128], 1.0), writes=[vaug[:, :, 64:128]])
        selv = self.sel[:].rearrange("p (h c) -> p h c", h=16)
        it = 0
        for pr in range(8):
            tp = wp.get()
            wq = wp.view(tp[1], 0, [KC, 128])
            wk = wp.view(tp[1], 1024, [KC, 128])
            wv_ = wp.view(tp[1], 2048, [KC, 128])
            self.fm_chunk(wq, hT, lambda tt, p: sch.op(
                "act", lambda e: e.activation(out=qTp[:, tt * TT:(tt + 1) * TT], in_=p, func=AF.Copy),
                reads=[p], writes=[qTp[:, tt * TT:(tt + 1) * TT]]))
            self.fm_chunk(wk, hT, lambda tt, p: sch.op(
                "dve", lambda e: e.tensor_copy(out=kTp[:, tt * TT:(tt + 1) * TT], in_=p),
                reads=[p], writes=[kTp[:, tt * TT:(tt + 1) * TT]]))
            for tb4 in range(4):
                pst = self.ps[self.pp % 2][:]
                self.pp += 1
                for jb in range(4):
                    tb = tb4 * 4 + jb
                    po = pst[:, jb * 128:(jb + 1) * 128]
                    for k in range(KC):
                        lh = hT[:, k, tb * 128:(tb + 1) * 128]
                        sch.op("pe", lambda e, k=k, lh=lh, po=po: e.matmul(
                            po, lhsT=lh, rhs=wv_[:, k, :], start=(k == 0), stop=(k == KC - 1)),
                            reads=[lh, wv_[:, k, :]], writes=[po], signal=(k == KC - 1))
                srcv = pst.rearrange("p (a b) -> p a b", a=4)
                d0 = vaug[:, tb4 * 4:(tb4 + 1) * 4, 0:64]
                d1 = vaug[:, tb4 * 4:(tb4 + 1) * 4, 128:192]
                sch.op("act", lambda e, d0=d0, srcv=srcv: e.activation(out=d0, in_=srcv[:, :, 0:64], func=AF.Copy),
                       reads=[pst], writes=[d0])
                sch.op("dve", lambda e, d1=d1, srcv=srcv: e.tensor_copy(out=d1, in_=srcv[:, :, 64:128]),
                       reads=[pst], writes=[d1])
            wp.release(tp[0])
            for hs in range(2):
                h = 2 * pr + hs
                r0 = hs * 64
                for tt in range(S // TT):
                    pst = self.ps[7][:]
                    rhs = cTb[0:16, tt * TT:(tt + 1) * TT]
                    sch.op("pe", lambda e, rhs=rhs, pst=pst, h=h: e.matmul(pst, lhsT=selv[0:16, h, :], rhs=rhs,
                                                                          start=True, stop=True),
                           reads=[selv[0:16, h, :], rhs], writes=[pst])
                    dst = Cb[:, tt * TT:(tt + 1) * TT]
                    sch.op("act", lambda e, dst=dst, pst=pst: e.activation(out=dst, in_=pst, func=AF.Copy),
                           reads=[pst], writes=[dst])
                for qc in range(4):
                    O = self.ps[5 + qc % 2][:]
                    nkb = 4 * qc + 4
                    for kb in range(nkb):
                        j = kb - 4 * qc
                        q0 = max(0, j) * 128
                        Sb = self.ps[2 + it % 3][:]
                        tmp = tmps[it % 2]
                        pT = pTs[it % 2]
                        it += 1
                        lh = kTp[r0:r0 + 64, kb * 128:(kb + 1) * 128]
                        rh = qTp[r0:r0 + 64, qc * TT + q0:(qc + 1) * TT]
                        sch.op("pe", lambda e, lh=lh, rh=rh, Sb=Sb, q0=q0: e.matmul(
                            Sb[:, q0:TT], lhsT=lh, rhs=rh, start=True, stop=True),
                            reads=[lh, rh], writes=[Sb[:, q0:TT]])
                        cbs = Cb[:, qc * TT + q0:(qc + 1) * TT]
                        sch.op("dve", lambda e, Sb=Sb, tmp=tmp, cbs=cbs, q0=q0: e.scalar_tensor_tensor(
                            out=tmp[:, q0:TT], in0=Sb[:, q0:TT], scalar=0.125, in1=cbs, op0=ALU.mult, op1=ALU.add),
                            reads=[Sb[:, q0:TT], cbs], writes=[tmp[:, q0:TT]])
                        if j >= 0:
                            dg = tmp[:, q0:q0 + 128]
                            sch.op("dve", lambda e, dg=dg: e.tensor_tensor(out=dg, in0=dg, in1=self.mc[:], op=ALU.add),
                                   reads=[dg, self.mc[:]], writes=[dg])
                        nb = negc[:, kb, h:h + 1]
                        sch.op("act", lambda e, tmp=tmp, pT=pT, nb=nb, q0=q0: e.activation(
                            out=pT[:, q0:TT], in_=tmp[:, q0:TT], func=AF.Exp, bias=nb, scale=1.0),
                            reads=[tmp[:, q0:TT], nb], writes=[pT[:, q0:TT]])
                        lv = vaug[:, kb, hs * 64:hs * 64 + 128]
                        sch.op("pe", lambda e, lv=lv, pT=pT, O=O, q0=q0, kb=kb, nkb=nkb: e.matmul(
                            O[:, q0:TT], lhsT=lv, rhs=pT[:, q0:TT], start=(kb == 0), stop=(kb == nkb - 1)),
                            reads=[lv, pT[:, q0:TT]], writes=[O[:, q0:TT]], signal=(kb == nkb - 1))
                    nr = hs * 64
                    dr = 64 - nr
                    sch.op("dve", lambda e, O=O, nr=nr, dr=dr: e.reciprocal(out=rec[nr:nr + 64, :], in_=O[dr:dr + 64, :]),
                           reads=[O[dr:dr + 64, :]], writes=[rec[nr:nr + 64, :]])
                    dst = OT[nr:nr + 64, pr, qc * TT:(qc + 1) * TT]
                    sch.op("dve", lambda e, O=O, nr=nr, dst=dst: e.tensor_tensor(
                        out=dst, in0=O[nr:nr + 64, :], in1=rec[nr:nr + 64, :], op=ALU.mult),
                        reads=[O[nr:nr + 64, :], rec[nr:nr + 64, :]], writes=[dst])
        sqh = []
        o = P0
        for _ in range(2):
            v, o = self.carve(o, [TT], BF16)
            sqh.append(v)
        rs2, o = self.carve(o, [TT], F32)
        self.proj_out_post(l, lambda k, tt: OT[:, k, tt * TT:(tt + 1) * TT], hout, sqh, rs2)


_CACHE = {}


def _get_prog(layers, subs):
    key = (tuple(layers), tuple(subs))
    if key not in _CACHE:
        p = Prog(layers, subs)
        p.build()
        _CACHE[key] = p
    return _CACHE[key]


def _consts(inputs):
    ng = np.asarray(inputs["norm_g"], dtype=np.float32)
    cst = np.zeros((128, NCST), np.float32)
    cst[:, 0:DEPTH * 6 * KC] = ng.reshape(DEPTH, 6, KC, 128).transpose(3, 0, 1, 2).reshape(128, -1)
    base = DEPTH * 6 * KC
    for i in range(2):
        b0 = base + i * 140
        cw = np.asarray(inputs["conv_w"][i], np.float32)
        cst[:, b0:b0 + 124] = cw.reshape(CONV_K, 4, 128).transpose(2, 1, 0).reshape(128, 124)
        cst[:, b0 + 124:b0 + 128] = np.asarray(inputs["conv_b"][i], np.float32).reshape(4, 128).T
        cst[:, b0 + 128:b0 + 132] = np.asarray(inputs["conv_ln_g"][i], np.float32).reshape(4, 128).T
        cst[:, b0 + 132:b0 + 136] = np.asarray(inputs["conv_ln_b"][i], np.float32).reshape(4, 128).T
        sk = np.asarray(inputs["swa_sinks"][i], np.float32)
        cst[0:64, b0 + 136:b0 + 140] = sk[0::2][None, :]
        cst[64:128, b0 + 136:b0 + 140] = sk[1::2][None, :]
        cst[0:16, base + 280 + i] = np.asarray(inputs["fox_b_f"][i], np.float32)
    k = np.arange(128)[:, None].astype(np.float64)
    q = np.arange(128)[None, :].astype(np.float64)
    mswa = np.zeros((128, 8, 2, 128), np.float64)
    for h in range(8):
        slope = 2.0 ** (-(h + 1))
        d0 = 128 + q - k
        mswa[:, h, 0, :] = np.exp(-slope * d0) * (q < k)
        d1 = q - k
        mswa[:, h, 1, :] = np.exp(-slope * d1) * (q >= k)
    mc = np.where(q >= k, 0.0, NEGBIG).astype(np.float32)
    sel = np.zeros((16, 16, 128), np.float32)
    for h in range(16):
        sel[h, h, :] = 1.0
    return {"cst": cst, "mswa": mswa.reshape(128, 2048).astype(np.float32), "mc": mc,
            "sel": sel.reshape(16, 2048), "ident": np.eye(16, dtype=np.float32)}


def _run(inputs, layers, subs=("f1", "mix", "f2"), x_override=None):
    x = np.asarray(inputs["x"], dtype=np.float32) if x_override is None else x_override
    shared = _consts(inputs)
    f32c = lambda a: np.ascontiguousarray(a, dtype=np.float32)
    for l in layers:
        for j, sub in ((0, "f1"), (1, "f2")):
            if sub in subs:
                shared[f"wg{l}{j}"] = f32c(inputs["ffn_w_gate"][l, j])
                shared[f"wu{l}{j}"] = f32c(inputs["ffn_w_up"][l, j])
                shared[f"wd{l}{j}"] = f32c(inputs["ffn_w_down"][l, j])
        if "mix" in subs:
            if l % 2 == 0:
                shared[f"win{l}"] = f32c(inputs["ab_w_in"][l // 2])
                shared[f"wout{l}"] = f32c(inputs["ab_w_out"][l // 2])
            else:
                shared[f"win{l}"] = f32c(inputs["fox_w_in"][l // 2])
                shared[f"wout{l}"] = f32c(inputs["fox_w_out"][l // 2])
    p = _get_prog(layers, subs)
    in_maps = []
    for b in range(NCORES):
        m = dict(shared)
        m["xT"] = np.ascontiguousarray(x[b].T)
        in_maps.append(m)
    import time as _t
    _t0 = _t.time()
    res = run_bass_kernel_spmd(p.nc, in_maps, core_ids=list(range(NCORES)))
    print("[kernel] launch wall s", round(_t.time() - _t0, 1), flush=True)
    out = np.stack([np.ascontiguousarray(r["yT"].T) for r in res.results], axis=0)
    return out.astype(np.float32)


LAUNCH_GROUPS = [[0, 1, 2, 3]]


def kernel(**inputs):
    x = None
    for grp in LAUNCH_GROUPS:
        x = _run(inputs, grp, x_override=x)
    return x
```

```python
import contextlib
import numpy as np
import concourse.bass as bass
import concourse.mybir as mybir
from concourse.bass_utils import run_bass_kernel_spmd

F32 = mybir.dt.float32
BF16 = mybir.dt.bfloat16
AF = mybir.ActivationFunctionType
ALU = mybir.AluOpType
ESZ = {F32: 4, BF16: 2}

D = 1024
S = 2048
DFF = 2816
KC = D // 128
FCH = DFF // 128
DEPTH = 4
EPS = 1e-6
NCORES = 8
TT = 512
TG = 1024
SLOT = 6144
NSLOT = 3
ARENA = 104 * 1024
NCST = 474
HD = 64
CONV_K = 31
NEGBIG = -30000.0


def _esz(dt):
    return ESZ[dt]


class Sched:
    def __init__(self, nc, es):
        self.nc = nc
        self.es = es
        self.eng = {}
        for name, obj in (("pe", nc.tensor), ("act", nc.scalar), ("dve", nc.vector),
                          ("pool", nc.gpsimd), ("sp", nc.sync)):
            sem = es.enter_context(nc.semaphore("sem_" + name))
            self.eng[name] = dict(obj=obj, sem=sem, cnt=0, waited={})
        self.recs = {}
        self.nwaits = 0
        self.nops = 0

    @staticmethod
    def region(ap):
        name = ap.tensor.name
        a = ap.ap
        esz = _esz(ap.dtype)
        pstep, pcnt = a[0]
        off = ap.offset
        if pstep > 0:
            p0 = off // pstep
            lo = off % pstep
        else:
            p0 = 0
            lo = off
        hi = lo + 1
        for st, c in a[1:]:
            hi += (c - 1) * abs(st)
        return (name, p0, p0 + pcnt, lo * esz, hi * esz)

    @staticmethod
    def _ov(r, q):
        return r[1] < q[2] and q[1] < r[2] and r[3] < q[4] and q[3] < r[4]

    @staticmethod
    def _contains(outer, inner):
        return (outer[1] <= inner[1] and inner[2] <= outer[2]
                and outer[3] <= inner[3] and inner[4] <= outer[4])

    def _collect(self, reads, writes):
        deps = {}

        def add(tok):
            h, v = tok[0], tok[1]
            k = h.name
            if k not in deps or deps[k][1] < v:
                deps[k] = (h, v)

        rregs = [self.region(a) for a in reads]
        wregs = [self.region(a) for a in writes]
        for r in rregs:
            rec = self.recs.get(r[0])
            if rec:
                for q, tok in rec["w"]:
                    if self._ov(r, q):
                        add(tok)
        for r in wregs:
            rec = self.recs.get(r[0])
            if rec:
                for q, tok in rec["w"]:
                    if self._ov(r, q):
                        add(tok)
                for q, tok in rec["r"]:
                    if self._ov(r, q):
                        add(tok)
        return deps, rregs, wregs

    def _emit_waits(self, E, deps, skip_self):
        need = []
        for k, (h, v) in deps.items():
            if skip_self and h is E["sem"]:
                continue
            if E["waited"].get(k, 0) >= v:
                continue
            need.append((k, h, v))
        for (k, h, v) in need[:-1]:
            E["obj"].wait_ge(h, v)
            E["waited"][k] = v
            self.nwaits += 1
        if need:
            k, h, v = need[-1]
            E["waited"][k] = v
            return (h, v)
        return None

    def _record(self, rregs, wregs, tok):
        for r in rregs:
            rec = self.recs.setdefault(r[0], {"w": [], "r": []})
            found = False
            for i, (q, t) in enumerate(rec["r"]):
                if q == r and t[0] is tok[0]:
                    if t[1] < tok[1]:
                        rec["r"][i] = (q, tok)
                    found = True
                    break
            if not found:
                rec["r"].append((r, tok))
        for r in wregs:
            rec = self.recs.setdefault(r[0], {"w": [], "r": []})
            rec["w"] = [(q, t) for (q, t) in rec["w"] if not self._contains(r, q)]
            rec["r"] = [(q, t) for (q, t) in rec["r"] if not self._contains(r, q)]
            rec["w"].append((r, tok))

    def op(self, eng, fn, reads=(), writes=(), signal=True, extra=()):
        E = self.eng[eng]
        deps, rregs, wregs = self._collect(reads, writes)
        for tok in extra:
            k = tok[0].name
            if k not in deps or deps[k][1] < tok[1]:
                deps[k] = (tok[0], tok[1])
        w = self._emit_waits(E, deps, skip_self=(eng == "pe"))
        ins = fn(E["obj"])
        if w is not None:
            ins._wait_ge(w[0], w[1])
        self.nops += 1
        if signal:
            E["cnt"] += 1
            ins.then_inc(E["sem"], 1)
            tok = [E["sem"], E["cnt"]]
        else:
            tok = [E["sem"], E["cnt"] + 1]
        self._record(rregs, wregs, tok)
        return tok

    def new_dsem(self, name):
        sem = self.es.enter_context(self.nc.semaphore(name))
        return dict(sem=sem, cnt=0)

    def dma(self, queue, ds, out, in_, tok, sb_reads=(), sb_writes=(), extra=()):
        E = self.eng[queue]
        deps, rregs, wregs = self._collect(sb_reads, sb_writes)
        for t in extra:
            k = t[0].name
            if k not in deps or deps[k][1] < t[1]:
                deps[k] = (t[0], t[1])
        w = self._emit_waits(E, deps, skip_self=False)
        ins = E["obj"].dma_start(out=out, in_=in_)
        if w is not None:
            ins._wait_ge(w[0], w[1])
        ins.then_inc(ds["sem"], 16)
        ds["cnt"] += 16
        tok[0] = ds["sem"]
        tok[1] = ds["cnt"]
        self._record(rregs, wregs, tok)
        self.nops += 1


class WeightPool:
    def __init__(self, sch, ring, nslot, slot_elems):
        self.sch = sch
        self.ring = ring
        self.nslot = nslot
        self.slot = slot_elems
        self.plan = []
        self.next_load = 0
        self.next_get = 0
        self.free = list(range(nslot))
        self.slot_of = {}
        self.dsems = [sch.new_dsem(f"wsem{i}") for i in range(nslot)]

    def add(self, tile):
        self.plan.append(tile)

    def view(self, slot, off, shape):
        n = 1
        for s in shape:
            n *= s
        base = slot * self.slot + off
        v = self.ring[:, base:base + n]
        if len(shape) == 2:
            v = v.rearrange("p (a b) -> p a b", a=shape[0])
        elif len(shape) == 3:
            v = v.rearrange("p (a b c) -> p a b c", a=shape[0], b=shape[1])
        return v

    def prefetch(self):
        while self.free and self.next_load < len(self.plan):
            slot = self.free.pop(0)
            idx = self.next_load
            self.next_load += 1
            self.slot_of[idx] = slot
            tok = [None, 0]
            for (dst_fn, src) in self.plan[idx]:
                dst = dst_fn(slot)
                self.sch.dma("pool", self.dsems[slot], dst, src, tok, sb_writes=[dst])

    def get(self):
        self.prefetch()
        idx = self.next_get
        self.next_get += 1
        if idx not in self.slot_of:
            self.prefetch()
        assert idx in self.slot_of, "weight pool starved (release missing?)"
        return idx, self.slot_of[idx]

    def release(self, idx):
        self.free.append(self.slot_of[idx])
        self.prefetch()


class Prog:
    def __init__(self, layers, subs=("f1", "mix", "f2")):
        self.layers = list(layers)
        self.subs = subs
        self.nc = bass.Bass("TRN2", target_bir_lowering=False)
        self.es = contextlib.ExitStack()
        self.pp = 0
        import os
        self.stop = int(os.environ.get('KSTOP', '99'))
        self.kswa = int(os.environ.get('KSWA', '99'))
        self.kn = int(os.environ.get('KN', '99'))

    def dram_in(self, name, shape):
        return self.nc.dram_tensor(name, list(shape), F32, kind="ExternalInput").ap()

    def build(self):
        nc = self.nc
        with self.es as es:
            self.sch = Sched(nc, es)
            sch = self.sch
            self.xT = self.dram_in("xT", [D, S])
            self.cst_d = self.dram_in("cst", [128, NCST])
            self.mswa_d = self.dram_in("mswa", [128, 2048])
            self.mc_d = self.dram_in("mc", [128, 128])
            self.sel_d = self.dram_in("sel", [16, 2048])
            self.ident_d = self.dram_in("ident", [16, 16])
            self.identb_d = self.dram_in("identb", [128, 128])
            self.win, self.wout = {}, {}
            if "mix" in self.subs:
                for l in self.layers:
                    self.win[l] = self.dram_in(f"win{l}", [D, 1792 if l % 2 == 0 else 3088])
                    self.wout[l] = self.dram_in(f"wout{l}", [D, D])
            self.wg, self.wu, self.wd = {}, {}, {}
            for l in self.layers:
                for j, sub in ((0, "f1"), (1, "f2")):
                    if sub in self.subs:
                        self.wg[l, j] = self.dram_in(f"wg{l}{j}", [D, DFF])
                        self.wu[l, j] = self.dram_in(f"wu{l}{j}", [D, DFF])
                        self.wd[l, j] = self.dram_in(f"wd{l}{j}", [DFF, D])
            self.yT = nc.dram_tensor("yT", [D, S], F32, kind="ExternalOutput").ap()
            sb = lambda n, shp, dt: es.enter_context(nc.sbuf_tensor(n, shp, dt))
            self.xs = sb("xs", [128, KC, S], F32)
            self.ring = sb("ring", [128, NSLOT * SLOT], BF16)
            self.cst = sb("cst_sb", [128, NCST], F32)
            self.gT = self.cst[:, 0:DEPTH * 6 * KC]
            self.identb = sb("identb_sb", [128, 128], BF16)
            self.mc = sb("mc_sb", [128, 128], F32)
            self.ident = sb("ident_sb", [16, 16], F32)
            self.g32 = sb("g32", [128, DEPTH * 6 * KC], F32)
            self.ones = sb("ones", [128, 128], BF16)
            self.arena = sb("arena", [128, ARENA], mybir.dt.uint8)
            self.ps = [es.enter_context(nc.psum_tensor(f"ps{i}", [128, TT], F32)) for i in range(8)]
            self.wp = WeightPool(sch, self.ring, NSLOT, SLOT)
            self.ds_x = sch.new_dsem("ds_x")
            self.ds_c = sch.new_dsem("ds_c")
            self.ds_c2 = sch.new_dsem("ds_c2")
            self.ds_o = sch.new_dsem("ds_o")

            for l in self.layers:
                for sub in self.subs:
                    if sub == "f1":
                        self.plan_ffn(l, 0)
                    elif sub == "f2":
                        self.plan_ffn(l, 1)
                    elif sub == "mix":
                        (self.plan_even if l % 2 == 0 else self.plan_odd)(l)

            tok = [None, 0]
            for k in range(KC):
                sch.dma("sp", self.ds_x, self.xs[:, k, :], self.xT[k * 128:(k + 1) * 128, :], tok,
                        sb_writes=[self.xs[:, k, :]])
            tok = [None, 0]
            sch.dma("sp", self.ds_c, self.cst[:], self.cst_d, tok, sb_writes=[self.cst[:]])
            sch.dma("sp", self.ds_c, self.mc[:], self.mc_d, tok, sb_writes=[self.mc[:]])
            sch.dma("sp", self.ds_c, self.ident[:], self.ident_d, tok, sb_writes=[self.ident[:]])
            tok = [None, 0]
            sch.dma("pool", self.ds_c2, self.identb[:], self.identb_d, tok, sb_writes=[self.identb[:]])
            sch.op("dve", lambda e: e.memset(self.ones[:], 1.0), writes=[self.ones[:]])
            self.wp.prefetch()
            gv = self.gT.rearrange("p (l n k) -> p l n k", l=DEPTH, n=6)
            g32v = self.g32[:].rearrange("p (l n k) -> p l n k", l=DEPTH, n=6)
            for n in range(6):
                f = 16.0 if n in (1, 5) else 32.0
                sch.op("dve", lambda e, n=n, f=f: e.tensor_scalar(
                    out=g32v[:, :, n, :], in0=gv[:, :, n, :], scalar1=f, scalar2=None, op0=ALU.mult),
                    reads=[gv[:, :, n, :]], writes=[g32v[:, :, n, :]])

            for l in self.layers:
                for sub in self.subs:
                    if sub == "f1":
                        self.ffn(l, 0)
                    elif sub == "f2":
                        self.ffn(l, 1)
                    elif sub == "mix":
                        (self.mixer_even if l % 2 == 0 else self.mixer_odd)(l)

            tok = [None, 0]
            for k in range(KC):
                sch.dma("sp", self.ds_o, self.yT[k * 128:(k + 1) * 128, :], self.xs[:, k, :], tok,
                        sb_reads=[self.xs[:, k, :]])
            nc.sync.wait_ge(self.ds_o["sem"], self.ds_o["cnt"])
        return nc

    def gcol(self, l, n, k):
        c = (l * 6 + n) * KC + k
        return self.g32[:, c:c + 1]

    def carve(self, off, shape, dt):
        n = 1
        for s in shape:
            n *= s
        nb = n * _esz(dt)
        v = self.arena[:, off:off + nb].bitcast(dt)
        if len(shape) == 2:
            v = v.rearrange("p (a b) -> p a b", a=shape[0])
        elif len(shape) == 3:
            v = v.rearrange("p (a b c) -> p a b c", a=shape[0], b=shape[1])
        return v, off + nb

    GU_STAGES = [(0, 3), (3, 6), (6, 9), (9, 12), (12, 15), (15, 18), (18, 21), (21, 22)]

    def plan_ffn(self, l, j):
        wg = self.wg[l, j].rearrange("(k p) c -> p k c", p=128)
        wu = self.wu[l, j].rearrange("(k p) c -> p k c", p=128)
        wd = self.wd[l, j].rearrange("(f p) c -> p f c", p=128)
        for g in range(S // TG):
            for (c0, c1) in self.GU_STAGES:
                n = (c1 - c0) * 128
                self.wp.add([(lambda s_, n=n: self.wp.view(s_, 0, [KC, n]), wg[:, :, c0 * 128:c1 * 128]),
                             (lambda s_, n=n: self.wp.view(s_, KC * n, [KC, n]), wu[:, :, c0 * 128:c1 * 128])])
            for tt in range(TG // TT):
                for dp in range(4):
                    self.wp.add([(lambda s_: self.wp.view(s_, 0, [FCH, 256]), wd[:, :, dp * 256:(dp + 1) * 256])])

    def rstd_from_stats(self, st_ps, rs):
        self.sch.op("act", lambda e: e.activation(out=rs, in_=st_ps, func=AF.Sqrt, bias=float(D * EPS), scale=1.0),
                    reads=[st_ps], writes=[rs])
        self.sch.op("dve", lambda e: e.reciprocal(out=rs, in_=rs), reads=[rs], writes=[rs])

    def prenorm_tile(self, l, n, t0, hdst, sqb, rs, st_ps):
        sch = self.sch
        xin = self.xs[:, :, t0:t0 + TT]
        sch.op("act", lambda e: e.activation(out=sqb, in_=xin, func=AF.Square), reads=[xin], writes=[sqb])
        for k in range(KC):
            sch.op("pe", lambda e, k=k: e.matmul(st_ps, lhsT=self.ones[:], rhs=sqb[:, k, :],
                                                 start=(k == 0), stop=(k == KC - 1)),
                   reads=[self.ones[:], sqb[:, k, :]], writes=[st_ps], signal=(k == KC - 1))
        self.rstd_from_stats(st_ps, rs)
        for k in range(KC):
            xi = self.xs[:, k, t0:t0 + TT]
            sch.op("dve", lambda e, k=k, xi=xi: e.scalar_tensor_tensor(
                out=hdst[:, k, :], in0=xi, scalar=self.gcol(l, n, k), in1=rs, op0=ALU.mult, op1=ALU.mult),
                reads=[xi, self.gcol(l, n, k), rs], writes=[hdst[:, k, :]])

    def postnorm_update(self, l, n, t0, hout, rs):
        sch = self.sch
        for k in range(KC):
            xi = self.xs[:, k, t0:t0 + TT]
            hk = hout[:, k, :]
            sch.op("dve", lambda e, k=k, hk=hk: e.scalar_tensor_tensor(
                out=hk, in0=hk, scalar=self.gcol(l, n, k), in1=rs, op0=ALU.mult, op1=ALU.mult),
                reads=[hk, self.gcol(l, n, k), rs], writes=[hk])
            sch.op("dve", lambda e, xi=xi, hk=hk: e.tensor_tensor(out=xi, in0=xi, in1=hk, op=ALU.add),
                   reads=[xi, hk], writes=[xi])

    def ffn(self, l, j):
        sch = self.sch
        wp = self.wp
        n_pre, n_post = (0, 1) if j == 0 else (4, 5)
        ntt = TG // TT
        off = 0
        hT, off = self.carve(off, [KC, TG], BF16)
        hout = self.arena[:, 0:KC * TT * 4].bitcast(F32).rearrange("p (a b) -> p a b", a=KC)
        actT, off = self.carve(off, [FCH, TG], BF16)
        sqb, off = self.carve(off, [KC, TT], BF16)
        rs, off = self.carve(off, [TT], F32)
        rs2, off = self.carve(off, [TT], F32)
        sgt = []
        for i in range(2):
            v, off = self.carve(off, [TT], BF16)
            sgt.append(v)
        sqh = []
        for i in range(2):
            v, off = self.carve(off, [TT], BF16)
            sqh.append(v)
        assert off <= ARENA
        psG = [self.ps[0][:], self.ps[1][:]]
        psU = [self.ps[2][:], self.ps[3][:]]
        psD = [self.ps[4][:], self.ps[5][:]]
        psS = [self.ps[6][:], self.ps[7][:]]
        it = 0
        for g in range(S // TG):
            g0 = g * TG
            for tt in range(ntt):
                self.prenorm_tile(l, n_pre, g0 + tt * TT, hT[:, :, tt * TT:(tt + 1) * TT], sqb, rs,
                                  psS[tt % 2])
            for (c0, c1) in self.GU_STAGES:
                idx, slot = wp.get()
                n = (c1 - c0) * 128
                wgv = wp.view(slot, 0, [KC, n])
                wuv = wp.view(slot, KC * n, [KC, n])
                for c in range(c0, c1):
                    cl = (c - c0) * 128
                    for tt in range(ntt):
                        b = it % 2
                        it += 1
                        hsl = hT[:, :, tt * TT:(tt + 1) * TT]
                        for (wv, pst) in ((wgv, psG[b]), (wuv, psU[b])):
                            for k in range(KC):
                                sch.op("pe", lambda e, k=k, wv=wv, pst=pst: e.matmul(
                                    pst, lhsT=wv[:, k, cl:cl + 128], rhs=hsl[:, k, :],
                                    start=(k == 0), stop=(k == KC - 1)),
                                    reads=[wv[:, k, cl:cl + 128], hsl[:, k, :]], writes=[pst],
                                    signal=(k == KC - 1))
                        sch.op("act", lambda e, b=b: e.activation(out=sgt[b], in_=psG[b], func=AF.Silu),
                               reads=[psG[b]], writes=[sgt[b]])
                        dst = actT[:, c, tt * TT:(tt + 1) * TT]
                        sch.op("dve", lambda e, b=b, dst=dst: e.tensor_tensor(
                            out=dst, in0=psU[b], in1=sgt[b], op=ALU.mult),
                            reads=[psU[b], sgt[b]], writes=[dst])
                wp.release(idx)
            for tt in range(ntt):
                t0 = g0 + tt * TT
                asl = actT[:, :, tt * TT:(tt + 1) * TT]
                stp = psS[tt % 2]
                for dp in range(4):
                    idx, slot = wp.get()
                    wdv = wp.view(slot, 0, [FCH, 256])
                    for dd in range(2):
                        dc = dp * 2 + dd
                        b = dc % 2
                        for f in range(FCH):
                            sch.op("pe", lambda e, f=f, b=b, dd=dd: e.matmul(
                                psD[b], lhsT=wdv[:, f, dd * 128:(dd + 1) * 128], rhs=asl[:, f, :],
                                start=(f == 0), stop=(f == FCH - 1)),
                                reads=[wdv[:, f, dd * 128:(dd + 1) * 128], asl[:, f, :]], writes=[psD[b]],
                                signal=(f == FCH - 1))
                        hk = hout[:, dc, :]
                        sch.op("act", lambda e, b=b, hk=hk: e.activation(out=hk, in_=psD[b], func=AF.Copy),
                               reads=[psD[b]], writes=[hk])
                        sch.op("act", lambda e, b=b: e.activation(out=sqh[b], in_=psD[b], func=AF.Square),
                               reads=[psD[b]], writes=[sqh[b]])
                        sch.op("pe", lambda e, b=b, dc=dc: e.matmul(
                            stp, lhsT=self.ones[:], rhs=sqh[b], start=(dc == 0), stop=(dc == KC - 1)),
                            reads=[self.ones[:], sqh[b]], writes=[stp], signal=True)
                    wp.release(idx)
                self.rstd_from_stats(stp, rs2)
                self.postnorm_update(l, n_post, t0, hout, rs2)


    def plan_proj_out(self, w):
        wv = w.rearrange("(k p) c -> p k c", p=128)
        for hh in range(2):
            self.wp.add([(lambda s_: self.wp.view(s_, 0, [KC, 512]), wv[:, :, hh * 512:(hh + 1) * 512])])

    def fm_chunk(self, wv, hT, evac):
        sch = self.sch
        for tt in range(S // TT):
            pst = self.ps[self.pp % 2][:]
            self.pp += 1
            for k in range(KC):
                rhs = hT[:, k, tt * TT:(tt + 1) * TT]
                sch.op("pe", lambda e, k=k, rhs=rhs, pst=pst: e.matmul(
                    pst, lhsT=wv[:, k, :], rhs=rhs, start=(k == 0), stop=(k == KC - 1)),
                    reads=[wv[:, k, :], rhs], writes=[pst], signal=(k == KC - 1))
            evac(tt, pst)

    def prenorm_full(self, l, n, hT, sqb, rs):
        for tt in range(S // TT):
            self.prenorm_tile(l, n, tt * TT, hT[:, :, tt * TT:(tt + 1) * TT], sqb, rs, self.ps[6 + tt % 2][:])

    def proj_out_post(self, l, rhs_fn, hout, sqh, rs2):
        sch, wp = self.sch, self.wp
        t1 = wp.get()
        t2 = wp.get()
        psD = [self.ps[0][:], self.ps[1][:]]
        stp = [self.ps[2][:], self.ps[3][:]]
        for tt in range(S // TT):
            st = stp[tt % 2]
            for dc in range(KC):
                tile = t1 if dc < 4 else t2
                wv = wp.view(tile[1], 0, [KC, 512])
                b = dc % 2
                c0 = (dc % 4) * 128
                for k in range(KC):
                    rhs = rhs_fn(k, tt)
                    sch.op("pe", lambda e, k=k, rhs=rhs, wv=wv, b=b, c0=c0: e.matmul(
                        psD[b], lhsT=wv[:, k, c0:c0 + 128], rhs=rhs, start=(k == 0), stop=(k == KC - 1)),
                        reads=[wv[:, k, c0:c0 + 128], rhs], writes=[psD[b]], signal=(k == KC - 1))
                hk = hout[:, dc, :]
                sch.op("act", lambda e, b=b, hk=hk: e.activation(out=hk, in_=psD[b], func=AF.Copy),
                       reads=[psD[b]], writes=[hk])
                sch.op("act", lambda e, b=b: e.activation(out=sqh[b], in_=psD[b], func=AF.Square),
                       reads=[psD[b]], writes=[sqh[b]])
                sch.op("pe", lambda e, b=b, dc=dc, st=st: e.matmul(
                    st, lhsT=self.ones[:], rhs=sqh[b], start=(dc == 0), stop=(dc == KC - 1)),
                    reads=[self.ones[:], sqh[b]], writes=[st], signal=True)
            self.rstd_from_stats(st, rs2)
            self.postnorm_update(l, 3, tt * TT, hout, rs2)
        wp.release(t1[0])
        wp.release(t2[0])

    def plan_even(self, l):
        w = self.win[l].rearrange("(k p) c -> p k c", p=128)
        V = self.wp.view
        self.wp.add([(lambda s_: V(s_, 0, [KC, 768]), w[:, :, 0:768])])
        self.wp.add([(lambda s_: V(s_, 0, [KC, 768]), w[:, :, 768:1536])])
        t3 = []
        for g in range(2):
            for hh in range(2):
                t3.append((lambda s_, g=g, hh=hh: V(s_, g * 1024, [KC, 128])[:, :, hh * 64:(hh + 1) * 64],
                           w[:, :, 1536 + g * 64:1536 + (g + 1) * 64]))
        t3.append((lambda s_: V(s_, 2048, [KC, 128]), w[:, :, 1664:1792]))
        self.wp.add(t3)
        self.plan_proj_out(self.wout[l])

    def mixer_even(self, l):
        sch, wp = self.sch, self.wp
        i = l // 2
        cb0 = DEPTH * 6 * KC + i * 140
        cst = self.cst
        cwv = cst[:, cb0:cb0 + 124].rearrange("p (c j) -> p c j", c=4)
        cbv = cst[:, cb0 + 124:cb0 + 128]
        lgv = cst[:, cb0 + 128:cb0 + 132]
        lbv = cst[:, cb0 + 132:cb0 + 136]
        skv = cst[:, cb0 + 136:cb0 + 140]
        K32 = 32 * 1024
        hT = self.carve(0, [KC, S], BF16)[0]
        y = self.carve(0, [4, S], F32)[0]
        boT = self.carve(0, [4, S], BF16)[0]
        hout = self.carve(16 * 1024, [KC, TT], F32)[0]
        aT, off = self.carve(K32, [4, S + 30], BF16)
        coT = self.carve(K32, [4, S], BF16)[0]
        off = (off + 63) // 64 * 64
        qT, off = self.carve(off, [4, S], BF16)
        kdT, off = self.carve(off, [2, S], BF16)
        vT, off = self.carve(off, [16, 128], BF16)
        mswa, off = self.carve(off, [2048], BF16)
        T0 = off
        assert T0 + 16 * 1024 <= ARENA, T0
        tokm = [None, 0]
        sch.dma("pool", self.ds_c2, mswa, self.mswa_d, tokm, sb_writes=[mswa])
        sqb, o = self.carve(T0, [KC, TT], BF16)
        rs, o = self.carve(o, [TT], F32)
        self.prenorm_full(l, 2, hT, sqb, rs)
        if self.stop <= 1:
            return
        sgs = []
        o = T0
        for _ in range(2):
            v, o = self.carve(o, [S], BF16)
            sgs.append(v)
        t1 = wp.get()
        t2 = wp.get()

        def wview(col):
            tile = t1 if col < 768 else t2
            v = wp.view(tile[1], 0, [KC, 768])
            lc = col % 768
            return v[:, :, lc:lc + 128]

        for c in range(4):
            sch.op("dve", lambda e, c=c: e.memset(aT[:, c, 0:30], 0.0), writes=[aT[:, c, 0:30]])
        for c in range(4):
            sg = sgs[c % 2]
            self.fm_chunk(wview(512 + 128 * c), hT, lambda tt, p, sg=sg: sch.op(
                "act", lambda e: e.activation(out=sg[:, tt * TT:(tt + 1) * TT], in_=p, func=AF.Sigmoid),
                reads=[p], writes=[sg[:, tt * TT:(tt + 1) * TT]]))
            self.fm_chunk(wview(128 * c), hT, lambda tt, p, sg=sg, c=c: sch.op(
                "dve", lambda e: e.tensor_tensor(out=aT[:, c, 30 + tt * TT:30 + (tt + 1) * TT], in0=p,
                                                 in1=sg[:, tt * TT:(tt + 1) * TT], op=ALU.mult),
                reads=[p, sg[:, tt * TT:(tt + 1) * TT]], writes=[aT[:, c, 30 + tt * TT:30 + (tt + 1) * TT]]))
        wp.release(t1[0])
        for pr in range(4):
            self.fm_chunk(wview(1024 + 128 * pr), hT, lambda tt, p, pr=pr: sch.op(
                "act", lambda e: e.activation(out=qT[:, pr, tt * TT:(tt + 1) * TT], in_=p, func=AF.Copy),
                reads=[p], writes=[qT[:, pr, tt * TT:(tt + 1) * TT]]))
        wp.release(t2[0])
        t3 = wp.get()
        for g in range(2):
            wv = wp.view(t3[1], g * 1024, [KC, 128])
            self.fm_chunk(wv, hT, lambda tt, p, g=g: sch.op(
                "dve", lambda e: e.tensor_copy(out=kdT[:, g, tt * TT:(tt + 1) * TT], in_=p),
                reads=[p], writes=[kdT[:, g, tt * TT:(tt + 1) * TT]]))
        wvv = wp.view(t3[1], 2048, [KC, 128])
        for tb4 in range(4):
            pst = self.ps[self.pp % 2][:]
            self.pp += 1
            for jb in range(4):
                tb = tb4 * 4 + jb
                po = pst[:, jb * 128:(jb + 1) * 128]
                for k in range(KC):
                    lh = hT[:, k, tb * 128:(tb + 1) * 128]
                    sch.op("pe", lambda e, k=k, lh=lh, po=po: e.matmul(
                        po, lhsT=lh, rhs=wvv[:, k, :], start=(k == 0), stop=(k == KC - 1)),
                        reads=[lh, wvv[:, k, :]], writes=[po], signal=(k == KC - 1))
            dst = vT[:, tb4 * 4:(tb4 + 1) * 4, :]
            src = pst.rearrange("p (a b) -> p a b", a=4)
            sch.op("act", lambda e, dst=dst, src=src: e.activation(out=dst, in_=src, func=AF.Copy),
                   reads=[pst], writes=[dst])
        wp.release(t3[0])
        if self.stop <= 2:
            return
        Dgs = []
        o = T0
        for _ in range(2):
            v, o = self.carve(o, [CONV_K, 128], BF16)
            Dgs.append(v)
        for c in range(4):
            Dg = Dgs[c % 2]
            idb = self.identb[:].unsqueeze(1).to_broadcast([128, CONV_K, 128])
            cwb = cwv[:, c, :].unsqueeze(2).to_broadcast([128, CONV_K, 128])
            sch.op("dve", lambda e, Dg=Dg, idb=idb, cwb=cwb: e.tensor_tensor(out=Dg, in0=idb, in1=cwb, op=ALU.mult),
                   reads=[self.identb[:], cwv[:, c, :]], writes=[Dg])
            for tt in range(S // TT):
                pst = self.ps[self.pp % 2][:]
                self.pp += 1
                for j in range(CONV_K):
                    rhs = aT[:, c, tt * TT + j:tt * TT + j + TT]
                    sch.op("pe", lambda e, j=j, rhs=rhs, pst=pst, Dg=Dg: e.matmul(
                        pst, lhsT=Dg[:, j, :], rhs=rhs, start=(j == 0), stop=(j == CONV_K - 1)),
                        reads=[Dg[:, j, :], rhs], writes=[pst], signal=(j == CONV_K - 1))
                yc = y[:, c, tt * TT:(tt + 1) * TT]
                sch.op("act", lambda e, yc=yc, pst=pst, c=c: e.activation(
                    out=yc, in_=pst, func=AF.Identity, bias=cbv[:, c:c + 1], scale=1.0),
                    reads=[pst, cbv[:, c:c + 1]], writes=[yc])
        if self.stop <= 3:
            return
        o = T0
        yb, o = self.carve(o, [4, TT], BF16)
        ysq, o = self.carve(o, [4, TT], BF16)
        mu, o = self.carve(o, [TT], F32)
        var, o = self.carve(o, [TT], F32)
        for tt in range(S // TT):
            ysl = y[:, :, tt * TT:(tt + 1) * TT]
            s0 = self.ps[2 + 2 * (tt % 2)][:]
            s1 = self.ps[3 + 2 * (tt % 2)][:]
            sch.op("act", lambda e, ysl=ysl: e.activation(out=yb, in_=ysl, func=AF.Copy), reads=[ysl], writes=[yb])
            sch.op("act", lambda e, ysl=ysl: e.activation(out=ysq, in_=ysl, func=AF.Square), reads=[ysl], writes=[ysq])
            for (src, st) in ((yb, s0), (ysq, s1)):
                for c in range(4):
                    sch.op("pe", lambda e, c=c, src=src, st=st: e.matmul(
                        st, lhsT=self.ones[:], rhs=src[:, c, :], start=(c == 0), stop=(c == 3)),
                        reads=[self.ones[:], src[:, c, :]], writes=[st], signal=(c == 3))
            sch.op("act", lambda e, s0=s0: e.activation(out=mu, in_=s0, func=AF.Copy, scale=1.0 / 512.0),
                   reads=[s0], writes=[mu])
            sch.op("dve", lambda e: e.tensor_tensor(out=var, in0=mu, in1=mu, op=ALU.mult), reads=[mu], writes=[var])
            sch.op("dve", lambda e, s1=s1: e.scalar_tensor_tensor(
                out=var, in0=s1, scalar=1.0 / 512.0, in1=var, op0=ALU.mult, op1=ALU.subtract),
                reads=[s1, var], writes=[var])
            sch.op("act", lambda e: e.activation(out=var, in_=var, func=AF.Sqrt, bias=float(EPS), scale=1.0),
                   reads=[var], writes=[var])
            sch.op("dve", lambda e: e.reciprocal(out=var, in_=var), reads=[var], writes=[var])
            for c in range(4):
                yc = y[:, c, tt * TT:(tt + 1) * TT]
                sch.op("dve", lambda e, yc=yc: e.tensor_tensor(out=yc, in0=yc, in1=mu, op=ALU.subtract),
                       reads=[yc, mu], writes=[yc])
                sch.op("dve", lambda e, yc=yc: e.tensor_tensor(out=yc, in0=yc, in1=var, op=ALU.mult),
                       reads=[yc, var], writes=[yc])
                dst = coT[:, c, tt * TT:(tt + 1) * TT]
                sch.op("act", lambda e, yc=yc, dst=dst, c=c: e.activation(
                    out=dst, in_=yc, func=AF.Silu, bias=lbv[:, c:c + 1], scale=lgv[:, c:c + 1]),
                    reads=[yc, lbv[:, c:c + 1], lgv[:, c:c + 1]], writes=[dst])
        if self.stop <= 4:
            return
        o = T0
        ets, pTs = [], []
        for _ in range(4):
            v, o = self.carve(o, [TT], F32)
            ets.append(v)
        for _ in range(4):
            v, o = self.carve(o, [TT], BF16)
            pTs.append(v)
        den, o = self.carve(o, [4, 128], F32)
        esk, o = self.carve(o, [4], F32)
        assert o <= ARENA
        sch.op("act", lambda e: e.activation(out=esk, in_=skv, func=AF.Exp), reads=[skv], writes=[esk])
        Mv = mswa.rearrange("p (r h b q) -> p r h b q", r=4, h=2, b=2)
        it = 0
        for n in range(min(S // 128, self.kn)):
            A = self.ps[4 + (n % 2) * 2][:]
            B = self.ps[5 + (n % 2) * 2][:]
            kbs = [1] if n == 0 else [0, 1]
            b0 = kbs[0]
            for g in range(2):
                par = it % 2
                it += 1
                Sb = [self.ps[2 * par + hs][:].rearrange("p (r b q) -> p r b q", r=2, b=2) for hs in range(2)]
                etv = [ets[2 * par + hs].rearrange("p (r b q) -> p r b q", r=2, b=2) for hs in range(2)]
                pTv = [pTs[2 * par + hs].rearrange("p (r b q) -> p r b q", r=2, b=2) for hs in range(2)]
                for prl in range(2):
                    pr = 2 * g + prl
                    for bs in kbs:
                        kb = n - 1 + bs
                        for hs in range(2):
                            r0 = hs * 64
                            lh = kdT[r0:r0 + 64, g, kb * 128:(kb + 1) * 128]
                            rh = qT[r0:r0 + 64, pr, n * 128:(n + 1) * 128]
                            po = Sb[hs][:, prl, bs, :]
                            sch.op("pe", lambda e, lh=lh, rh=rh, po=po: e.matmul(po, lhsT=lh, rhs=rh, start=True, stop=True),
                                   reads=[lh, rh], writes=[po], signal=(prl == 1 and bs == kbs[-1]))
                if self.kswa <= 1:
                    continue
                for hs in range(2):
                    si, eo, po_ = Sb[hs][:, :, b0:, :], etv[hs][:, :, b0:, :], pTv[hs][:, :, b0:, :]
                    mi = Mv[:, 2 * g:2 * g + 2, hs, b0:, :]
                    sch.op("act", lambda e, si=si, eo=eo: e.activation(out=eo, in_=si, func=AF.Exp, scale=0.125),
                           reads=[si], writes=[eo])
                    if self.kswa <= 2:
                        continue
                    sch.op("dve", lambda e, eo=eo, po_=po_, mi=mi: e.tensor_tensor(out=po_, in0=eo, in1=mi, op=ALU.mult),
                           reads=[eo, mi], writes=[po_])
                if self.kswa <= 3:
                    continue
                for prl in range(2):
                    pr = 2 * g + prl
                    for hs in range(2):
                        r0 = hs * 64
                        for bs in kbs:
                            kb = n - 1 + bs
                            rh = pTv[hs][:, prl, bs, :]
                            lv = vT[:, kb, g * 64:(g + 1) * 64]
                            pa = A[r0:r0 + 64, pr * 128:(pr + 1) * 128]
                            pb = B[r0:r0 + 64, pr * 128:(pr + 1) * 128]
                            last = (prl == 1 and hs == 1 and bs == kbs[-1])
                            sch.op("pe", lambda e, lv=lv, rh=rh, pa=pa, bs=bs: e.matmul(
                                pa, lhsT=lv, rhs=rh, start=(bs == kbs[0]), stop=(bs == kbs[-1])),
                                reads=[lv, rh], writes=[pa], signal=False)
                            sch.op("pe", lambda e, rh=rh, pb=pb, bs=bs: e.matmul(
                                pb, lhsT=self.ones[:, 0:64], rhs=rh, start=(bs == kbs[0]), stop=(bs == kbs[-1])),
                                reads=[self.ones[:, 0:64], rh], writes=[pb], signal=last)
            if self.kswa <= 4:
                continue
            Bv = B.rearrange("p (r q) -> p r q", r=4)
            Av = A.rearrange("p (r q) -> p r q", r=4)
            ebc = esk.unsqueeze(2).to_broadcast([128, 4, 128])
            sch.op("dve", lambda e, Bv=Bv, ebc=ebc: e.tensor_tensor(out=den, in0=Bv, in1=ebc, op=ALU.add),
                   reads=[B, esk], writes=[den])
            sch.op("dve", lambda e: e.reciprocal(out=den, in_=den), reads=[den], writes=[den])
            dst = boT[:, :, n * 128:(n + 1) * 128]
            sch.op("dve", lambda e, Av=Av, dst=dst: e.tensor_tensor(out=dst, in0=Av, in1=den, op=ALU.mult),
                   reads=[A, den], writes=[dst])
        if self.stop <= 5:
            return
        o = T0
        sqh = []
        for _ in range(2):
            v, o = self.carve(o, [TT], BF16)
            sqh.append(v)
        rs2, o = self.carve(o, [TT], F32)
        self.proj_out_post(l, lambda k, tt: (coT if k < 4 else boT)[:, k % 4, tt * TT:(tt + 1) * TT],
                           hout, sqh, rs2)

    def plan_odd(self, l):
        w = self.win[l].rearrange("(k p) c -> p k c", p=128)
        V = self.wp.view
        self.wp.add([(lambda s_: V(s_, 0, [KC, 16]), w[:, :, 3072:3088])])
        for pr in range(8):
            self.wp.add([
                (lambda s_: V(s_, 0, [KC, 128]), w[:, :, pr * 128:(pr + 1) * 128]),
                (lambda s_: V(s_, 1024, [KC, 128]), w[:, :, 1024 + pr * 128:1024 + (pr + 1) * 128]),
                (lambda s_: V(s_, 2048, [KC, 128]), w[:, :, 2048 + pr * 128:2048 + (pr + 1) * 128])])
        self.plan_proj_out(self.wout[l])

    def mixer_odd(self, l):
        sch, wp = self.sch, self.wp
        i = l // 2
        bfv = self.cst[0:16, DEPTH * 6 * KC + 280 + i:DEPTH * 6 * KC + 280 + i + 1]
        K32 = 32 * 1024
        hT = self.carve(0, [KC, S], BF16)[0]
        hout = self.carve(0, [KC, TT], F32)[0]
        OT, off = self.carve(K32, [KC, S], BF16)
        P0 = off
        qTp, off = self.carve(off, [S], BF16)
        kTp, off = self.carve(off, [S], BF16)
        vaug, off = self.carve(off, [16, 192], BF16)
        Cbs = []
        for _ in range(4):
            v, off = self.carve(off, [TT], BF16)
            Cbs.append(v)
        negc, off = self.carve(off, [16, 16], F32)
        cTb, off = self.carve(off, [S], BF16)
        tmps, pTs = [], []
        for _ in range(3):
            v, off = self.carve(off, [TT], F32)
            tmps.append(v)
        for _ in range(3):
            v, off = self.carve(off, [TT], BF16)
            pTs.append(v)
        rec, off = self.carve(off, [TT], F32)
        sel, off = self.carve(off, [2048], BF16)
        assert off <= ARENA, off
        toks = [None, 0]
        sch.dma("pool", self.ds_c2, sel[0:16, :], self.sel_d, toks, sb_writes=[sel[0:16, :]])
        sqb, o = self.carve(P0, [KC, TT], BF16)
        rs, o = self.carve(o, [TT], F32)
        lf = self.carve(P0, [S], F32)[0]
        cpT = self.carve(tmps[0].offset * 0 + (off - 0), [1], F32)[0] if False else None
        cpT = self.carve(K32, [S], F32)[0]
        nbf, _ = self.carve(K32 + S * 4, [1], F32)
        self.prenorm_full(l, 2, hT, sqb, rs)
        tF = wp.get()
        wf = wp.view(tF[1], 0, [KC, 16])
        sch.op("dve", lambda e: e.tensor_scalar(out=nbf[0:16, :], in0=bfv, scalar1=-1.0, scalar2=None, op0=ALU.mult),
               reads=[bfv], writes=[nbf[0:16, :]])
        for tt in range(S // TT):
            pst = self.ps[self.pp % 2][:]
            self.pp += 1
            for k in range(KC):
                rhs = hT[:, k, tt * TT:(tt + 1) * TT]
                sch.op("pe", lambda e, k=k, rhs=rhs, pst=pst: e.matmul(
                    pst[0:16, :], lhsT=wf[:, k, :], rhs=rhs, start=(k == 0), stop=(k == KC - 1)),
                    reads=[wf[:, k, :], rhs], writes=[pst[0:16, :]], signal=(k == KC - 1))
            dst = lf[0:16, tt * TT:(tt + 1) * TT]
            sch.op("act", lambda e, pst=pst, dst=dst: e.activation(
                out=dst, in_=pst[0:16, :], func=AF.Exp, bias=nbf[0:16, :], scale=-1.0),
                reads=[pst[0:16, :], nbf[0:16, :]], writes=[dst])
        wp.release(tF[0])
        sch.op("act", lambda e: e.activation(out=lf[0:16, :], in_=lf[0:16, :], func=AF.Ln, bias=1.0, scale=1.0),
               reads=[lf[0:16, :]], writes=[lf[0:16, :]])
        sch.op("dve", lambda e: e.tensor_tensor_scan(out=cpT[0:16, :], data0=lf[0:16, :], data1=lf[0:16, :],
                                                     initial=0.0, op0=ALU.add, op1=ALU.bypass),
               reads=[lf[0:16, :]], writes=[cpT[0:16, :]])
        pst = self.ps[7][:]
        for kb in range(16):
            po = pst[:, kb * 16:(kb + 1) * 16]
            src = cpT[0:16, kb * 128:(kb + 1) * 128]
            sch.op("pe", lambda e, po=po, src=src: e.transpose(po, src, self.ident[:]),
                   reads=[src, self.ident[:]], writes=[po], signal=(kb == 15))
        sch.op("act", lambda e: e.activation(out=negc, in_=pst[:, 0:256].rearrange("p (a b) -> p a b", a=16), func=AF.Copy),
               reads=[pst[:, 0:256]], writes=[negc])
        sch.op("dve", lambda e: e.tensor_scalar(out=cTb[0:16, :], in0=cpT[0:16, :], scalar1=-1.0, scalar2=None, op0=ALU.mult),
               reads=[cpT[0:16, :]], writes=[cTb[0:16, :]])
        sch.op("dve", lambda e: e.memset(vaug[:, :, 64:128], 1.0), writes=[vaug[:, :, 64:128]])
        selv = sel.rearrange("p (h c) -> p h c", h=16)
        it = 0
        cbi = 0
        for pr in range(8):
            tp = wp.get()
            wq = wp.view(tp[1], 0, [KC, 128])
            wk = wp.view(tp[1], 1024, [KC, 128])
            wv_ = wp.view(tp[1], 2048, [KC, 128])
            self.fm_chunk(wq, hT, lambda tt, p: sch.op(
                "act", lambda e: e.activation(out=qTp[:, tt * TT:(tt + 1) * TT], in_=p, func=AF.Copy),
                reads=[p], writes=[qTp[:, tt * TT:(tt + 1) * TT]]))
            self.fm_chunk(wk, hT, lambda tt, p: sch.op(
                "dve", lambda e: e.tensor_copy(out=kTp[:, tt * TT:(tt + 1) * TT], in_=p),
                reads=[p], writes=[kTp[:, tt * TT:(tt + 1) * TT]]))
            for tb4 in range(4):
                pst = self.ps[self.pp % 2][:]
                self.pp += 1
                for jb in range(4):
                    tb = tb4 * 4 + jb
                    po = pst[:, jb * 128:(jb + 1) * 128]
                    for k in range(KC):
                        lh = hT[:, k, tb * 128:(tb + 1) * 128]
                        sch.op("pe", lambda e, k=k, lh=lh, po=po: e.matmul(
                            po, lhsT=lh, rhs=wv_[:, k, :], start=(k == 0), stop=(k == KC - 1)),
                            reads=[lh, wv_[:, k, :]], writes=[po], signal=(k == KC - 1))
                srcv = pst.rearrange("p (a b) -> p a b", a=4)
                d0 = vaug[:, tb4 * 4:(tb4 + 1) * 4, 0:64]
                d1 = vaug[:, tb4 * 4:(tb4 + 1) * 4, 128:192]
                sch.op("act", lambda e, d0=d0, srcv=srcv: e.activation(out=d0, in_=srcv[:, :, 0:64], func=AF.Copy),
                       reads=[pst], writes=[d0])
                sch.op("dve", lambda e, d1=d1, srcv=srcv: e.tensor_copy(out=d1, in_=srcv[:, :, 64:128]),
                       reads=[pst], writes=[d1])
            wp.release(tp[0])
            chunks = [(hs, qc) for hs in range(2) for qc in range(4)]
            tiles = []
            for ci, (hs, qc) in enumerate(chunks):
                nkb = 4 * qc + 4
                for kb in range(nkb):
                    tiles.append((ci, hs, qc, kb, nkb))
            cbuf = {}

            def emit_cb(ci):
                nonlocal cbi
                hs, qc = chunks[ci]
                h = 2 * pr + hs
                buf = Cbs[cbi % 4]
                cbi += 1
                cbuf[ci] = buf
                pst = self.ps[7][:]
                rhs = cTb[0:16, qc * TT:(qc + 1) * TT]
                sch.op("pe", lambda e: e.matmul(pst, lhsT=selv[0:16, h, :], rhs=rhs, start=True, stop=True),
                       reads=[selv[0:16, h, :], rhs], writes=[pst])
                sch.op("act", lambda e: e.activation(out=buf, in_=pst, func=AF.Copy), reads=[pst], writes=[buf])

            emit_cb(0)
            emit_cb(1)
            LA = 2
            slots = {}
            for i in range(len(tiles) + LA):
                if i < len(tiles):
                    ci, hs, qc, kb, nkb = tiles[i]
                    h = 2 * pr + hs
                    r0 = hs * 64
                    if kb == 0 and ci + 2 < len(chunks):
                        emit_cb(ci + 2)
                    j = kb - 4 * qc
                    q0 = max(0, j) * 128
                    Sb = self.ps[2 + it % 3][:]
                    tmp = tmps[it % 3]
                    pT = pTs[it % 3]
                    it += 1
                    slots[i] = (pT, q0)
                    lh = kTp[r0:r0 + 64, kb * 128:(kb + 1) * 128]
                    rh = qTp[r0:r0 + 64, qc * TT + q0:(qc + 1) * TT]
                    sch.op("pe", lambda e, lh=lh, rh=rh, Sb=Sb, q0=q0: e.matmul(
                        Sb[:, q0:TT], lhsT=lh, rhs=rh, start=True, stop=True),
                        reads=[lh, rh], writes=[Sb[:, q0:TT]])
                    cbs = cbuf[ci][:, q0:TT]
                    sch.op("dve", lambda e, Sb=Sb, tmp=tmp, cbs=cbs, q0=q0: e.scalar_tensor_tensor(
                        out=tmp[:, q0:TT], in0=Sb[:, q0:TT], scalar=0.125, in1=cbs, op0=ALU.mult, op1=ALU.add),
                        reads=[Sb[:, q0:TT], cbs], writes=[tmp[:, q0:TT]])
                    if j >= 0:
                        dg = tmp[:, q0:q0 + 128]
                        sch.op("dve", lambda e, dg=dg: e.tensor_tensor(out=dg, in0=dg, in1=self.mc[:], op=ALU.add),
                               reads=[dg, self.mc[:]], writes=[dg])
                    nb = negc[:, kb, h:h + 1]
                    sch.op("act", lambda e, tmp=tmp, pT=pT, nb=nb, q0=q0: e.activation(
                        out=pT[:, q0:TT], in_=tmp[:, q0:TT], func=AF.Exp, bias=nb, scale=1.0),
                        reads=[tmp[:, q0:TT], nb], writes=[pT[:, q0:TT]])
                if i - LA >= 0:
                    ci, hs, qc, kb, nkb = tiles[i - LA]
                    pT, q0 = slots.pop(i - LA)
                    O = self.ps[5 + ci % 2][:]
                    lv = vaug[:, kb, hs * 64:hs * 64 + 128]
                    sch.op("pe", lambda e, lv=lv, pT=pT, O=O, q0=q0, kb=kb, nkb=nkb: e.matmul(
                        O[:, q0:TT], lhsT=lv, rhs=pT[:, q0:TT], start=(kb == 0), stop=(kb == nkb - 1)),
                        reads=[lv, pT[:, q0:TT]], writes=[O[:, q0:TT]], signal=(kb == nkb - 1))
                    if kb == nkb - 1:
                        nr = hs * 64
                        dr = 64 - nr
                        sch.op("dve", lambda e, O=O, nr=nr, dr=dr: e.reciprocal(out=rec[nr:nr + 64, :], in_=O[dr:dr + 64, :]),
                               reads=[O[dr:dr + 64, :]], writes=[rec[nr:nr + 64, :]])
                        dst = OT[nr:nr + 64, pr, qc * TT:(qc + 1) * TT]
                        sch.op("dve", lambda e, O=O, nr=nr, dst=dst: e.tensor_tensor(
                            out=dst, in0=O[nr:nr + 64, :], in1=rec[nr:nr + 64, :], op=ALU.mult),
                            reads=[O[nr:nr + 64, :], rec[nr:nr + 64, :]], writes=[dst])
        sqh = []
        o = P0
        for _ in range(2):
            v, o = self.carve(o, [TT], BF16)
            sqh.append(v)
        rs2, o = self.carve(o, [TT], F32)
        self.proj_out_post(l, lambda k, tt: OT[:, k, tt * TT:(tt + 1) * TT], hout, sqh, rs2)


_CACHE = {}


def _get_prog(layers, subs):
    key = (tuple(layers), tuple(subs))
    if key not in _CACHE:
        p = Prog(layers, subs)
        p.build()
        _CACHE[key] = p
    return _CACHE[key]


def _consts(inputs):
    ng = np.asarray(inputs["norm_g"], dtype=np.float32)
    cst = np.zeros((128, NCST), np.float32)
    cst[:, 0:DEPTH * 6 * KC] = ng.reshape(DEPTH, 6, KC, 128).transpose(3, 0, 1, 2).reshape(128, -1)
    base = DEPTH * 6 * KC
    for i in range(2):
        b0 = base + i * 140
        cw = np.asarray(inputs["conv_w"][i], np.float32)
        cst[:, b0:b0 + 124] = cw.reshape(CONV_K, 4, 128).transpose(2, 1, 0).reshape(128, 124)
        cst[:, b0 + 124:b0 + 128] = np.asarray(inputs["conv_b"][i], np.float32).reshape(4, 128).T
        cst[:, b0 + 128:b0 + 132] = np.asarray(inputs["conv_ln_g"][i], np.float32).reshape(4, 128).T
        cst[:, b0 + 132:b0 + 136] = np.asarray(inputs["conv_ln_b"][i], np.float32).reshape(4, 128).T
        sk = np.asarray(inputs["swa_sinks"][i], np.float32)
        cst[0:64, b0 + 136:b0 + 140] = sk[0::2][None, :]
        cst[64:128, b0 + 136:b0 + 140] = sk[1::2][None, :]
        cst[0:16, base + 280 + i] = np.asarray(inputs["fox_b_f"][i], np.float32)
    k = np.arange(128)[:, None].astype(np.float64)
    q = np.arange(128)[None, :].astype(np.float64)
    mswa = np.zeros((128, 8, 2, 128), np.float64)
    for h in range(8):
        slope = 2.0 ** (-(h + 1))
        d0 = 128 + q - k
        mswa[:, h, 0, :] = np.exp(-slope * d0) * (q < k)
        d1 = q - k
        mswa[:, h, 1, :] = np.exp(-slope * d1) * (q >= k)
    mc = np.where(q >= k, 0.0, NEGBIG).astype(np.float32)
    sel = np.zeros((16, 16, 128), np.float32)
    for h in range(16):
        sel[h, h, :] = 1.0
    return {"cst": cst, "mswa": mswa.reshape(128, 2048).astype(np.float32), "mc": mc,
            "sel": sel.reshape(16, 2048), "ident": np.eye(16, dtype=np.float32),
            "identb": np.eye(128, dtype=np.float32)}


def _run(inputs, layers, subs=("f1", "mix", "f2"), x_override=None):
    x = np.asarray(inputs["x"], dtype=np.float32) if x_override is None else x_override
    shared = _consts(inputs)
    f32c = lambda a: np.ascontiguousarray(a, dtype=np.float32)
    for l in layers:
        for j, sub in ((0, "f1"), (1, "f2")):
            if sub in subs:
                shared[f"wg{l}{j}"] = f32c(inputs["ffn_w_gate"][l, j])
                shared[f"wu{l}{j}"] = f32c(inputs["ffn_w_up"][l, j])
                shared[f"wd{l}{j}"] = f32c(inputs["ffn_w_down"][l, j])
        if "mix" in subs:
            if l % 2 == 0:
                shared[f"win{l}"] = f32c(inputs["ab_w_in"][l // 2])
                shared[f"wout{l}"] = f32c(inputs["ab_w_out"][l // 2])
            else:
                shared[f"win{l}"] = f32c(inputs["fox_w_in"][l // 2])
                shared[f"wout{l}"] = f32c(inputs["fox_w_out"][l // 2])
    p = _get_prog(layers, subs)
    in_maps = []
    for b in range(NCORES):
        m = dict(shared)
        m["xT"] = np.ascontiguousarray(x[b].T)
        in_maps.append(m)
    import time as _t
    _t0 = _t.time()
    import os as _os
    if _os.environ.get("KTRACE"):
        res = run_bass_kernel_spmd(p.nc, in_maps, core_ids=list(range(NCORES)), trace=True)
        print("[kernel] exec_time_ns", res.exec_time_ns, flush=True)
    else:
        res = run_bass_kernel_spmd(p.nc, in_maps, core_ids=list(range(NCORES)))
    print("[kernel] launch wall s", round(_t.time() - _t0, 1), flush=True)
    out = np.stack([np.ascontiguousarray(r["yT"].T) for r in res.results], axis=0)
    return out.astype(np.float32)


LAUNCH_GROUPS = [[0, 1, 2, 3]]


def kernel(**inputs):
    x = None
    for grp in LAUNCH_GROUPS:
        x = _run(inputs, grp, x_override=x)
    return x
```

```python
import contextlib
import numpy as np
import concourse.bass as bass
import concourse.mybir as mybir
from concourse.bass_utils import run_bass_kernel_spmd

F32 = mybir.dt.float32
BF16 = mybir.dt.bfloat16
AF = mybir.ActivationFunctionType
ALU = mybir.AluOpType
ESZ = {F32: 4, BF16: 2}

D = 1024
S = 2048
DFF = 2816
KC = D // 128
FCH = DFF // 128
DEPTH = 4
EPS = 1e-6
NCORES = 8
TT = 512
TG = 1024
SLOT = 6144
NSLOT = 3
ARENA = 104 * 1024
NCST = 474
HD = 64
CONV_K = 31
NEGBIG = -30000.0


def _esz(dt):
    return ESZ[dt]


class Sched:
    def __init__(self, nc, es):
        self.nc = nc
        self.es = es
        self.eng = {}
        for name, obj in (("pe", nc.tensor), ("act", nc.scalar), ("dve", nc.vector),
                          ("pool", nc.gpsimd), ("sp", nc.sync)):
            sem = es.enter_context(nc.semaphore("sem_" + name))
            self.eng[name] = dict(obj=obj, sem=sem, cnt=0, waited={})
        self.recs = {}
        self.nwaits = 0
        self.nops = 0

    @staticmethod
    def region(ap):
        name = ap.tensor.name
        a = ap.ap
        esz = _esz(ap.dtype)
        pstep, pcnt = a[0]
        off = ap.offset
        if pstep > 0:
            p0 = off // pstep
            lo = off % pstep
        else:
            p0 = 0
            lo = off
        hi = lo + 1
        for st, c in a[1:]:
            hi += (c - 1) * abs(st)
        return (name, p0, p0 + pcnt, lo * esz, hi * esz)

    @staticmethod
    def _ov(r, q):
        return r[1] < q[2] and q[1] < r[2] and r[3] < q[4] and q[3] < r[4]

    @staticmethod
    def _contains(outer, inner):
        return (outer[1] <= inner[1] and inner[2] <= outer[2]
                and outer[3] <= inner[3] and inner[4] <= outer[4])

    def _collect(self, reads, writes):
        deps = {}

        def add(tok):
            h, v = tok[0], tok[1]
            k = h.name
            if k not in deps or deps[k][1] < v:
                deps[k] = (h, v)

        rregs = [self.region(a) for a in reads]
        wregs = [self.region(a) for a in writes]
        for r in rregs:
            rec = self.recs.get(r[0])
            if rec:
                for q, tok in rec["w"]:
                    if self._ov(r, q):
                        add(tok)
        for r in wregs:
            rec = self.recs.get(r[0])
            if rec:
                for q, tok in rec["w"]:
                    if self._ov(r, q):
                        add(tok)
                for q, tok in rec["r"]:
                    if self._ov(r, q):
                        add(tok)
        return deps, rregs, wregs

    def _emit_waits(self, E, deps, skip_self):
        need = []
        for k, (h, v) in deps.items():
            if skip_self and h is E["sem"]:
                continue
            if E["waited"].get(k, 0) >= v:
                continue
            need.append((k, h, v))
        for (k, h, v) in need[:-1]:
            E["obj"].wait_ge(h, v)
            E["waited"][k] = v
            self.nwaits += 1
        if need:
            k, h, v = need[-1]
            E["waited"][k] = v
            return (h, v)
        return None

    def _record(self, rregs, wregs, tok):
        for r in rregs:
            rec = self.recs.setdefault(r[0], {"w": [], "r": []})
            found = False
            for i, (q, t) in enumerate(rec["r"]):
                if q == r and t[0] is tok[0]:
                    if t[1] < tok[1]:
                        rec["r"][i] = (q, tok)
                    found = True
                    break
            if not found:
                rec["r"].append((r, tok))
        for r in wregs:
            rec = self.recs.setdefault(r[0], {"w": [], "r": []})
            rec["w"] = [(q, t) for (q, t) in rec["w"] if not self._contains(r, q)]
            rec["r"] = [(q, t) for (q, t) in rec["r"] if not self._contains(r, q)]
            rec["w"].append((r, tok))

    def op(self, eng, fn, reads=(), writes=(), signal=True, extra=()):
        E = self.eng[eng]
        deps, rregs, wregs = self._collect(reads, writes)
        for tok in extra:
            k = tok[0].name
            if k not in deps or deps[k][1] < tok[1]:
                deps[k] = (tok[0], tok[1])
        w = self._emit_waits(E, deps, skip_self=(eng == "pe"))
        ins = fn(E["obj"])
        if w is not None:
            ins._wait_ge(w[0], w[1])
        self.nops += 1
        if signal:
            E["cnt"] += 1
            ins.then_inc(E["sem"], 1)
            tok = [E["sem"], E["cnt"]]
        else:
            tok = [E["sem"], E["cnt"] + 1]
        self._record(rregs, wregs, tok)
        return tok

    def new_dsem(self, name):
        sem = self.es.enter_context(self.nc.semaphore(name))
        return dict(sem=sem, cnt=0)

    def dma(self, queue, ds, out, in_, tok, sb_reads=(), sb_writes=(), extra=()):
        E = self.eng[queue]
        deps, rregs, wregs = self._collect(sb_reads, sb_writes)
        for t in extra:
            k = t[0].name
            if k not in deps or deps[k][1] < t[1]:
                deps[k] = (t[0], t[1])
        w = self._emit_waits(E, deps, skip_self=False)
        ins = E["obj"].dma_start(out=out, in_=in_)
        if w is not None:
            ins._wait_ge(w[0], w[1])
        ins.then_inc(ds["sem"], 16)
        ds["cnt"] += 16
        tok[0] = ds["sem"]
        tok[1] = ds["cnt"]
        self._record(rregs, wregs, tok)
        self.nops += 1


class WeightPool:
    def __init__(self, sch, ring, nslot, slot_elems):
        self.sch = sch
        self.ring = ring
        self.nslot = nslot
        self.slot = slot_elems
        self.plan = []
        self.next_load = 0
        self.next_get = 0
        self.free = list(range(nslot))
        self.slot_of = {}
        self.dsems = [sch.new_dsem(f"wsem{i}") for i in range(nslot)]

    def add(self, tile):
        self.plan.append(tile)

    def view(self, slot, off, shape):
        n = 1
        for s in shape:
            n *= s
        base = slot * self.slot + off
        v = self.ring[:, base:base + n]
        if len(shape) == 2:
            v = v.rearrange("p (a b) -> p a b", a=shape[0])
        elif len(shape) == 3:
            v = v.rearrange("p (a b c) -> p a b c", a=shape[0], b=shape[1])
        return v

    def prefetch(self):
        while self.free and self.next_load < len(self.plan):
            slot = self.free.pop(0)
            idx = self.next_load
            self.next_load += 1
            self.slot_of[idx] = slot
            tok = [None, 0]
            for (dst_fn, src) in self.plan[idx]:
                dst = dst_fn(slot)
                self.sch.dma("pool", self.dsems[slot], dst, src, tok, sb_writes=[dst])

    def get(self):
        self.prefetch()
        idx = self.next_get
        self.next_get += 1
        if idx not in self.slot_of:
            self.prefetch()
        assert idx in self.slot_of, "weight pool starved (release missing?)"
        return idx, self.slot_of[idx]

    def release(self, idx):
        self.free.append(self.slot_of[idx])
        self.prefetch()


class Prog:
    def __init__(self, layers, subs=("f1", "mix", "f2")):
        self.layers = list(layers)
        self.subs = subs
        self.nc = bass.Bass("TRN2", target_bir_lowering=False)
        self.es = contextlib.ExitStack()
        self.pp = 0
        import os
        self.stop = int(os.environ.get('KSTOP', '99'))
        self.kswa = int(os.environ.get('KSWA', '99'))
        self.kn = int(os.environ.get('KN', '99'))

    def dram_in(self, name, shape):
        return self.nc.dram_tensor(name, list(shape), F32, kind="ExternalInput").ap()

    def build(self):
        nc = self.nc
        with self.es as es:
            self.sch = Sched(nc, es)
            sch = self.sch
            self.xT = self.dram_in("xT", [D, S])
            self.cst_d = self.dram_in("cst", [128, NCST])
            self.mswa_d = self.dram_in("mswa", [128, 2048])
            self.mc_d = self.dram_in("mc", [128, 128])
            self.sel_d = self.dram_in("sel", [16, 2048])
            self.ident_d = self.dram_in("ident", [16, 16])
            self.identb_d = self.dram_in("identb", [128, 128])
            self.win, self.wout = {}, {}
            if "mix" in self.subs:
                for l in self.layers:
                    self.win[l] = self.dram_in(f"win{l}", [D, 1792 if l % 2 == 0 else 3088])
                    self.wout[l] = self.dram_in(f"wout{l}", [D, D])
            self.wg, self.wu, self.wd = {}, {}, {}
            for l in self.layers:
                for j, sub in ((0, "f1"), (1, "f2")):
                    if sub in self.subs:
                        self.wg[l, j] = self.dram_in(f"wg{l}{j}", [D, DFF])
                        self.wu[l, j] = self.dram_in(f"wu{l}{j}", [D, DFF])
                        self.wd[l, j] = self.dram_in(f"wd{l}{j}", [DFF, D])
            self.yT = nc.dram_tensor("yT", [D, S], F32, kind="ExternalOutput").ap()
            sb = lambda n, shp, dt: es.enter_context(nc.sbuf_tensor(n, shp, dt))
            self.xs = sb("xs", [128, KC, S], F32)
            self.ring = sb("ring", [128, NSLOT * SLOT], BF16)
            self.cst = sb("cst_sb", [128, NCST], F32)
            self.gT = self.cst[:, 0:DEPTH * 6 * KC]
            self.identb = sb("identb_sb", [128, 128], BF16)
            self.mc = sb("mc_sb", [128, 128], F32)
            self.ident = sb("ident_sb", [16, 16], F32)
            self.g32 = sb("g32", [128, DEPTH * 6 * KC], F32)
            self.ones = sb("ones", [128, 128], BF16)
            self.arena = sb("arena", [128, ARENA], mybir.dt.uint8)
            self.ps = [es.enter_context(nc.psum_tensor(f"ps{i}", [128, TT], F32)) for i in range(8)]
            self.wp = WeightPool(sch, self.ring, NSLOT, SLOT)
            self.ds_x = sch.new_dsem("ds_x")
            self.ds_c = sch.new_dsem("ds_c")
            self.ds_c2 = sch.new_dsem("ds_c2")
            self.ds_o = sch.new_dsem("ds_o")

            for l in self.layers:
                for sub in self.subs:
                    if sub == "f1":
                        self.plan_ffn(l, 0)
                    elif sub == "f2":
                        self.plan_ffn(l, 1)
                    elif sub == "mix":
                        (self.plan_even if l % 2 == 0 else self.plan_odd)(l)

            tok = [None, 0]
            for k in range(KC):
                sch.dma("sp", self.ds_x, self.xs[:, k, :], self.xT[k * 128:(k + 1) * 128, :], tok,
                        sb_writes=[self.xs[:, k, :]])
            tok = [None, 0]
            sch.dma("sp", self.ds_c, self.cst[:], self.cst_d, tok, sb_writes=[self.cst[:]])
            sch.dma("sp", self.ds_c, self.mc[:], self.mc_d, tok, sb_writes=[self.mc[:]])
            sch.dma("sp", self.ds_c, self.ident[:], self.ident_d, tok, sb_writes=[self.ident[:]])
            tok = [None, 0]
            sch.dma("pool", self.ds_c2, self.identb[:], self.identb_d, tok, sb_writes=[self.identb[:]])
            sch.op("dve", lambda e: e.memset(self.ones[:], 1.0), writes=[self.ones[:]])
            self.wp.prefetch()
            gv = self.gT.rearrange("p (l n k) -> p l n k", l=DEPTH, n=6)
            g32v = self.g32[:].rearrange("p (l n k) -> p l n k", l=DEPTH, n=6)
            for n in range(6):
                f = 16.0 if n in (1, 5) else 32.0
                sch.op("dve", lambda e, n=n, f=f: e.tensor_scalar(
                    out=g32v[:, :, n, :], in0=gv[:, :, n, :], scalar1=f, scalar2=None, op0=ALU.mult),
                    reads=[gv[:, :, n, :]], writes=[g32v[:, :, n, :]])

            for l in self.layers:
                for sub in self.subs:
                    if sub == "f1":
                        self.ffn(l, 0)
                    elif sub == "f2":
                        self.ffn(l, 1)
                    elif sub == "mix":
                        (self.mixer_even if l % 2 == 0 else self.mixer_odd)(l)

            tok = [None, 0]
            for k in range(KC):
                sch.dma("sp", self.ds_o, self.yT[k * 128:(k + 1) * 128, :], self.xs[:, k, :], tok,
                        sb_reads=[self.xs[:, k, :]])
            nc.sync.wait_ge(self.ds_o["sem"], self.ds_o["cnt"])
        return nc

    def gcol(self, l, n, k):
        c = (l * 6 + n) * KC + k
        return self.g32[:, c:c + 1]

    def carve(self, off, shape, dt):
        n = 1
        for s in shape:
            n *= s
        nb = n * _esz(dt)
        v = self.arena[:, off:off + nb].bitcast(dt)
        if len(shape) == 2:
            v = v.rearrange("p (a b) -> p a b", a=shape[0])
        elif len(shape) == 3:
            v = v.rearrange("p (a b c) -> p a b c", a=shape[0], b=shape[1])
        return v, off + nb

    GU_STAGES = [(0, 3), (3, 6), (6, 9), (9, 12), (12, 15), (15, 18), (18, 21), (21, 22)]

    def plan_ffn(self, l, j):
        wg = self.wg[l, j].rearrange("(k p) c -> p k c", p=128)
        wu = self.wu[l, j].rearrange("(k p) c -> p k c", p=128)
        wd = self.wd[l, j].rearrange("(f p) c -> p f c", p=128)
        for g in range(S // TG):
            for (c0, c1) in self.GU_STAGES:
                n = (c1 - c0) * 128
                self.wp.add([(lambda s_, n=n: self.wp.view(s_, 0, [KC, n]), wg[:, :, c0 * 128:c1 * 128]),
                             (lambda s_, n=n: self.wp.view(s_, KC * n, [KC, n]), wu[:, :, c0 * 128:c1 * 128])])
            for tt in range(TG // TT):
                for dp in range(4):
                    self.wp.add([(lambda s_: self.wp.view(s_, 0, [FCH, 256]), wd[:, :, dp * 256:(dp + 1) * 256])])

    def rstd_from_stats(self, st_ps, rs):
        self.sch.op("act", lambda e: e.activation(out=rs, in_=st_ps, func=AF.Sqrt, bias=float(D * EPS), scale=1.0),
                    reads=[st_ps], writes=[rs])
        self.sch.op("dve", lambda e: e.reciprocal(out=rs, in_=rs), reads=[rs], writes=[rs])

    def prenorm_tile(self, l, n, t0, hdst, sqb, rs, st_ps):
        sch = self.sch
        xin = self.xs[:, :, t0:t0 + TT]
        sch.op("act", lambda e: e.activation(out=sqb, in_=xin, func=AF.Square), reads=[xin], writes=[sqb])
        for k in range(KC):
            sch.op("pe", lambda e, k=k: e.matmul(st_ps, lhsT=self.ones[:], rhs=sqb[:, k, :],
                                                 start=(k == 0), stop=(k == KC - 1)),
                   reads=[self.ones[:], sqb[:, k, :]], writes=[st_ps], signal=(k == KC - 1))
        self.rstd_from_stats(st_ps, rs)
        for k in range(KC):
            xi = self.xs[:, k, t0:t0 + TT]
            sch.op("dve", lambda e, k=k, xi=xi: e.scalar_tensor_tensor(
                out=hdst[:, k, :], in0=xi, scalar=self.gcol(l, n, k), in1=rs, op0=ALU.mult, op1=ALU.mult),
                reads=[xi, self.gcol(l, n, k), rs], writes=[hdst[:, k, :]])

    def postnorm_update(self, l, n, t0, hout, rs):
        sch = self.sch
        for k in range(KC):
            xi = self.xs[:, k, t0:t0 + TT]
            hk = hout[:, k, :]
            sch.op("dve", lambda e, k=k, hk=hk: e.scalar_tensor_tensor(
                out=hk, in0=hk, scalar=self.gcol(l, n, k), in1=rs, op0=ALU.mult, op1=ALU.mult),
                reads=[hk, self.gcol(l, n, k), rs], writes=[hk])
            sch.op("dve", lambda e, xi=xi, hk=hk: e.tensor_tensor(out=xi, in0=xi, in1=hk, op=ALU.add),
                   reads=[xi, hk], writes=[xi])

    def ffn(self, l, j):
        sch = self.sch
        wp = self.wp
        n_pre, n_post = (0, 1) if j == 0 else (4, 5)
        ntt = TG // TT
        ng = S // TG
        off = 0
        hTs, houts = [], []
        for _ in range(2):
            houts.append(self.arena[:, off:off + KC * TT * 4].bitcast(F32).rearrange("p (a b) -> p a b", a=KC))
            v, off = self.carve(off, [KC, TG], BF16)
            hTs.append(v)
        actT, off = self.carve(off, [FCH, TG], BF16)
        sqb, off = self.carve(off, [KC, TT], BF16)
        rs, off = self.carve(off, [TT], F32)
        rs2, off = self.carve(off, [TT], F32)
        sgt = []
        for i in range(2):
            v, off = self.carve(off, [TT], BF16)
            sgt.append(v)
        sqh = []
        for i in range(2):
            v, off = self.carve(off, [TT], BF16)
            sqh.append(v)
        assert off <= ARENA
        psG = [self.ps[0][:], self.ps[1][:]]
        psU = [self.ps[2][:], self.ps[3][:]]
        psD = [self.ps[4][:], self.ps[5][:]]
        stp = self.ps[6][:]
        stq = self.ps[7][:]
        it = 0

        def prenorm(g, tt):
            hT = hTs[g % 2]
            self.prenorm_tile(l, n_pre, g * TG + tt * TT, hT[:, :, tt * TT:(tt + 1) * TT], sqb, rs, stq)

        for tt in range(ntt):
            prenorm(0, tt)
        for g in range(ng):
            g0 = g * TG
            hT = hTs[g % 2]
            hout = houts[g % 2]
            for (c0, c1) in self.GU_STAGES:
                idx, slot = wp.get()
                n = (c1 - c0) * 128
                wgv = wp.view(slot, 0, [KC, n])
                wuv = wp.view(slot, KC * n, [KC, n])
                for c in range(c0, c1):
                    cl = (c - c0) * 128
                    for tt in range(ntt):
                        b = it % 2
                        it += 1
                        hsl = hT[:, :, tt * TT:(tt + 1) * TT]
                        for (wv, pst) in ((wgv, psG[b]), (wuv, psU[b])):
                            for k in range(KC):
                                sch.op("pe", lambda e, k=k, wv=wv, pst=pst: e.matmul(
                                    pst, lhsT=wv[:, k, cl:cl + 128], rhs=hsl[:, k, :],
                                    start=(k == 0), stop=(k == KC - 1)),
                                    reads=[wv[:, k, cl:cl + 128], hsl[:, k, :]], writes=[pst],
                                    signal=(k == KC - 1))
                        sch.op("act", lambda e, b=b: e.activation(out=sgt[b], in_=psG[b], func=AF.Silu),
                               reads=[psG[b]], writes=[sgt[b]])
                        dst = actT[:, c, tt * TT:(tt + 1) * TT]
                        sch.op("dve", lambda e, b=b, dst=dst: e.tensor_tensor(
                            out=dst, in0=psU[b], in1=sgt[b], op=ALU.mult),
                            reads=[psU[b], sgt[b]], writes=[dst])
                wp.release(idx)
            for tt in range(ntt):
                t0 = g0 + tt * TT
                asl = actT[:, :, tt * TT:(tt + 1) * TT]
                pend = None
                for dp in range(4):
                    idx, slot = wp.get()
                    wdv = wp.view(slot, 0, [FCH, 256])
                    for dd in range(2):
                        dc = dp * 2 + dd
                        b = dc % 2
                        for f in range(FCH):
                            sch.op("pe", lambda e, f=f, b=b, dd=dd: e.matmul(
                                psD[b], lhsT=wdv[:, f, dd * 128:(dd + 1) * 128], rhs=asl[:, f, :],
                                start=(f == 0), stop=(f == FCH - 1)),
                                reads=[wdv[:, f, dd * 128:(dd + 1) * 128], asl[:, f, :]], writes=[psD[b]],
                                signal=(f == FCH - 1))
                        if pend is not None:
                            pend()
                        hk = hout[:, dc, :]
                        sch.op("act", lambda e, b=b, hk=hk: e.activation(out=hk, in_=psD[b], func=AF.Copy),
                               reads=[psD[b]], writes=[hk])
                        sch.op("act", lambda e, b=b: e.activation(out=sqh[b], in_=psD[b], func=AF.Square),
                               reads=[psD[b]], writes=[sqh[b]])
                        pend = (lambda b=b, dc=dc: sch.op("pe", lambda e: e.matmul(
                            stp, lhsT=self.ones[:], rhs=sqh[b], start=(dc == 0), stop=(dc == KC - 1)),
                            reads=[self.ones[:], sqh[b]], writes=[stp], signal=True))
                    wp.release(idx)
                    if g + 1 < ng and tt == 0 and dp < ntt:
                        prenorm(g + 1, dp)
                pend()
                self.rstd_from_stats(stp, rs2)
                self.postnorm_update(l, n_post, t0, hout, rs2)

    def plan_proj_out(self, w):
        wv = w.rearrange("(k p) c -> p k c", p=128)
        for hh in range(2):
            self.wp.add([(lambda s_: self.wp.view(s_, 0, [KC, 512]), wv[:, :, hh * 512:(hh + 1) * 512])])

    def fm_chunk(self, wv, hT, evac):
        sch = self.sch
        for tt in range(S // TT):
            pst = self.ps[self.pp % 2][:]
            self.pp += 1
            for k in range(KC):
                rhs = hT[:, k, tt * TT:(tt + 1) * TT]
                sch.op("pe", lambda e, k=k, rhs=rhs, pst=pst: e.matmul(
                    pst, lhsT=wv[:, k, :], rhs=rhs, start=(k == 0), stop=(k == KC - 1)),
                    reads=[wv[:, k, :], rhs], writes=[pst], signal=(k == KC - 1))
            evac(tt, pst)

    def prenorm_full(self, l, n, hT, sqb, rs):
        for tt in range(S // TT):
            self.prenorm_tile(l, n, tt * TT, hT[:, :, tt * TT:(tt + 1) * TT], sqb, rs, self.ps[6 + tt % 2][:])

    def proj_out_post(self, l, rhs_fn, hout, sqh, rs2):
        sch, wp = self.sch, self.wp
        t1 = wp.get()
        t2 = wp.get()
        psD = [self.ps[0][:], self.ps[1][:]]
        stp = [self.ps[2][:], self.ps[3][:]]
        for tt in range(S // TT):
            st = stp[tt % 2]
            pend = None
            for dc in range(KC):
                tile = t1 if dc < 4 else t2
                wv = wp.view(tile[1], 0, [KC, 512])
                b = dc % 2
                c0 = (dc % 4) * 128
                for k in range(KC):
                    rhs = rhs_fn(k, tt)
                    sch.op("pe", lambda e, k=k, rhs=rhs, wv=wv, b=b, c0=c0: e.matmul(
                        psD[b], lhsT=wv[:, k, c0:c0 + 128], rhs=rhs, start=(k == 0), stop=(k == KC - 1)),
                        reads=[wv[:, k, c0:c0 + 128], rhs], writes=[psD[b]], signal=(k == KC - 1))
                if pend is not None:
                    pend()
                hk = hout[:, dc, :]
                sch.op("act", lambda e, b=b, hk=hk: e.activation(out=hk, in_=psD[b], func=AF.Copy),
                       reads=[psD[b]], writes=[hk])
                sch.op("act", lambda e, b=b: e.activation(out=sqh[b], in_=psD[b], func=AF.Square),
                       reads=[psD[b]], writes=[sqh[b]])
                pend = (lambda b=b, dc=dc, st=st: sch.op("pe", lambda e: e.matmul(
                    st, lhsT=self.ones[:], rhs=sqh[b], start=(dc == 0), stop=(dc == KC - 1)),
                    reads=[self.ones[:], sqh[b]], writes=[st], signal=True))
            pend()
            self.rstd_from_stats(st, rs2)
            self.postnorm_update(l, 3, tt * TT, hout, rs2)
        wp.release(t1[0])
        wp.release(t2[0])

    def plan_even(self, l):
        w = self.win[l].rearrange("(k p) c -> p k c", p=128)
        V = self.wp.view
        self.wp.add([(lambda s_: V(s_, 0, [KC, 768]), w[:, :, 0:768])])
        self.wp.add([(lambda s_: V(s_, 0, [KC, 768]), w[:, :, 768:1536])])
        t3 = []
        for g in range(2):
            for hh in range(2):
                t3.append((lambda s_, g=g, hh=hh: V(s_, g * 1024, [KC, 128])[:, :, hh * 64:(hh + 1) * 64],
                           w[:, :, 1536 + g * 64:1536 + (g + 1) * 64]))
        t3.append((lambda s_: V(s_, 2048, [KC, 128]), w[:, :, 1664:1792]))
        self.wp.add(t3)
        self.plan_proj_out(self.wout[l])

    def mixer_even(self, l):
        sch, wp = self.sch, self.wp
        i = l // 2
        cb0 = DEPTH * 6 * KC + i * 140
        cst = self.cst
        cwv = cst[:, cb0:cb0 + 124].rearrange("p (c j) -> p c j", c=4)
        cbv = cst[:, cb0 + 124:cb0 + 128]
        lgv = cst[:, cb0 + 128:cb0 + 132]
        lbv = cst[:, cb0 + 132:cb0 + 136]
        skv = cst[:, cb0 + 136:cb0 + 140]
        K32 = 32 * 1024
        hT = self.carve(0, [KC, S], BF16)[0]
        y = self.carve(0, [4, S], F32)[0]
        boT = self.carve(0, [4, S], BF16)[0]
        hout = self.carve(16 * 1024, [KC, TT], F32)[0]
        aT, off = self.carve(K32, [4, S + 30], BF16)
        coT = self.carve(K32, [4, S], BF16)[0]
        off = (off + 63) // 64 * 64
        qT, off = self.carve(off, [4, S], BF16)
        kdT, off = self.carve(off, [2, S], BF16)
        vT, off = self.carve(off, [16, 128], BF16)
        mswa, off = self.carve(off, [2048], BF16)
        T0 = off
        assert T0 + 16 * 1024 <= ARENA, T0
        tokm = [None, 0]
        sch.dma("pool", self.ds_c2, mswa, self.mswa_d, tokm, sb_writes=[mswa])
        sqb, o = self.carve(T0, [KC, TT], BF16)
        rs, o = self.carve(o, [TT], F32)
        self.prenorm_full(l, 2, hT, sqb, rs)
        if self.stop <= 1:
            return
        sgs = []
        o = T0
        for _ in range(2):
            v, o = self.carve(o, [S], BF16)
            sgs.append(v)
        t1 = wp.get()
        t2 = wp.get()

        def wview(col):
            tile = t1 if col < 768 else t2
            v = wp.view(tile[1], 0, [KC, 768])
            lc = col % 768
            return v[:, :, lc:lc + 128]

        for c in range(4):
            sch.op("dve", lambda e, c=c: e.memset(aT[:, c, 0:30], 0.0), writes=[aT[:, c, 0:30]])
        for c in range(4):
            sg = sgs[c % 2]
            self.fm_chunk(wview(512 + 128 * c), hT, lambda tt, p, sg=sg: sch.op(
                "act", lambda e: e.activation(out=sg[:, tt * TT:(tt + 1) * TT], in_=p, func=AF.Sigmoid),
                reads=[p], writes=[sg[:, tt * TT:(tt + 1) * TT]]))
            self.fm_chunk(wview(128 * c), hT, lambda tt, p, sg=sg, c=c: sch.op(
                "dve", lambda e: e.tensor_tensor(out=aT[:, c, 30 + tt * TT:30 + (tt + 1) * TT], in0=p,
                                                 in1=sg[:, tt * TT:(tt + 1) * TT], op=ALU.mult),
                reads=[p, sg[:, tt * TT:(tt + 1) * TT]], writes=[aT[:, c, 30 + tt * TT:30 + (tt + 1) * TT]]))
        wp.release(t1[0])
        for pr in range(4):
            self.fm_chunk(wview(1024 + 128 * pr), hT, lambda tt, p, pr=pr: sch.op(
                "act", lambda e: e.activation(out=qT[:, pr, tt * TT:(tt + 1) * TT], in_=p, func=AF.Copy),
                reads=[p], writes=[qT[:, pr, tt * TT:(tt + 1) * TT]]))
        wp.release(t2[0])
        t3 = wp.get()
        for g in range(2):
            wv = wp.view(t3[1], g * 1024, [KC, 128])
            self.fm_chunk(wv, hT, lambda tt, p, g=g: sch.op(
                "dve", lambda e: e.tensor_copy(out=kdT[:, g, tt * TT:(tt + 1) * TT], in_=p),
                reads=[p], writes=[kdT[:, g, tt * TT:(tt + 1) * TT]]))
        wvv = wp.view(t3[1], 2048, [KC, 128])
        for tb4 in range(4):
            pst = self.ps[self.pp % 2][:]
            self.pp += 1
            for jb in range(4):
                tb = tb4 * 4 + jb
                po = pst[:, jb * 128:(jb + 1) * 128]
                for k in range(KC):
                    lh = hT[:, k, tb * 128:(tb + 1) * 128]
                    sch.op("pe", lambda e, k=k, lh=lh, po=po: e.matmul(
                        po, lhsT=lh, rhs=wvv[:, k, :], start=(k == 0), stop=(k == KC - 1)),
                        reads=[lh, wvv[:, k, :]], writes=[po], signal=(k == KC - 1))
            dst = vT[:, tb4 * 4:(tb4 + 1) * 4, :]
            src = pst.rearrange("p (a b) -> p a b", a=4)
            sch.op("act", lambda e, dst=dst, src=src: e.activation(out=dst, in_=src, func=AF.Copy),
                   reads=[pst], writes=[dst])
        wp.release(t3[0])
        if self.stop <= 2:
            return
        Dgs = []
        o = T0
        for _ in range(2):
            v, o = self.carve(o, [CONV_K, 128], BF16)
            Dgs.append(v)
        for c in range(4):
            Dg = Dgs[c % 2]
            idb = self.identb[:].unsqueeze(1).to_broadcast([128, CONV_K, 128])
            cwb = cwv[:, c, :].unsqueeze(2).to_broadcast([128, CONV_K, 128])
            sch.op("dve", lambda e, Dg=Dg, idb=idb, cwb=cwb: e.tensor_tensor(out=Dg, in0=idb, in1=cwb, op=ALU.mult),
                   reads=[self.identb[:], cwv[:, c, :]], writes=[Dg])
            for tt in range(S // TT):
                pst = self.ps[self.pp % 2][:]
                self.pp += 1
                for j in range(CONV_K):
                    rhs = aT[:, c, tt * TT + j:tt * TT + j + TT]
                    sch.op("pe", lambda e, j=j, rhs=rhs, pst=pst, Dg=Dg: e.matmul(
                        pst, lhsT=Dg[:, j, :], rhs=rhs, start=(j == 0), stop=(j == CONV_K - 1)),
                        reads=[Dg[:, j, :], rhs], writes=[pst], signal=(j == CONV_K - 1))
                yc = y[:, c, tt * TT:(tt + 1) * TT]
                sch.op("act", lambda e, yc=yc, pst=pst, c=c: e.activation(
                    out=yc, in_=pst, func=AF.Identity, bias=cbv[:, c:c + 1], scale=1.0),
                    reads=[pst, cbv[:, c:c + 1]], writes=[yc])
        if self.stop <= 3:
            return
        o = T0
        yb, o = self.carve(o, [4, TT], BF16)
        ysq, o = self.carve(o, [4, TT], BF16)
        mu, o = self.carve(o, [TT], F32)
        var, o = self.carve(o, [TT], F32)
        for tt in range(S // TT):
            ysl = y[:, :, tt * TT:(tt + 1) * TT]
            s0 = self.ps[2 + 2 * (tt % 2)][:]
            s1 = self.ps[3 + 2 * (tt % 2)][:]
            sch.op("act", lambda e, ysl=ysl: e.activation(out=yb, in_=ysl, func=AF.Copy), reads=[ysl], writes=[yb])
            sch.op("act", lambda e, ysl=ysl: e.activation(out=ysq, in_=ysl, func=AF.Square), reads=[ysl], writes=[ysq])
            for (src, st) in ((yb, s0), (ysq, s1)):
                for c in range(4):
                    sch.op("pe", lambda e, c=c, src=src, st=st: e.matmul(
                        st, lhsT=self.ones[:], rhs=src[:, c, :], start=(c == 0), stop=(c == 3)),
                        reads=[self.ones[:], src[:, c, :]], writes=[st], signal=(c == 3))
            sch.op("act", lambda e, s0=s0: e.activation(out=mu, in_=s0, func=AF.Copy, scale=1.0 / 512.0),
                   reads=[s0], writes=[mu])
            sch.op("dve", lambda e: e.tensor_tensor(out=var, in0=mu, in1=mu, op=ALU.mult), reads=[mu], writes=[var])
            sch.op("dve", lambda e, s1=s1: e.scalar_tensor_tensor(
                out=var, in0=s1, scalar=1.0 / 512.0, in1=var, op0=ALU.mult, op1=ALU.subtract),
                reads=[s1, var], writes=[var])
            sch.op("act", lambda e: e.activation(out=var, in_=var, func=AF.Sqrt, bias=float(EPS), scale=1.0),
                   reads=[var], writes=[var])
            sch.op("dve", lambda e: e.reciprocal(out=var, in_=var), reads=[var], writes=[var])
            for c in range(4):
                yc = y[:, c, tt * TT:(tt + 1) * TT]
                sch.op("dve", lambda e, yc=yc: e.tensor_tensor(out=yc, in0=yc, in1=mu, op=ALU.subtract),
                       reads=[yc, mu], writes=[yc])
                sch.op("dve", lambda e, yc=yc: e.tensor_tensor(out=yc, in0=yc, in1=var, op=ALU.mult),
                       reads=[yc, var], writes=[yc])
                dst = coT[:, c, tt * TT:(tt + 1) * TT]
                sch.op("act", lambda e, yc=yc, dst=dst, c=c: e.activation(
                    out=dst, in_=yc, func=AF.Silu, bias=lbv[:, c:c + 1], scale=lgv[:, c:c + 1]),
                    reads=[yc, lbv[:, c:c + 1], lgv[:, c:c + 1]], writes=[dst])
        if self.stop <= 4:
            return
        o = T0
        ets, pTs = [], []
        for _ in range(4):
            v, o = self.carve(o, [TT], F32)
            ets.append(v)
        for _ in range(4):
            v, o = self.carve(o, [TT], BF16)
            pTs.append(v)
        den, o = self.carve(o, [4, 128], F32)
        esk, o = self.carve(o, [4], F32)
        assert o <= ARENA
        sch.op("act", lambda e: e.activation(out=esk, in_=skv, func=AF.Exp), reads=[skv], writes=[esk])
        Mv = mswa.rearrange("p (r h b q) -> p r h b q", r=4, h=2, b=2)
        it = 0
        for n in range(min(S // 128, self.kn)):
            A = self.ps[4 + (n % 2) * 2][:]
            B = self.ps[5 + (n % 2) * 2][:]
            kbs = [1] if n == 0 else [0, 1]
            b0 = kbs[0]
            for g in range(2):
                par = it % 2
                it += 1
                Sb = [self.ps[2 * par + hs][:].rearrange("p (r b q) -> p r b q", r=2, b=2) for hs in range(2)]
                etv = [ets[2 * par + hs].rearrange("p (r b q) -> p r b q", r=2, b=2) for hs in range(2)]
                pTv = [pTs[2 * par + hs].rearrange("p (r b q) -> p r b q", r=2, b=2) for hs in range(2)]
                for prl in range(2):
                    pr = 2 * g + prl
                    for bs in kbs:
                        kb = n - 1 + bs
                        for hs in range(2):
                            r0 = hs * 64
                            lh = kdT[r0:r0 + 64, g, kb * 128:(kb + 1) * 128]
                            rh = qT[r0:r0 + 64, pr, n * 128:(n + 1) * 128]
                            po = Sb[hs][:, prl, bs, :]
                            sch.op("pe", lambda e, lh=lh, rh=rh, po=po: e.matmul(po, lhsT=lh, rhs=rh, start=True, stop=True),
                                   reads=[lh, rh], writes=[po], signal=(prl == 1 and bs == kbs[-1]))
                if self.kswa <= 1:
                    continue
                for hs in range(2):
                    si, eo, po_ = Sb[hs][:, :, b0:, :], etv[hs][:, :, b0:, :], pTv[hs][:, :, b0:, :]
                    mi = Mv[:, 2 * g:2 * g + 2, hs, b0:, :]
                    sch.op("act", lambda e, si=si, eo=eo: e.activation(out=eo, in_=si, func=AF.Exp, scale=0.125),
                           reads=[si], writes=[eo])
                    if self.kswa <= 2:
                        continue
                    sch.op("dve", lambda e, eo=eo, po_=po_, mi=mi: e.tensor_tensor(out=po_, in0=eo, in1=mi, op=ALU.mult),
                           reads=[eo, mi], writes=[po_])
                if self.kswa <= 3:
                    continue
                for prl in range(2):
                    pr = 2 * g + prl
                    for hs in range(2):
                        r0 = hs * 64
                        for bs in kbs:
                            kb = n - 1 + bs
                            rh = pTv[hs][:, prl, bs, :]
                            lv = vT[:, kb, g * 64:(g + 1) * 64]
                            pa = A[r0:r0 + 64, pr * 128:(pr + 1) * 128]
                            pb = B[r0:r0 + 64, pr * 128:(pr + 1) * 128]
                            last = (prl == 1 and hs == 1 and bs == kbs[-1])
                            sch.op("pe", lambda e, lv=lv, rh=rh, pa=pa, bs=bs: e.matmul(
                                pa, lhsT=lv, rhs=rh, start=(bs == kbs[0]), stop=(bs == kbs[-1])),
                                reads=[lv, rh], writes=[pa], signal=False)
                            sch.op("pe", lambda e, rh=rh, pb=pb, bs=bs: e.matmul(
                                pb, lhsT=self.ones[:, 0:64], rhs=rh, start=(bs == kbs[0]), stop=(bs == kbs[-1])),
                                reads=[self.ones[:, 0:64], rh], writes=[pb], signal=last)
            if self.kswa <= 4:
                continue
            Bv = B.rearrange("p (r q) -> p r q", r=4)
            Av = A.rearrange("p (r q) -> p r q", r=4)
            ebc = esk.unsqueeze(2).to_broadcast([128, 4, 128])
            sch.op("dve", lambda e, Bv=Bv, ebc=ebc: e.tensor_tensor(out=den, in0=Bv, in1=ebc, op=ALU.add),
                   reads=[B, esk], writes=[den])
            sch.op("dve", lambda e: e.reciprocal(out=den, in_=den), reads=[den], writes=[den])
            dst = boT[:, :, n * 128:(n + 1) * 128]
            sch.op("dve", lambda e, Av=Av, dst=dst: e.tensor_tensor(out=dst, in0=Av, in1=den, op=ALU.mult),
                   reads=[A, den], writes=[dst])
        if self.stop <= 5:
            return
        o = T0
        sqh = []
        for _ in range(2):
            v, o = self.carve(o, [TT], BF16)
            sqh.append(v)
        rs2, o = self.carve(o, [TT], F32)
        self.proj_out_post(l, lambda k, tt: (coT if k < 4 else boT)[:, k % 4, tt * TT:(tt + 1) * TT],
                           hout, sqh, rs2)

    def plan_odd(self, l):
        w = self.win[l].rearrange("(k p) c -> p k c", p=128)
        V = self.wp.view
        self.wp.add([(lambda s_: V(s_, 0, [KC, 16]), w[:, :, 3072:3088])])
        for pr in range(8):
            self.wp.add([
                (lambda s_: V(s_, 0, [KC, 128]), w[:, :, pr * 128:(pr + 1) * 128]),
                (lambda s_: V(s_, 1024, [KC, 128]), w[:, :, 1024 + pr * 128:1024 + (pr + 1) * 128]),
                (lambda s_: V(s_, 2048, [KC, 128]), w[:, :, 2048 + pr * 128:2048 + (pr + 1) * 128])])
        self.plan_proj_out(self.wout[l])

    def mixer_odd(self, l):
        sch, wp = self.sch, self.wp
        i = l // 2
        bfv = self.cst[0:16, DEPTH * 6 * KC + 280 + i:DEPTH * 6 * KC + 280 + i + 1]
        K32 = 32 * 1024
        hT = self.carve(0, [KC, S], BF16)[0]
        hout = self.carve(0, [KC, TT], F32)[0]
        OT, off = self.carve(K32, [KC, S], BF16)
        P0 = off
        qTp, off = self.carve(off, [S], BF16)
        kTp, off = self.carve(off, [S], BF16)
        vaug, off = self.carve(off, [16, 192], BF16)
        Cbs = []
        for _ in range(4):
            v, off = self.carve(off, [TT], BF16)
            Cbs.append(v)
        negc, off = self.carve(off, [16, 16], F32)
        cTb, off = self.carve(off, [S], BF16)
        tmps, pTs = [], []
        for _ in range(3):
            v, off = self.carve(off, [TT], F32)
            tmps.append(v)
        for _ in range(5):
            v, off = self.carve(off, [TT], BF16)
            pTs.append(v)
        rec, off = self.carve(off, [TT], F32)
        sel, off = self.carve(off, [2048], BF16)
        assert off <= ARENA, off
        toks = [None, 0]
        sch.dma("pool", self.ds_c2, sel[0:16, :], self.sel_d, toks, sb_writes=[sel[0:16, :]])
        sqb, o = self.carve(P0, [KC, TT], BF16)
        rs, o = self.carve(o, [TT], F32)
        lf = self.carve(P0, [S], F32)[0]
        cpT = self.carve(tmps[0].offset * 0 + (off - 0), [1], F32)[0] if False else None
        cpT = self.carve(K32, [S], F32)[0]
        nbf, _ = self.carve(K32 + S * 4, [1], F32)
        self.prenorm_full(l, 2, hT, sqb, rs)
        tF = wp.get()
        wf = wp.view(tF[1], 0, [KC, 16])
        sch.op("dve", lambda e: e.tensor_scalar(out=nbf[0:16, :], in0=bfv, scalar1=-1.0, scalar2=None, op0=ALU.mult),
               reads=[bfv], writes=[nbf[0:16, :]])
        for tt in range(S // TT):
            pst = self.ps[self.pp % 2][:]
            self.pp += 1
            for k in range(KC):
                rhs = hT[:, k, tt * TT:(tt + 1) * TT]
                sch.op("pe", lambda e, k=k, rhs=rhs, pst=pst: e.matmul(
                    pst[0:16, :], lhsT=wf[:, k, :], rhs=rhs, start=(k == 0), stop=(k == KC - 1)),
                    reads=[wf[:, k, :], rhs], writes=[pst[0:16, :]], signal=(k == KC - 1))
            dst = lf[0:16, tt * TT:(tt + 1) * TT]
            sch.op("act", lambda e, pst=pst, dst=dst: e.activation(
                out=dst, in_=pst[0:16, :], func=AF.Exp, bias=nbf[0:16, :], scale=-1.0),
                reads=[pst[0:16, :], nbf[0:16, :]], writes=[dst])
        wp.release(tF[0])
        sch.op("act", lambda e: e.activation(out=lf[0:16, :], in_=lf[0:16, :], func=AF.Ln, bias=1.0, scale=1.0),
               reads=[lf[0:16, :]], writes=[lf[0:16, :]])
        sch.op("dve", lambda e: e.tensor_tensor_scan(out=cpT[0:16, :], data0=lf[0:16, :], data1=lf[0:16, :],
                                                     initial=0.0, op0=ALU.add, op1=ALU.bypass),
               reads=[lf[0:16, :]], writes=[cpT[0:16, :]])
        pst = self.ps[7][:]
        for kb in range(16):
            po = pst[:, kb * 16:(kb + 1) * 16]
            src = cpT[0:16, kb * 128:(kb + 1) * 128]
            sch.op("pe", lambda e, po=po, src=src: e.transpose(po, src, self.ident[:]),
                   reads=[src, self.ident[:]], writes=[po], signal=(kb == 15))
        sch.op("act", lambda e: e.activation(out=negc, in_=pst[:, 0:256].rearrange("p (a b) -> p a b", a=16), func=AF.Copy),
               reads=[pst[:, 0:256]], writes=[negc])
        sch.op("dve", lambda e: e.tensor_scalar(out=cTb[0:16, :], in0=cpT[0:16, :], scalar1=-1.0, scalar2=None, op0=ALU.mult),
               reads=[cpT[0:16, :]], writes=[cTb[0:16, :]])
        sch.op("dve", lambda e: e.memset(vaug[:, :, 64:128], 1.0), writes=[vaug[:, :, 64:128]])
        selv = sel.rearrange("p (h c) -> p h c", h=16)
        it = 0
        cbi = 0
        for pr in range(8):
            tp = wp.get()
            wq = wp.view(tp[1], 0, [KC, 128])
            wk = wp.view(tp[1], 1024, [KC, 128])
            wv_ = wp.view(tp[1], 2048, [KC, 128])
            self.fm_chunk(wq, hT, lambda tt, p: sch.op(
                "act", lambda e: e.activation(out=qTp[:, tt * TT:(tt + 1) * TT], in_=p, func=AF.Copy),
                reads=[p], writes=[qTp[:, tt * TT:(tt + 1) * TT]]))
            self.fm_chunk(wk, hT, lambda tt, p: sch.op(
                "dve", lambda e: e.tensor_copy(out=kTp[:, tt * TT:(tt + 1) * TT], in_=p),
                reads=[p], writes=[kTp[:, tt * TT:(tt + 1) * TT]]))
            for tb4 in range(4):
                pst = self.ps[self.pp % 2][:]
                self.pp += 1
                for jb in range(4):
                    tb = tb4 * 4 + jb
                    po = pst[:, jb * 128:(jb + 1) * 128]
                    for k in range(KC):
                        lh = hT[:, k, tb * 128:(tb + 1) * 128]
                        sch.op("pe", lambda e, k=k, lh=lh, po=po: e.matmul(
                            po, lhsT=lh, rhs=wv_[:, k, :], start=(k == 0), stop=(k == KC - 1)),
                            reads=[lh, wv_[:, k, :]], writes=[po], signal=(k == KC - 1))
                srcv = pst.rearrange("p (a b) -> p a b", a=4)
                d0 = vaug[:, tb4 * 4:(tb4 + 1) * 4, 0:64]
                d1 = vaug[:, tb4 * 4:(tb4 + 1) * 4, 128:192]
                sch.op("act", lambda e, d0=d0, srcv=srcv: e.activation(out=d0, in_=srcv[:, :, 0:64], func=AF.Copy),
                       reads=[pst], writes=[d0])
                sch.op("dve", lambda e, d1=d1, srcv=srcv: e.tensor_copy(out=d1, in_=srcv[:, :, 64:128]),
                       reads=[pst], writes=[d1])
            wp.release(tp[0])
            chunks = [(hs, qc) for hs in range(2) for qc in range(4)]
            tiles = []
            for ci, (hs, qc) in enumerate(chunks):
                nkb = 4 * qc + 4
                for kb in range(nkb):
                    tiles.append((ci, hs, qc, kb, nkb))
            cbuf = {}

            def emit_cb(ci):
                nonlocal cbi
                hs, qc = chunks[ci]
                h = 2 * pr + hs
                buf = Cbs[cbi % 4]
                cbi += 1
                cbuf[ci] = buf
                pst = self.ps[7][:]
                rhs = cTb[0:16, qc * TT:(qc + 1) * TT]
                sch.op("pe", lambda e: e.matmul(pst, lhsT=selv[0:16, h, :], rhs=rhs, start=True, stop=True),
                       reads=[selv[0:16, h, :], rhs], writes=[pst])
                sch.op("act", lambda e: e.activation(out=buf, in_=pst, func=AF.Copy), reads=[pst], writes=[buf])

            emit_cb(0)
            emit_cb(1)
            LA = 4
            slots = {}
            for i in range(len(tiles) + LA):
                if i < len(tiles):
                    ci, hs, qc, kb, nkb = tiles[i]
                    h = 2 * pr + hs
                    r0 = hs * 64
                    if kb == 0 and ci + 2 < len(chunks):
                        emit_cb(ci + 2)
                    j = kb - 4 * qc
                    q0 = max(0, j) * 128
                    Sb = self.ps[2 + it % 3][:]
                    tmp = tmps[it % 3]
                    pT = pTs[it % 5]
                    it += 1
                    slots[i] = (pT, q0)
                    lh = kTp[r0:r0 + 64, kb * 128:(kb + 1) * 128]
                    rh = qTp[r0:r0 + 64, qc * TT + q0:(qc + 1) * TT]
                    sch.op("pe", lambda e, lh=lh, rh=rh, Sb=Sb, q0=q0: e.matmul(
                        Sb[:, q0:TT], lhsT=lh, rhs=rh, start=True, stop=True),
                        reads=[lh, rh], writes=[Sb[:, q0:TT]])
                    cbs = cbuf[ci][:, q0:TT]
                    sch.op("dve", lambda e, Sb=Sb, tmp=tmp, cbs=cbs, q0=q0: e.scalar_tensor_tensor(
                        out=tmp[:, q0:TT], in0=Sb[:, q0:TT], scalar=0.125, in1=cbs, op0=ALU.mult, op1=ALU.add),
                        reads=[Sb[:, q0:TT], cbs], writes=[tmp[:, q0:TT]])
                    if j >= 0:
                        dg = tmp[:, q0:q0 + 128]
                        sch.op("dve", lambda e, dg=dg: e.tensor_tensor(out=dg, in0=dg, in1=self.mc[:], op=ALU.add),
                               reads=[dg, self.mc[:]], writes=[dg])
                    nb = negc[:, kb, h:h + 1]
                    sch.op("act", lambda e, tmp=tmp, pT=pT, nb=nb, q0=q0: e.activation(
                        out=pT[:, q0:TT], in_=tmp[:, q0:TT], func=AF.Exp, bias=nb, scale=1.0),
                        reads=[tmp[:, q0:TT], nb], writes=[pT[:, q0:TT]])
                if i - LA >= 0:
                    ci, hs, qc, kb, nkb = tiles[i - LA]
                    pT, q0 = slots.pop(i - LA)
                    O = self.ps[5 + ci % 2][:]
                    lv = vaug[:, kb, hs * 64:hs * 64 + 128]
                    sch.op("pe", lambda e, lv=lv, pT=pT, O=O, q0=q0, kb=kb, nkb=nkb: e.matmul(
                        O[:, q0:TT], lhsT=lv, rhs=pT[:, q0:TT], start=(kb == 0), stop=(kb == nkb - 1)),
                        reads=[lv, pT[:, q0:TT]], writes=[O[:, q0:TT]], signal=(kb == nkb - 1))
                    if kb == nkb - 1:
                        nr = hs * 64
                        dr = 64 - nr
                        sch.op("dve", lambda e, O=O, nr=nr, dr=dr: e.reciprocal(out=rec[nr:nr + 64, :], in_=O[dr:dr + 64, :]),
                               reads=[O[dr:dr + 64, :]], writes=[rec[nr:nr + 64, :]])
                        dst = OT[nr:nr + 64, pr, qc * TT:(qc + 1) * TT]
                        sch.op("dve", lambda e, O=O, nr=nr, dst=dst: e.tensor_tensor(
                            out=dst, in0=O[nr:nr + 64, :], in1=rec[nr:nr + 64, :], op=ALU.mult),
                            reads=[O[nr:nr + 64, :], rec[nr:nr + 64, :]], writes=[dst])
        sqh = []
        o = P0
        for _ in range(2):
            v, o = self.carve(o, [TT], BF16)
            sqh.append(v)
        rs2, o = self.carve(o, [TT], F32)
        self.proj_out_post(l, lambda k, tt: OT[:, k, tt * TT:(tt + 1) * TT], hout, sqh, rs2)


_CACHE = {}


def _get_prog(layers, subs):
    key = (tuple(layers), tuple(subs))
    if key not in _CACHE:
        p = Prog(layers, subs)
        p.build()
        _CACHE[key] = p
    return _CACHE[key]


def _consts(inputs):
    ng = np.asarray(inputs["norm_g"], dtype=np.float32)
    cst = np.zeros((128, NCST), np.float32)
    cst[:, 0:DEPTH * 6 * KC] = ng.reshape(DEPTH, 6, KC, 128).transpose(3, 0, 1, 2).reshape(128, -1)
    base = DEPTH * 6 * KC
    for i in range(2):
        b0 = base + i * 140
        cw = np.asarray(inputs["conv_w"][i], np.float32)
        cst[:, b0:b0 + 124] = cw.reshape(CONV_K, 4, 128).transpose(2, 1, 0).reshape(128, 124)
        cst[:, b0 + 124:b0 + 128] = np.asarray(inputs["conv_b"][i], np.float32).reshape(4, 128).T
        cst[:, b0 + 128:b0 + 132] = np.asarray(inputs["conv_ln_g"][i], np.float32).reshape(4, 128).T
        cst[:, b0 + 132:b0 + 136] = np.asarray(inputs["conv_ln_b"][i], np.float32).reshape(4, 128).T
        sk = np.asarray(inputs["swa_sinks"][i], np.float32)
        cst[0:64, b0 + 136:b0 + 140] = sk[0::2][None, :]
        cst[64:128, b0 + 136:b0 + 140] = sk[1::2][None, :]
        cst[0:16, base + 280 + i] = np.asarray(inputs["fox_b_f"][i], np.float32)
    k = np.arange(128)[:, None].astype(np.float64)
    q = np.arange(128)[None, :].astype(np.float64)
    mswa = np.zeros((128, 8, 2, 128), np.float64)
    for h in range(8):
        slope = 2.0 ** (-(h + 1))
        d0 = 128 + q - k
        mswa[:, h, 0, :] = np.exp(-slope * d0) * (q < k)
        d1 = q - k
        mswa[:, h, 1, :] = np.exp(-slope * d1) * (q >= k)
    mc = np.where(q >= k, 0.0, NEGBIG).astype(np.float32)
    sel = np.zeros((16, 16, 128), np.float32)
    for h in range(16):
        sel[h, h, :] = 1.0
    return {"cst": cst, "mswa": mswa.reshape(128, 2048).astype(np.float32), "mc": mc,
            "sel": sel.reshape(16, 2048), "ident": np.eye(16, dtype=np.float32),
            "identb": np.eye(128, dtype=np.float32)}


def _run(inputs, layers, subs=("f1", "mix", "f2"), x_override=None):
    x = np.asarray(inputs["x"], dtype=np.float32) if x_override is None else x_override
    shared = _consts(inputs)
    f32c = lambda a: np.ascontiguousarray(a, dtype=np.float32)
    for l in layers:
        for j, sub in ((0, "f1"), (1, "f2")):
            if sub in subs:
                shared[f"wg{l}{j}"] = f32c(inputs["ffn_w_gate"][l, j])
                shared[f"wu{l}{j}"] = f32c(inputs["ffn_w_up"][l, j])
                shared[f"wd{l}{j}"] = f32c(inputs["ffn_w_down"][l, j])
        if "mix" in subs:
            if l % 2 == 0:
                shared[f"win{l}"] = f32c(inputs["ab_w_in"][l // 2])
                shared[f"wout{l}"] = f32c(inputs["ab_w_out"][l // 2])
            else:
                shared[f"win{l}"] = f32c(inputs["fox_w_in"][l // 2])
                shared[f"wout{l}"] = f32c(inputs["fox_w_out"][l // 2])
    p = _get_prog(layers, subs)
    in_maps = []
    for b in range(NCORES):
        m = dict(shared)
        m["xT"] = np.ascontiguousarray(x[b].T)
        in_maps.append(m)
    import time as _t
    _t0 = _t.time()
    import os as _os
    if _os.environ.get("KTRACE"):
        res = run_bass_kernel_spmd(p.nc, in_maps, core_ids=list(range(NCORES)), trace=True)
        print("[kernel] exec_time_ns", res.exec_time_ns, flush=True)
    else:
        res = run_bass_kernel_spmd(p.nc, in_maps, core_ids=list(range(NCORES)))
    print("[kernel] launch wall s", round(_t.time() - _t0, 1), flush=True)
    out = np.stack([np.ascontiguousarray(r["yT"].T) for r in res.results], axis=0)
    return out.astype(np.float32)


LAUNCH_GROUPS = [[0, 1, 2, 3]]


def kernel(**inputs):
    x = None
    for grp in LAUNCH_GROUPS:
        x = _run(inputs, grp, x_override=x)
    return x
```

```python
import contextlib
import numpy as np
import concourse.bass as bass
import concourse.mybir as mybir
from concourse.bass_utils import run_bass_kernel_spmd

F32 = mybir.dt.float32
BF16 = mybir.dt.bfloat16
AF = mybir.ActivationFunctionType
ALU = mybir.AluOpType
ESZ = {F32: 4, BF16: 2}

D = 1024
S = 2048
DFF = 2816
KC = D // 128
FCH = DFF // 128
DEPTH = 4
EPS = 1e-6
NCORES = 8
TT = 512
TG = 1024
SLOT = 6144
NSLOT = 3
ARENA = 104 * 1024 - 512
NCST = 474
HD = 64
CONV_K = 31
NEGBIG = -30000.0


def _esz(dt):
    return ESZ[dt]


class Sched:
    def __init__(self, nc, es):
        self.nc = nc
        self.es = es
        self.eng = {}
        for name, obj in (("pe", nc.tensor), ("act", nc.scalar), ("dve", nc.vector),
                          ("pool", nc.gpsimd), ("sp", nc.sync)):
            sem = es.enter_context(nc.semaphore("sem_" + name))
            self.eng[name] = dict(obj=obj, sem=sem, cnt=0, waited={})
        self.recs = {}
        self.nwaits = 0
        self.nops = 0

    @staticmethod
    def region(ap):
        name = ap.tensor.name
        a = ap.ap
        esz = _esz(ap.dtype)
        pstep, pcnt = a[0]
        off = ap.offset
        if pstep > 0:
            p0 = off // pstep
            lo = off % pstep
        else:
            p0 = 0
            lo = off
        hi = lo + 1
        for st, c in a[1:]:
            hi += (c - 1) * abs(st)
        return (name, p0, p0 + pcnt, lo * esz, hi * esz)

    @staticmethod
    def _ov(r, q):
        return r[1] < q[2] and q[1] < r[2] and r[3] < q[4] and q[3] < r[4]

    @staticmethod
    def _contains(outer, inner):
        return (outer[1] <= inner[1] and inner[2] <= outer[2]
                and outer[3] <= inner[3] and inner[4] <= outer[4])

    def _collect(self, reads, writes):
        deps = {}

        def add(tok):
            h, v = tok[0], tok[1]
            k = h.name
            if k not in deps or deps[k][1] < v:
                deps[k] = (h, v)

        rregs = [self.region(a) for a in reads]
        wregs = [self.region(a) for a in writes]
        for r in rregs:
            rec = self.recs.get(r[0])
            if rec:
                for q, tok in rec["w"]:
                    if self._ov(r, q):
                        add(tok)
        for r in wregs:
            rec = self.recs.get(r[0])
            if rec:
                for q, tok in rec["w"]:
                    if self._ov(r, q):
                        add(tok)
                for q, tok in rec["r"]:
                    if self._ov(r, q):
                        add(tok)
        return deps, rregs, wregs

    def _emit_waits(self, E, deps, skip_self):
        need = []
        for k, (h, v) in deps.items():
            if skip_self and h is E["sem"]:
                continue
            if E["waited"].get(k, 0) >= v:
                continue
            need.append((k, h, v))
        for (k, h, v) in need[:-1]:
            E["obj"].wait_ge(h, v)
            E["waited"][k] = v
            self.nwaits += 1
        if need:
            k, h, v = need[-1]
            E["waited"][k] = v
            return (h, v)
        return None

    def _record(self, rregs, wregs, tok):
        for r in rregs:
            rec = self.recs.setdefault(r[0], {"w": [], "r": []})
            found = False
            for i, (q, t) in enumerate(rec["r"]):
                if q == r and t[0] is tok[0]:
                    if t[1] < tok[1]:
                        rec["r"][i] = (q, tok)
                    found = True
                    break
            if not found:
                rec["r"].append((r, tok))
        for r in wregs:
            rec = self.recs.setdefault(r[0], {"w": [], "r": []})
            rec["w"] = [(q, t) for (q, t) in rec["w"] if not self._contains(r, q)]
            rec["r"] = [(q, t) for (q, t) in rec["r"] if not self._contains(r, q)]
            rec["w"].append((r, tok))

    def op(self, eng, fn, reads=(), writes=(), signal=True, extra=()):
        E = self.eng[eng]
        deps, rregs, wregs = self._collect(reads, writes)
        for tok in extra:
            k = tok[0].name
            if k not in deps or deps[k][1] < tok[1]:
                deps[k] = (tok[0], tok[1])
        w = self._emit_waits(E, deps, skip_self=(eng == "pe"))
        ins = fn(E["obj"])
        if w is not None:
            ins._wait_ge(w[0], w[1])
        self.nops += 1
        if signal:
            E["cnt"] += 1
            ins.then_inc(E["sem"], 1)
            tok = [E["sem"], E["cnt"]]
        else:
            tok = [E["sem"], E["cnt"] + 1]
        self._record(rregs, wregs, tok)
        return tok

    def new_dsem(self, name):
        sem = self.es.enter_context(self.nc.semaphore(name))
        return dict(sem=sem, cnt=0)

    def dma(self, queue, ds, out, in_, tok, sb_reads=(), sb_writes=(), extra=()):
        E = self.eng[queue]
        deps, rregs, wregs = self._collect(sb_reads, sb_writes)
        for t in extra:
            k = t[0].name
            if k not in deps or deps[k][1] < t[1]:
                deps[k] = (t[0], t[1])
        w = self._emit_waits(E, deps, skip_self=False)
        ins = E["obj"].dma_start(out=out, in_=in_)
        if w is not None:
            ins._wait_ge(w[0], w[1])
        ins.then_inc(ds["sem"], 16)
        ds["cnt"] += 16
        tok[0] = ds["sem"]
        tok[1] = ds["cnt"]
        self._record(rregs, wregs, tok)
        self.nops += 1


class WeightPool:
    def __init__(self, sch, ring, nslot, slot_elems):
        self.sch = sch
        self.ring = ring
        self.nslot = nslot
        self.slot = slot_elems
        self.plan = []
        self.next_load = 0
        self.next_get = 0
        self.free = list(range(nslot))
        self.slot_of = {}
        self.dsems = [sch.new_dsem(f"wsem{i}") for i in range(nslot)]

    def add(self, tile):
        self.plan.append(tile)

    def view(self, slot, off, shape):
        n = 1
        for s in shape:
            n *= s
        base = slot * self.slot + off
        v = self.ring[:, base:base + n]
        if len(shape) == 2:
            v = v.rearrange("p (a b) -> p a b", a=shape[0])
        elif len(shape) == 3:
            v = v.rearrange("p (a b c) -> p a b c", a=shape[0], b=shape[1])
        return v

    def prefetch(self):
        while self.free and self.next_load < len(self.plan):
            slot = self.free.pop(0)
            idx = self.next_load
            self.next_load += 1
            self.slot_of[idx] = slot
            tok = [None, 0]
            for (dst_fn, src) in self.plan[idx]:
                dst = dst_fn(slot)
                self.sch.dma("pool", self.dsems[slot], dst, src, tok, sb_writes=[dst])

    def get(self):
        self.prefetch()
        idx = self.next_get
        self.next_get += 1
        if idx not in self.slot_of:
            self.prefetch()
        assert idx in self.slot_of, "weight pool starved (release missing?)"
        return idx, self.slot_of[idx]

    def release(self, idx):
        self.free.append(self.slot_of[idx])
        self.prefetch()


class Prog:
    def __init__(self, layers, subs=("f1", "mix", "f2")):
        self.layers = list(layers)
        self.subs = subs
        self.nc = bass.Bass("TRN2", target_bir_lowering=False)
        self.es = contextlib.ExitStack()
        self.pp = 0
        import os
        self.stop = int(os.environ.get('KSTOP', '99'))
        self.kswa = int(os.environ.get('KSWA', '99'))
        self.kn = int(os.environ.get('KN', '99'))

    def dram_in(self, name, shape):
        return self.nc.dram_tensor(name, list(shape), F32, kind="ExternalInput").ap()

    def build(self):
        nc = self.nc
        with self.es as es:
            self.sch = Sched(nc, es)
            sch = self.sch
            self.xT = self.dram_in("xT", [D, S])
            self.cst_d = self.dram_in("cst", [128, NCST])
            self.mswa_d = self.dram_in("mswa", [128, 2048])
            self.mc_d = self.dram_in("mc", [128, 128])
            self.sel_d = self.dram_in("sel", [16, 2048])
            self.ident_d = self.dram_in("ident", [16, 16])
            self.identb_d = self.dram_in("identb", [128, 128])
            self.win, self.wout = {}, {}
            if "mix" in self.subs:
                for l in self.layers:
                    self.win[l] = self.dram_in(f"win{l}", [D, 1792 if l % 2 == 0 else 3088])
                    self.wout[l] = self.dram_in(f"wout{l}", [D, D])
            self.wg, self.wu, self.wd = {}, {}, {}
            for l in self.layers:
                for j, sub in ((0, "f1"), (1, "f2")):
                    if sub in self.subs:
                        self.wg[l, j] = self.dram_in(f"wg{l}{j}", [D, DFF])
                        self.wu[l, j] = self.dram_in(f"wu{l}{j}", [D, DFF])
                        self.wd[l, j] = self.dram_in(f"wd{l}{j}", [DFF, D])
            self.yT = nc.dram_tensor("yT", [D, S], F32, kind="ExternalOutput").ap()
            sb = lambda n, shp, dt: es.enter_context(nc.sbuf_tensor(n, shp, dt))
            self.xs = sb("xs", [128, KC, S], F32)
            self.ring = sb("ring", [128, NSLOT * SLOT], BF16)
            self.cst = sb("cst_sb", [128, NCST], F32)
            self.gT = self.cst[:, 0:DEPTH * 6 * KC]
            self.identb = sb("identb_sb", [128, 128], BF16)
            self.mc = sb("mc_sb", [128, 128], F32)
            self.ident = sb("ident_sb", [16, 16], F32)
            self.mcb = sb("mcb_sb", [128, 128], BF16)
            self.g32 = sb("g32", [128, DEPTH * 6 * KC], F32)
            self.ones = sb("ones", [128, 128], BF16)
            self.arena = sb("arena", [128, ARENA], mybir.dt.uint8)
            self.ps = [es.enter_context(nc.psum_tensor(f"ps{i}", [128, TT], F32)) for i in range(8)]
            self.wp = WeightPool(sch, self.ring, NSLOT, SLOT)
            self.ds_x = sch.new_dsem("ds_x")
            self.ds_c = sch.new_dsem("ds_c")
            self.ds_c2 = sch.new_dsem("ds_c2")
            self.ds_o = sch.new_dsem("ds_o")

            for l in self.layers:
                for sub in self.subs:
                    if sub == "f1":
                        self.plan_ffn(l, 0)
                    elif sub == "f2":
                        self.plan_ffn(l, 1)
                    elif sub == "mix":
                        (self.plan_even if l % 2 == 0 else self.plan_odd)(l)

            tok = [None, 0]
            for k in range(KC):
                sch.dma("sp", self.ds_x, self.xs[:, k, :], self.xT[k * 128:(k + 1) * 128, :], tok,
                        sb_writes=[self.xs[:, k, :]])
            tok = [None, 0]
            sch.dma("sp", self.ds_c, self.cst[:], self.cst_d, tok, sb_writes=[self.cst[:]])
            sch.dma("sp", self.ds_c, self.mc[:], self.mc_d, tok, sb_writes=[self.mc[:]])
            sch.dma("sp", self.ds_c, self.ident[:], self.ident_d, tok, sb_writes=[self.ident[:]])
            tok = [None, 0]
            sch.dma("pool", self.ds_c2, self.identb[:], self.identb_d, tok, sb_writes=[self.identb[:]])
            sch.op("dve", lambda e: e.memset(self.ones[:], 1.0), writes=[self.ones[:]])
            sch.op("dve", lambda e: e.tensor_scalar(out=self.mcb[:], in0=self.mc[:], scalar1=8.0, scalar2=None,
                                                    op0=ALU.mult), reads=[self.mc[:]], writes=[self.mcb[:]])
            self.wp.prefetch()
            gv = self.gT.rearrange("p (l n k) -> p l n k", l=DEPTH, n=6)
            g32v = self.g32[:].rearrange("p (l n k) -> p l n k", l=DEPTH, n=6)
            for n in range(6):
                f = 16.0 if n in (1, 5) else 32.0
                sch.op("dve", lambda e, n=n, f=f: e.tensor_scalar(
                    out=g32v[:, :, n, :], in0=gv[:, :, n, :], scalar1=f, scalar2=None, op0=ALU.mult),
                    reads=[gv[:, :, n, :]], writes=[g32v[:, :, n, :]])

            for l in self.layers:
                for sub in self.subs:
                    if sub == "f1":
                        self.ffn(l, 0)
                    elif sub == "f2":
                        self.ffn(l, 1)
                    elif sub == "mix":
                        (self.mixer_even if l % 2 == 0 else self.mixer_odd)(l)

            tok = [None, 0]
            for k in range(KC):
                sch.dma("sp", self.ds_o, self.yT[k * 128:(k + 1) * 128, :], self.xs[:, k, :], tok,
                        sb_reads=[self.xs[:, k, :]])
            nc.sync.wait_ge(self.ds_o["sem"], self.ds_o["cnt"])
        return nc

    def gcol(self, l, n, k):
        c = (l * 6 + n) * KC + k
        return self.g32[:, c:c + 1]

    def carve(self, off, shape, dt):
        n = 1
        for s in shape:
            n *= s
        nb = n * _esz(dt)
        v = self.arena[:, off:off + nb].bitcast(dt)
        if len(shape) == 2:
            v = v.rearrange("p (a b) -> p a b", a=shape[0])
        elif len(shape) == 3:
            v = v.rearrange("p (a b c) -> p a b c", a=shape[0], b=shape[1])
        return v, off + nb

    GU_STAGES = [(0, 3), (3, 6), (6, 9), (9, 12), (12, 15), (15, 18), (18, 21), (21, 22)]

    def plan_ffn(self, l, j):
        wg = self.wg[l, j].rearrange("(k p) c -> p k c", p=128)
        wu = self.wu[l, j].rearrange("(k p) c -> p k c", p=128)
        wd = self.wd[l, j].rearrange("(f p) c -> p f c", p=128)
        for g in range(S // TG):
            for (c0, c1) in self.GU_STAGES:
                n = (c1 - c0) * 128
                self.wp.add([(lambda s_, n=n: self.wp.view(s_, 0, [KC, n]), wg[:, :, c0 * 128:c1 * 128]),
                             (lambda s_, n=n: self.wp.view(s_, KC * n, [KC, n]), wu[:, :, c0 * 128:c1 * 128])])
            for tt in range(TG // TT):
                for dp in range(4):
                    self.wp.add([(lambda s_: self.wp.view(s_, 0, [FCH, 256]), wd[:, :, dp * 256:(dp + 1) * 256])])

    def rstd_from_stats(self, st_ps, rs):
        self.sch.op("act", lambda e: e.activation(out=rs, in_=st_ps, func=AF.Sqrt, bias=float(D * EPS), scale=1.0),
                    reads=[st_ps], writes=[rs])
        self.sch.op("dve", lambda e: e.reciprocal(out=rs, in_=rs), reads=[rs], writes=[rs])

    def prenorm_tile(self, l, n, t0, hdst, sqb, rs, st_ps):
        sch = self.sch
        xin = self.xs[:, :, t0:t0 + TT]
        sch.op("act", lambda e: e.activation(out=sqb, in_=xin, func=AF.Square), reads=[xin], writes=[sqb])
        for k in range(KC):
            sch.op("pe", lambda e, k=k: e.matmul(st_ps, lhsT=self.ones[:], rhs=sqb[:, k, :],
                                                 start=(k == 0), stop=(k == KC - 1)),
                   reads=[self.ones[:], sqb[:, k, :]], writes=[st_ps], signal=(k == KC - 1))
        self.rstd_from_stats(st_ps, rs)
        for k in range(KC):
            xi = self.xs[:, k, t0:t0 + TT]
            sch.op("dve", lambda e, k=k, xi=xi: e.scalar_tensor_tensor(
                out=hdst[:, k, :], in0=xi, scalar=self.gcol(l, n, k), in1=rs, op0=ALU.mult, op1=ALU.mult),
                reads=[xi, self.gcol(l, n, k), rs], writes=[hdst[:, k, :]])

    def postnorm_update(self, l, n, t0, hout, rs):
        sch = self.sch
        for k in range(KC):
            xi = self.xs[:, k, t0:t0 + TT]
            hk = hout[:, k, :]
            sch.op("dve", lambda e, k=k, hk=hk: e.scalar_tensor_tensor(
                out=hk, in0=hk, scalar=self.gcol(l, n, k), in1=rs, op0=ALU.mult, op1=ALU.mult),
                reads=[hk, self.gcol(l, n, k), rs], writes=[hk])
            sch.op("dve", lambda e, xi=xi, hk=hk: e.tensor_tensor(out=xi, in0=xi, in1=hk, op=ALU.add),
                   reads=[xi, hk], writes=[xi])

    def ffn(self, l, j):
        sch = self.sch
        wp = self.wp
        n_pre, n_post = (0, 1) if j == 0 else (4, 5)
        ntt = TG // TT
        ng = S // TG
        off = 0
        hTs, houts = [], []
        for _ in range(2):
            houts.append(self.arena[:, off:off + KC * TT * 4].bitcast(F32).rearrange("p (a b) -> p a b", a=KC))
            v, off = self.carve(off, [KC, TG], BF16)
            hTs.append(v)
        actT, off = self.carve(off, [FCH, TG], BF16)
        sqb, off = self.carve(off, [KC, TT], BF16)
        rs, off = self.carve(off, [TT], F32)
        rs2, off = self.carve(off, [TT], F32)
        sgt = []
        for i in range(2):
            v, off = self.carve(off, [TT], BF16)
            sgt.append(v)
        sqh = []
        for i in range(2):
            v, off = self.carve(off, [TT], BF16)
            sqh.append(v)
        assert off <= ARENA
        psG = [self.ps[0][:], self.ps[1][:]]
        psU = [self.ps[2][:], self.ps[3][:]]
        psD = [self.ps[4][:], self.ps[5][:]]
        stp = self.ps[6][:]
        stq = self.ps[7][:]
        it = 0

        def prenorm(g, tt):
            hT = hTs[g % 2]
            self.prenorm_tile(l, n_pre, g * TG + tt * TT, hT[:, :, tt * TT:(tt + 1) * TT], sqb, rs, stq)

        for tt in range(ntt):
            prenorm(0, tt)
        for g in range(ng):
            g0 = g * TG
            hT = hTs[g % 2]
            hout = houts[g % 2]
            for (c0, c1) in self.GU_STAGES:
                idx, slot = wp.get()
                n = (c1 - c0) * 128
                wgv = wp.view(slot, 0, [KC, n])
                wuv = wp.view(slot, KC * n, [KC, n])
                for c in range(c0, c1):
                    cl = (c - c0) * 128
                    for tt in range(ntt):
                        b = it % 2
                        it += 1
                        hsl = hT[:, :, tt * TT:(tt + 1) * TT]
                        for (wv, pst) in ((wgv, psG[b]), (wuv, psU[b])):
                            for k in range(KC):
                                sch.op("pe", lambda e, k=k, wv=wv, pst=pst: e.matmul(
                                    pst, lhsT=wv[:, k, cl:cl + 128], rhs=hsl[:, k, :],
                                    start=(k == 0), stop=(k == KC - 1)),
                                    reads=[wv[:, k, cl:cl + 128], hsl[:, k, :]], writes=[pst],
                                    signal=(k == KC - 1))
                        sch.op("act", lambda e, b=b: e.activation(out=sgt[b], in_=psG[b], func=AF.Silu),
                               reads=[psG[b]], writes=[sgt[b]])
                        dst = actT[:, c, tt * TT:(tt + 1) * TT]
                        sch.op("dve", lambda e, b=b, dst=dst: e.tensor_tensor(
                            out=dst, in0=psU[b], in1=sgt[b], op=ALU.mult),
                            reads=[psU[b], sgt[b]], writes=[dst])
                wp.release(idx)
            for tt in range(ntt):
                t0 = g0 + tt * TT
                asl = actT[:, :, tt * TT:(tt + 1) * TT]
                pend = None
                for dp in range(4):
                    idx, slot = wp.get()
                    wdv = wp.view(slot, 0, [FCH, 256])
                    for dd in range(2):
                        dc = dp * 2 + dd
                        b = dc % 2
                        for f in range(FCH):
                            sch.op("pe", lambda e, f=f, b=b, dd=dd: e.matmul(
                                psD[b], lhsT=wdv[:, f, dd * 128:(dd + 1) * 128], rhs=asl[:, f, :],
                                start=(f == 0), stop=(f == FCH - 1)),
                                reads=[wdv[:, f, dd * 128:(dd + 1) * 128], asl[:, f, :]], writes=[psD[b]],
                                signal=(f == FCH - 1))
                        if pend is not None:
                            pend()
                        hk = hout[:, dc, :]
                        sch.op("act", lambda e, b=b, hk=hk: e.activation(out=hk, in_=psD[b], func=AF.Copy),
                               reads=[psD[b]], writes=[hk])
                        sch.op("act", lambda e, b=b: e.activation(out=sqh[b], in_=psD[b], func=AF.Square),
                               reads=[psD[b]], writes=[sqh[b]])
                        pend = (lambda b=b, dc=dc: sch.op("pe", lambda e: e.matmul(
                            stp, lhsT=self.ones[:], rhs=sqh[b], start=(dc == 0), stop=(dc == KC - 1)),
                            reads=[self.ones[:], sqh[b]], writes=[stp], signal=True))
                    wp.release(idx)
                    if g + 1 < ng and tt == 0 and dp < ntt:
                        prenorm(g + 1, dp)
                pend()
                self.rstd_from_stats(stp, rs2)
                self.postnorm_update(l, n_post, t0, hout, rs2)

    def plan_proj_out(self, w):
        wv = w.rearrange("(k p) c -> p k c", p=128)
        for hh in range(2):
            self.wp.add([(lambda s_: self.wp.view(s_, 0, [KC, 512]), wv[:, :, hh * 512:(hh + 1) * 512])])

    def fm_chunk(self, wv, hT, evac):
        sch = self.sch
        for tt in range(S // TT):
            pst = self.ps[self.pp % 2][:]
            self.pp += 1
            for k in range(KC):
                rhs = hT[:, k, tt * TT:(tt + 1) * TT]
                sch.op("pe", lambda e, k=k, rhs=rhs, pst=pst: e.matmul(
                    pst, lhsT=wv[:, k, :], rhs=rhs, start=(k == 0), stop=(k == KC - 1)),
                    reads=[wv[:, k, :], rhs], writes=[pst], signal=(k == KC - 1))
            evac(tt, pst)

    def prenorm_full(self, l, n, hT, sqb, rs):
        for tt in range(S // TT):
            self.prenorm_tile(l, n, tt * TT, hT[:, :, tt * TT:(tt + 1) * TT], sqb, rs, self.ps[6 + tt % 2][:])

    def proj_out_post(self, l, rhs_fn, hout, sqh, rs2):
        sch, wp = self.sch, self.wp
        t1 = wp.get()
        t2 = wp.get()
        psD = [self.ps[0][:], self.ps[1][:]]
        stp = [self.ps[2][:], self.ps[3][:]]
        for tt in range(S // TT):
            st = stp[tt % 2]
            pend = None
            for dc in range(KC):
                tile = t1 if dc < 4 else t2
                wv = wp.view(tile[1], 0, [KC, 512])
                b = dc % 2
                c0 = (dc % 4) * 128
                for k in range(KC):
                    rhs = rhs_fn(k, tt)
                    sch.op("pe", lambda e, k=k, rhs=rhs, wv=wv, b=b, c0=c0: e.matmul(
                        psD[b], lhsT=wv[:, k, c0:c0 + 128], rhs=rhs, start=(k == 0), stop=(k == KC - 1)),
                        reads=[wv[:, k, c0:c0 + 128], rhs], writes=[psD[b]], signal=(k == KC - 1))
                if pend is not None:
                    pend()
                hk = hout[:, dc, :]
                sch.op("act", lambda e, b=b, hk=hk: e.activation(out=hk, in_=psD[b], func=AF.Copy),
                       reads=[psD[b]], writes=[hk])
                sch.op("act", lambda e, b=b: e.activation(out=sqh[b], in_=psD[b], func=AF.Square),
                       reads=[psD[b]], writes=[sqh[b]])
                pend = (lambda b=b, dc=dc, st=st: sch.op("pe", lambda e: e.matmul(
                    st, lhsT=self.ones[:], rhs=sqh[b], start=(dc == 0), stop=(dc == KC - 1)),
                    reads=[self.ones[:], sqh[b]], writes=[st], signal=True))
            pend()
            self.rstd_from_stats(st, rs2)
            self.postnorm_update(l, 3, tt * TT, hout, rs2)
        wp.release(t1[0])
        wp.release(t2[0])

    def plan_even(self, l):
        w = self.win[l].rearrange("(k p) c -> p k c", p=128)
        V = self.wp.view
        self.wp.add([(lambda s_: V(s_, 0, [KC, 768]), w[:, :, 0:768])])
        self.wp.add([(lambda s_: V(s_, 0, [KC, 768]), w[:, :, 768:1536])])
        t3 = []
        for g in range(2):
            for hh in range(2):
                t3.append((lambda s_, g=g, hh=hh: V(s_, g * 1024, [KC, 128])[:, :, hh * 64:(hh + 1) * 64],
                           w[:, :, 1536 + g * 64:1536 + (g + 1) * 64]))
        t3.append((lambda s_: V(s_, 2048, [KC, 128]), w[:, :, 1664:1792]))
        self.wp.add(t3)
        self.plan_proj_out(self.wout[l])

    def mixer_even(self, l):
        sch, wp = self.sch, self.wp
        i = l // 2
        cb0 = DEPTH * 6 * KC + i * 140
        cst = self.cst
        cwv = cst[:, cb0:cb0 + 124].rearrange("p (c j) -> p c j", c=4)
        cbv = cst[:, cb0 + 124:cb0 + 128]
        lgv = cst[:, cb0 + 128:cb0 + 132]
        lbv = cst[:, cb0 + 132:cb0 + 136]
        skv = cst[:, cb0 + 136:cb0 + 140]
        K32 = 32 * 1024
        hT = self.carve(0, [KC, S], BF16)[0]
        y = self.carve(0, [4, S], F32)[0]
        boT = self.carve(0, [4, S], BF16)[0]
        hout = self.carve(16 * 1024, [KC, TT], F32)[0]
        aT, off = self.carve(K32, [4, S + 30], BF16)
        coT = self.carve(K32, [4, S], BF16)[0]
        off = (off + 63) // 64 * 64
        qT, off = self.carve(off, [4, S], BF16)
        kdT, off = self.carve(off, [2, S], BF16)
        vT, off = self.carve(off, [16, 128], BF16)
        mswa, off = self.carve(off, [2048], BF16)
        T0 = off
        assert T0 + 16 * 1024 <= ARENA, T0
        tokm = [None, 0]
        sch.dma("pool", self.ds_c2, mswa, self.mswa_d, tokm, sb_writes=[mswa])
        sqb, o = self.carve(T0, [KC, TT], BF16)
        rs, o = self.carve(o, [TT], F32)
        self.prenorm_full(l, 2, hT, sqb, rs)
        if self.stop <= 1:
            return
        sgs = []
        o = T0
        for _ in range(2):
            v, o = self.carve(o, [S], BF16)
            sgs.append(v)
        t1 = wp.get()
        t2 = wp.get()

        def wview(col):
            tile = t1 if col < 768 else t2
            v = wp.view(tile[1], 0, [KC, 768])
            lc = col % 768
            return v[:, :, lc:lc + 128]

        for c in range(4):
            sch.op("dve", lambda e, c=c: e.memset(aT[:, c, 0:30], 0.0), writes=[aT[:, c, 0:30]])
        for c in range(4):
            sg = sgs[c % 2]
            self.fm_chunk(wview(512 + 128 * c), hT, lambda tt, p, sg=sg: sch.op(
                "act", lambda e: e.activation(out=sg[:, tt * TT:(tt + 1) * TT], in_=p, func=AF.Sigmoid),
                reads=[p], writes=[sg[:, tt * TT:(tt + 1) * TT]]))
            self.fm_chunk(wview(128 * c), hT, lambda tt, p, sg=sg, c=c: sch.op(
                "dve", lambda e: e.tensor_tensor(out=aT[:, c, 30 + tt * TT:30 + (tt + 1) * TT], in0=p,
                                                 in1=sg[:, tt * TT:(tt + 1) * TT], op=ALU.mult),
                reads=[p, sg[:, tt * TT:(tt + 1) * TT]], writes=[aT[:, c, 30 + tt * TT:30 + (tt + 1) * TT]]))
        wp.release(t1[0])
        for pr in range(4):
            self.fm_chunk(wview(1024 + 128 * pr), hT, lambda tt, p, pr=pr: sch.op(
                "act", lambda e: e.activation(out=qT[:, pr, tt * TT:(tt + 1) * TT], in_=p, func=AF.Copy),
                reads=[p], writes=[qT[:, pr, tt * TT:(tt + 1) * TT]]))
        wp.release(t2[0])
        t3 = wp.get()
        for g in range(2):
            wv = wp.view(t3[1], g * 1024, [KC, 128])
            self.fm_chunk(wv, hT, lambda tt, p, g=g: sch.op(
                "dve", lambda e: e.tensor_copy(out=kdT[:, g, tt * TT:(tt + 1) * TT], in_=p),
                reads=[p], writes=[kdT[:, g, tt * TT:(tt + 1) * TT]]))
        wvv = wp.view(t3[1], 2048, [KC, 128])
        for tb4 in range(4):
            pst = self.ps[self.pp % 2][:]
            self.pp += 1
            for jb in range(4):
                tb = tb4 * 4 + jb
                po = pst[:, jb * 128:(jb + 1) * 128]
                for k in range(KC):
                    lh = hT[:, k, tb * 128:(tb + 1) * 128]
                    sch.op("pe", lambda e, k=k, lh=lh, po=po: e.matmul(
                        po, lhsT=lh, rhs=wvv[:, k, :], start=(k == 0), stop=(k == KC - 1)),
                        reads=[lh, wvv[:, k, :]], writes=[po], signal=(k == KC - 1))
            dst = vT[:, tb4 * 4:(tb4 + 1) * 4, :]
            src = pst.rearrange("p (a b) -> p a b", a=4)
            sch.op("act", lambda e, dst=dst, src=src: e.activation(out=dst, in_=src, func=AF.Copy),
                   reads=[pst], writes=[dst])
        wp.release(t3[0])
        if self.stop <= 2:
            return
        Dgs = []
        o = T0
        for _ in range(2):
            v, o = self.carve(o, [CONV_K, 128], BF16)
            Dgs.append(v)
        for c in range(4):
            Dg = Dgs[c % 2]
            idb = self.identb[:].unsqueeze(1).to_broadcast([128, CONV_K, 128])
            cwb = cwv[:, c, :].unsqueeze(2).to_broadcast([128, CONV_K, 128])
            sch.op("dve", lambda e, Dg=Dg, idb=idb, cwb=cwb: e.tensor_tensor(out=Dg, in0=idb, in1=cwb, op=ALU.mult),
                   reads=[self.identb[:], cwv[:, c, :]], writes=[Dg])
            for tt in range(S // TT):
                pst = self.ps[self.pp % 2][:]
                self.pp += 1
                for j in range(CONV_K):
                    rhs = aT[:, c, tt * TT + j:tt * TT + j + TT]
                    sch.op("pe", lambda e, j=j, rhs=rhs, pst=pst, Dg=Dg: e.matmul(
                        pst, lhsT=Dg[:, j, :], rhs=rhs, start=(j == 0), stop=(j == CONV_K - 1)),
                        reads=[Dg[:, j, :], rhs], writes=[pst], signal=(j == CONV_K - 1))
                yc = y[:, c, tt * TT:(tt + 1) * TT]
                sch.op("act", lambda e, yc=yc, pst=pst, c=c: e.activation(
                    out=yc, in_=pst, func=AF.Identity, bias=cbv[:, c:c + 1], scale=1.0),
                    reads=[pst, cbv[:, c:c + 1]], writes=[yc])
        if self.stop <= 3:
            return
        o = T0
        yb, o = self.carve(o, [4, TT], BF16)
        ysq, o = self.carve(o, [4, TT], BF16)
        mu, o = self.carve(o, [TT], F32)
        var, o = self.carve(o, [TT], F32)
        for tt in range(S // TT):
            ysl = y[:, :, tt * TT:(tt + 1) * TT]
            s0 = self.ps[2 + 2 * (tt % 2)][:]
            s1 = self.ps[3 + 2 * (tt % 2)][:]
            sch.op("act", lambda e, ysl=ysl: e.activation(out=yb, in_=ysl, func=AF.Copy), reads=[ysl], writes=[yb])
            sch.op("act", lambda e, ysl=ysl: e.activation(out=ysq, in_=ysl, func=AF.Square), reads=[ysl], writes=[ysq])
            for (src, st) in ((yb, s0), (ysq, s1)):
                for c in range(4):
                    sch.op("pe", lambda e, c=c, src=src, st=st: e.matmul(
                        st, lhsT=self.ones[:], rhs=src[:, c, :], start=(c == 0), stop=(c == 3)),
                        reads=[self.ones[:], src[:, c, :]], writes=[st], signal=(c == 3))
            sch.op("act", lambda e, s0=s0: e.activation(out=mu, in_=s0, func=AF.Copy, scale=1.0 / 512.0),
                   reads=[s0], writes=[mu])
            sch.op("dve", lambda e: e.tensor_tensor(out=var, in0=mu, in1=mu, op=ALU.mult), reads=[mu], writes=[var])
            sch.op("dve", lambda e, s1=s1: e.scalar_tensor_tensor(
                out=var, in0=s1, scalar=1.0 / 512.0, in1=var, op0=ALU.mult, op1=ALU.subtract),
                reads=[s1, var], writes=[var])
            sch.op("act", lambda e: e.activation(out=var, in_=var, func=AF.Sqrt, bias=float(EPS), scale=1.0),
                   reads=[var], writes=[var])
            sch.op("dve", lambda e: e.reciprocal(out=var, in_=var), reads=[var], writes=[var])
            for c in range(4):
                yc = y[:, c, tt * TT:(tt + 1) * TT]
                sch.op("dve", lambda e, yc=yc: e.tensor_tensor(out=yc, in0=yc, in1=mu, op=ALU.subtract),
                       reads=[yc, mu], writes=[yc])
                sch.op("dve", lambda e, yc=yc: e.tensor_tensor(out=yc, in0=yc, in1=var, op=ALU.mult),
                       reads=[yc, var], writes=[yc])
                dst = coT[:, c, tt * TT:(tt + 1) * TT]
                sch.op("act", lambda e, yc=yc, dst=dst, c=c: e.activation(
                    out=dst, in_=yc, func=AF.Silu, bias=lbv[:, c:c + 1], scale=lgv[:, c:c + 1]),
                    reads=[yc, lbv[:, c:c + 1], lgv[:, c:c + 1]], writes=[dst])
        if self.stop <= 4:
            return
        o = T0
        ets, pTs = [], []
        for _ in range(4):
            v, o = self.carve(o, [TT], F32)
            ets.append(v)
        for _ in range(4):
            v, o = self.carve(o, [TT], BF16)
            pTs.append(v)
        den, o = self.carve(o, [4, 128], F32)
        esk, o = self.carve(o, [4], F32)
        assert o <= ARENA
        sch.op("act", lambda e: e.activation(out=esk, in_=skv, func=AF.Exp), reads=[skv], writes=[esk])
        Mv = mswa.rearrange("p (r h b q) -> p r h b q", r=4, h=2, b=2)
        it = 0
        for n in range(min(S // 128, self.kn)):
            A = self.ps[4 + (n % 2) * 2][:]
            B = self.ps[5 + (n % 2) * 2][:]
            kbs = [1] if n == 0 else [0, 1]
            b0 = kbs[0]
            for g in range(2):
                par = it % 2
                it += 1
                Sb = [self.ps[2 * par + hs][:].rearrange("p (r b q) -> p r b q", r=2, b=2) for hs in range(2)]
                etv = [ets[2 * par + hs].rearrange("p (r b q) -> p r b q", r=2, b=2) for hs in range(2)]
                pTv = [pTs[2 * par + hs].rearrange("p (r b q) -> p r b q", r=2, b=2) for hs in range(2)]
                for prl in range(2):
                    pr = 2 * g + prl
                    for bs in kbs:
                        kb = n - 1 + bs
                        for hs in range(2):
                            r0 = hs * 64
                            lh = kdT[r0:r0 + 64, g, kb * 128:(kb + 1) * 128]
                            rh = qT[r0:r0 + 64, pr, n * 128:(n + 1) * 128]
                            po = Sb[hs][:, prl, bs, :]
                            sch.op("pe", lambda e, lh=lh, rh=rh, po=po: e.matmul(po, lhsT=lh, rhs=rh, start=True, stop=True),
                                   reads=[lh, rh], writes=[po], signal=(prl == 1 and bs == kbs[-1]))
                if self.kswa <= 1:
                    continue
                for hs in range(2):
                    si, eo, po_ = Sb[hs][:, :, b0:, :], etv[hs][:, :, b0:, :], pTv[hs][:, :, b0:, :]
                    mi = Mv[:, 2 * g:2 * g + 2, hs, b0:, :]
                    sch.op("act", lambda e, si=si, eo=eo: e.activation(out=eo, in_=si, func=AF.Exp, scale=0.125),
                           reads=[si], writes=[eo])
                    if self.kswa <= 2:
                        continue
                    sch.op("dve", lambda e, eo=eo, po_=po_, mi=mi: e.tensor_tensor(out=po_, in0=eo, in1=mi, op=ALU.mult),
                           reads=[eo, mi], writes=[po_])
                if self.kswa <= 3:
                    continue
                for prl in range(2):
                    pr = 2 * g + prl
                    for hs in range(2):
                        r0 = hs * 64
                        for bs in kbs:
                            kb = n - 1 + bs
                            rh = pTv[hs][:, prl, bs, :]
                            lv = vT[:, kb, g * 64:(g + 1) * 64]
                            pa = A[r0:r0 + 64, pr * 128:(pr + 1) * 128]
                            pb = B[r0:r0 + 64, pr * 128:(pr + 1) * 128]
                            last = (prl == 1 and hs == 1 and bs == kbs[-1])
                            sch.op("pe", lambda e, lv=lv, rh=rh, pa=pa, bs=bs: e.matmul(
                                pa, lhsT=lv, rhs=rh, start=(bs == kbs[0]), stop=(bs == kbs[-1])),
                                reads=[lv, rh], writes=[pa], signal=False)
                            sch.op("pe", lambda e, rh=rh, pb=pb, bs=bs: e.matmul(
                                pb, lhsT=self.ones[:, 0:64], rhs=rh, start=(bs == kbs[0]), stop=(bs == kbs[-1])),
                                reads=[self.ones[:, 0:64], rh], writes=[pb], signal=last)
            if self.kswa <= 4:
                continue
            Bv = B.rearrange("p (r q) -> p r q", r=4)
            Av = A.rearrange("p (r q) -> p r q", r=4)
            ebc = esk.unsqueeze(2).to_broadcast([128, 4, 128])
            sch.op("dve", lambda e, Bv=Bv, ebc=ebc: e.tensor_tensor(out=den, in0=Bv, in1=ebc, op=ALU.add),
                   reads=[B, esk], writes=[den])
            sch.op("dve", lambda e: e.reciprocal(out=den, in_=den), reads=[den], writes=[den])
            dst = boT[:, :, n * 128:(n + 1) * 128]
            sch.op("dve", lambda e, Av=Av, dst=dst: e.tensor_tensor(out=dst, in0=Av, in1=den, op=ALU.mult),
                   reads=[A, den], writes=[dst])
        if self.stop <= 5:
            return
        o = T0
        sqh = []
        for _ in range(2):
            v, o = self.carve(o, [TT], BF16)
            sqh.append(v)
        rs2, o = self.carve(o, [TT], F32)
        self.proj_out_post(l, lambda k, tt: (coT if k < 4 else boT)[:, k % 4, tt * TT:(tt + 1) * TT],
                           hout, sqh, rs2)

    def plan_odd(self, l):
        w = self.win[l].rearrange("(k p) c -> p k c", p=128)
        V = self.wp.view
        self.wp.add([(lambda s_: V(s_, 0, [KC, 16]), w[:, :, 3072:3088])])
        for pr in range(8):
            self.wp.add([
                (lambda s_: V(s_, 0, [KC, 128]), w[:, :, pr * 128:(pr + 1) * 128]),
                (lambda s_: V(s_, 1024, [KC, 128]), w[:, :, 1024 + pr * 128:1024 + (pr + 1) * 128]),
                (lambda s_: V(s_, 2048, [KC, 128]), w[:, :, 2048 + pr * 128:2048 + (pr + 1) * 128])])
        self.plan_proj_out(self.wout[l])

    def mixer_odd(self, l):
        sch, wp = self.sch, self.wp
        i = l // 2
        bfv = self.cst[0:16, DEPTH * 6 * KC + 280 + i:DEPTH * 6 * KC + 280 + i + 1]
        K32 = 32 * 1024
        hT = self.carve(0, [KC, S], BF16)[0]
        hout = self.carve(0, [KC, TT], F32)[0]
        OT, off = self.carve(K32, [KC, S], BF16)
        P0 = off
        qTp, off = self.carve(off, [S], BF16)
        kTp, off = self.carve(off, [S], BF16)
        vaug, off = self.carve(off, [16, 192], BF16)
        negc, off = self.carve(off, [16, 16], F32)
        cTb, off = self.carve(off, [S], BF16)
        pTs = []
        for _ in range(10):
            v, off = self.carve(off, [TT], BF16)
            pTs.append(v)
        rec, off = self.carve(off, [TT], F32)
        sel, off = self.carve(off, [2048], BF16)
        assert off <= ARENA, off
        toks = [None, 0]
        sch.dma("pool", self.ds_c2, sel[0:16, :], self.sel_d, toks, sb_writes=[sel[0:16, :]])
        sch.dma("pool", self.ds_c2, sel[64:80, :], self.sel_d, toks, sb_writes=[sel[64:80, :]])
        sqb, o = self.carve(P0, [KC, TT], BF16)
        rs, o = self.carve(o, [TT], F32)
        lf = self.carve(P0, [S], F32)[0]
        cpT = self.carve(K32, [S], F32)[0]
        nbf, _ = self.carve(K32 + S * 4, [1], F32)
        self.prenorm_full(l, 2, hT, sqb, rs)
        tF = wp.get()
        wf = wp.view(tF[1], 0, [KC, 16])
        sch.op("dve", lambda e: e.tensor_scalar(out=nbf[0:16, :], in0=bfv, scalar1=-1.0, scalar2=None, op0=ALU.mult),
               reads=[bfv], writes=[nbf[0:16, :]])
        for tt in range(S // TT):
            pst = self.ps[self.pp % 2][:]
            self.pp += 1
            for k in range(KC):
                rhs = hT[:, k, tt * TT:(tt + 1) * TT]
                sch.op("pe", lambda e, k=k, rhs=rhs, pst=pst: e.matmul(
                    pst[0:16, :], lhsT=wf[:, k, :], rhs=rhs, start=(k == 0), stop=(k == KC - 1)),
                    reads=[wf[:, k, :], rhs], writes=[pst[0:16, :]], signal=(k == KC - 1))
            dst = lf[0:16, tt * TT:(tt + 1) * TT]
            sch.op("act", lambda e, pst=pst, dst=dst: e.activation(
                out=dst, in_=pst[0:16, :], func=AF.Exp, bias=nbf[0:16, :], scale=-1.0),
                reads=[pst[0:16, :], nbf[0:16, :]], writes=[dst])
        wp.release(tF[0])
        sch.op("act", lambda e: e.activation(out=lf[0:16, :], in_=lf[0:16, :], func=AF.Ln, bias=1.0, scale=1.0),
               reads=[lf[0:16, :]], writes=[lf[0:16, :]])
        sch.op("dve", lambda e: e.tensor_tensor_scan(out=cpT[0:16, :], data0=lf[0:16, :], data1=lf[0:16, :],
                                                     initial=0.0, op0=ALU.add, op1=ALU.bypass),
               reads=[lf[0:16, :]], writes=[cpT[0:16, :]])
        pst = self.ps[7][:]
        for kb in range(16):
            po = pst[:, kb * 16:(kb + 1) * 16]
            src = cpT[0:16, kb * 128:(kb + 1) * 128]
            sch.op("pe", lambda e, po=po, src=src: e.transpose(po, src, self.ident[:]),
                   reads=[src, self.ident[:]], writes=[po], signal=(kb == 15))
        sch.op("act", lambda e: e.activation(out=negc, in_=pst[:, 0:256].rearrange("p (a b) -> p a b", a=16), func=AF.Copy),
               reads=[pst[:, 0:256]], writes=[negc])
        sch.op("dve", lambda e: e.tensor_scalar(out=cTb[0:16, :], in0=cpT[0:16, :], scalar1=-8.0, scalar2=None, op0=ALU.mult),
               reads=[cpT[0:16, :]], writes=[cTb[0:16, :]])
        sch.op("dve", lambda e: e.tensor_copy(out=cTb[64:80, :], in_=cTb[0:16, :]),
               reads=[cTb[0:16, :]], writes=[cTb[64:80, :]])
        sch.op("dve", lambda e: e.memset(vaug[:, :, 64:128], 1.0), writes=[vaug[:, :, 64:128]])
        selv = sel.rearrange("p (h c) -> p h c", h=16)
        it = 0
        for pr in range(8):
            tp = wp.get()
            wq = wp.view(tp[1], 0, [KC, 128])
            wk = wp.view(tp[1], 1024, [KC, 128])
            wv_ = wp.view(tp[1], 2048, [KC, 128])
            self.fm_chunk(wq, hT, lambda tt, p: sch.op(
                "act", lambda e: e.activation(out=qTp[:, tt * TT:(tt + 1) * TT], in_=p, func=AF.Copy),
                reads=[p], writes=[qTp[:, tt * TT:(tt + 1) * TT]]))
            self.fm_chunk(wk, hT, lambda tt, p: sch.op(
                "dve", lambda e: e.tensor_copy(out=kTp[:, tt * TT:(tt + 1) * TT], in_=p),
                reads=[p], writes=[kTp[:, tt * TT:(tt + 1) * TT]]))
            for tb4 in range(4):
                pst = self.ps[self.pp % 2][:]
                self.pp += 1
                for jb in range(4):
                    tb = tb4 * 4 + jb
                    po = pst[:, jb * 128:(jb + 1) * 128]
                    for k in range(KC):
                        lh = hT[:, k, tb * 128:(tb + 1) * 128]
                        sch.op("pe", lambda e, k=k, lh=lh, po=po: e.matmul(
                            po, lhsT=lh, rhs=wv_[:, k, :], start=(k == 0), stop=(k == KC - 1)),
                            reads=[lh, wv_[:, k, :]], writes=[po], signal=(k == KC - 1))
                srcv = pst.rearrange("p (a b) -> p a b", a=4)
                d0 = vaug[:, tb4 * 4:(tb4 + 1) * 4, 0:64]
                d1 = vaug[:, tb4 * 4:(tb4 + 1) * 4, 128:192]
                sch.op("act", lambda e, d0=d0, srcv=srcv: e.activation(out=d0, in_=srcv[:, :, 0:64], func=AF.Copy),
                       reads=[pst], writes=[d0])
                sch.op("dve", lambda e, d1=d1, srcv=srcv: e.tensor_copy(out=d1, in_=srcv[:, :, 64:128]),
                       reads=[pst], writes=[d1])
            wp.release(tp[0])
            tiles = []
            for qc in range(4):
                nkb = 4 * qc + 4
                for kb in range(nkb):
                    tiles.append((qc, kb, nkb))
            LA = 4
            slots = {}
            for i in range(len(tiles) + LA):
                if i < len(tiles):
                    qc, kb, nkb = tiles[i]
                    j = kb - 4 * qc
                    q0 = max(0, j) * 128
                    cur = []
                    for hs in range(2):
                        Sb = self.ps[it % 4][:]
                        pT = pTs[it % 10]
                        it += 1
                        cur.append((Sb, pT))
                    slots[i] = (cur, q0)
                    for hs in range(2):
                        r0 = hs * 64
                        Sb = cur[hs][0]
                        lh = kTp[r0:r0 + 64, kb * 128:(kb + 1) * 128]
                        rh = qTp[r0:r0 + 64, qc * TT + q0:(qc + 1) * TT]
                        sch.op("pe", lambda e, lh=lh, rh=rh, Sb=Sb, q0=q0: e.matmul(
                            Sb[:, q0:TT], lhsT=lh, rhs=rh, start=True, stop=False),
                            reads=[lh, rh], writes=[Sb[:, q0:TT]], signal=False)
                    for hs in range(2):
                        r0 = hs * 64
                        h = 2 * pr + hs
                        Sb = cur[hs][0]
                        ls = selv[r0:r0 + 16, h, :]
                        rc = cTb[r0:r0 + 16, qc * TT + q0:(qc + 1) * TT]
                        sch.op("pe", lambda e, ls=ls, rc=rc, Sb=Sb, q0=q0, j=j: e.matmul(
                            Sb[:, q0:TT], lhsT=ls, rhs=rc, start=False, stop=(j < 0)),
                            reads=[ls, rc], writes=[Sb[:, q0:TT]], signal=(j < 0))
                    if j >= 0:
                        for hs in range(2):
                            Sb = cur[hs][0]
                            sch.op("pe", lambda e, Sb=Sb, q0=q0: e.matmul(
                                Sb[:, q0:q0 + 128], lhsT=self.identb[:], rhs=self.mcb[:], start=False, stop=True),
                                reads=[self.identb[:], self.mcb[:]], writes=[Sb[:, q0:q0 + 128]], signal=True)
                    for hs in range(2):
                        h = 2 * pr + hs
                        Sb, pT = cur[hs]
                        nb = negc[:, kb, h:h + 1]
                        sch.op("act", lambda e, Sb=Sb, pT=pT, nb=nb, q0=q0: e.activation(
                            out=pT[:, q0:TT], in_=Sb[:, q0:TT], func=AF.Exp, bias=nb, scale=0.125),
                            reads=[Sb[:, q0:TT], nb], writes=[pT[:, q0:TT]])
                if i - LA >= 0:
                    qc, kb, nkb = tiles[i - LA]
                    cur, q0 = slots.pop(i - LA)
                    for hs in range(2):
                        pT = cur[hs][1]
                        O = self.ps[4 + 2 * (qc % 2) + hs][:]
                        lv = vaug[:, kb, hs * 64:hs * 64 + 128]
                        sch.op("pe", lambda e, lv=lv, pT=pT, O=O, q0=q0, kb=kb, nkb=nkb: e.matmul(
                            O[:, q0:TT], lhsT=lv, rhs=pT[:, q0:TT], start=(kb == 0), stop=(kb == nkb - 1)),
                            reads=[lv, pT[:, q0:TT]], writes=[O[:, q0:TT]], signal=(kb == nkb - 1))
                    if kb == nkb - 1:
                        for hs in range(2):
                            O = self.ps[4 + 2 * (qc % 2) + hs][:]
                            nr = hs * 64
                            dr = 64 - nr
                            sch.op("dve", lambda e, O=O, nr=nr, dr=dr: e.reciprocal(out=rec[nr:nr + 64, :], in_=O[dr:dr + 64, :]),
                                   reads=[O[dr:dr + 64, :]], writes=[rec[nr:nr + 64, :]])
                            dst = OT[nr:nr + 64, pr, qc * TT:(qc + 1) * TT]
                            sch.op("dve", lambda e, O=O, nr=nr, dst=dst: e.tensor_tensor(
                                out=dst, in0=O[nr:nr + 64, :], in1=rec[nr:nr + 64, :], op=ALU.mult),
                                reads=[O[nr:nr + 64, :], rec[nr:nr + 64, :]], writes=[dst])
        sqh = []
        o = P0
        for _ in range(2):
            v, o = self.carve(o, [TT], BF16)
            sqh.append(v)
        rs2, o = self.carve(o, [TT], F32)
        self.proj_out_post(l, lambda k, tt: OT[:, k, tt * TT:(tt + 1) * TT], hout, sqh, rs2)


_CACHE = {}


def _get_prog(layers, subs):
    key = (tuple(layers), tuple(subs))
    if key not in _CACHE:
        p = Prog(layers, subs)
        p.build()
        _CACHE[key] = p
    return _CACHE[key]


def _consts(inputs):
    ng = np.asarray(inputs["norm_g"], dtype=np.float32)
    cst = np.zeros((128, NCST), np.float32)
    cst[:, 0:DEPTH * 6 * KC] = ng.reshape(DEPTH, 6, KC, 128).transpose(3, 0, 1, 2).reshape(128, -1)
    base = DEPTH * 6 * KC
    for i in range(2):
        b0 = base + i * 140
        cw = np.asarray(inputs["conv_w"][i], np.float32)
        cst[:, b0:b0 + 124] = cw.reshape(CONV_K, 4, 128).transpose(2, 1, 0).reshape(128, 124)
        cst[:, b0 + 124:b0 + 128] = np.asarray(inputs["conv_b"][i], np.float32).reshape(4, 128).T
        cst[:, b0 + 128:b0 + 132] = np.asarray(inputs["conv_ln_g"][i], np.float32).reshape(4, 128).T
        cst[:, b0 + 132:b0 + 136] = np.asarray(inputs["conv_ln_b"][i], np.float32).reshape(4, 128).T
        sk = np.asarray(inputs["swa_sinks"][i], np.float32)
        cst[0:64, b0 + 136:b0 + 140] = sk[0::2][None, :]
        cst[64:128, b0 + 136:b0 + 140] = sk[1::2][None, :]
        cst[0:16, base + 280 + i] = np.asarray(inputs["fox_b_f"][i], np.float32)
    k = np.arange(128)[:, None].astype(np.float64)
    q = np.arange(128)[None, :].astype(np.float64)
    mswa = np.zeros((128, 8, 2, 128), np.float64)
    for h in range(8):
        slope = 2.0 ** (-(h + 1))
        d0 = 128 + q - k
        mswa[:, h, 0, :] = np.exp(-slope * d0) * (q < k)
        d1 = q - k
        mswa[:, h, 1, :] = np.exp(-slope * d1) * (q >= k)
    mc = np.where(q >= k, 0.0, NEGBIG).astype(np.float32)
    sel = np.zeros((16, 16, 128), np.float32)
    for h in range(16):
        sel[h, h, :] = 1.0
    return {"cst": cst, "mswa": mswa.reshape(128, 2048).astype(np.float32), "mc": mc,
            "sel": sel.reshape(16, 2048), "ident": np.eye(16, dtype=np.float32),
            "identb": np.eye(128, dtype=np.float32)}


def _run(inputs, layers, subs=("f1", "mix", "f2"), x_override=None):
    x = np.asarray(inputs["x"], dtype=np.float32) if x_override is None else x_override
    shared = _consts(inputs)
    f32c = lambda a: np.ascontiguousarray(a, dtype=np.float32)
    for l in layers:
        for j, sub in ((0, "f1"), (1, "f2")):
            if sub in subs:
                shared[f"wg{l}{j}"] = f32c(inputs["ffn_w_gate"][l, j])
                shared[f"wu{l}{j}"] = f32c(inputs["ffn_w_up"][l, j])
                shared[f"wd{l}{j}"] = f32c(inputs["ffn_w_down"][l, j])
        if "mix" in subs:
            if l % 2 == 0:
                shared[f"win{l}"] = f32c(inputs["ab_w_in"][l // 2])
                shared[f"wout{l}"] = f32c(inputs["ab_w_out"][l // 2])
            else:
                shared[f"win{l}"] = f32c(inputs["fox_w_in"][l // 2])
                shared[f"wout{l}"] = f32c(inputs["fox_w_out"][l // 2])
    p = _get_prog(layers, subs)
    in_maps = []
    for b in range(NCORES):
        m = dict(shared)
        m["xT"] = np.ascontiguousarray(x[b].T)
        in_maps.append(m)
    import time as _t
    _t0 = _t.time()
    import os as _os
    if _os.environ.get("KTRACE"):
        res = run_bass_kernel_spmd(p.nc, in_maps, core_ids=list(range(NCORES)), trace=True)
        print("[kernel] exec_time_ns", res.exec_time_ns, flush=True)
    else:
        res = run_bass_kernel_spmd(p.nc, in_maps, core_ids=list(range(NCORES)))
    print("[kernel] launch wall s", round(_t.time() - _t0, 1), flush=True)
    out = np.stack([np.ascontiguousarray(r["yT"].T) for r in res.results], axis=0)
    return out.astype(np.float32)


LAUNCH_GROUPS = [[0, 1, 2, 3]]


def kernel(**inputs):
    x = None
    for grp in LAUNCH_GROUPS:
        x = _run(inputs, grp, x_override=x)
    return x
```

```python
import contextlib
import numpy as np
import concourse.bass as bass
import concourse.mybir as mybir
from concourse.bass_utils import run_bass_kernel_spmd

F32 = mybir.dt.float32
BF16 = mybir.dt.bfloat16
AF = mybir.ActivationFunctionType
ALU = mybir.AluOpType
ESZ = {F32: 4, BF16: 2}

D = 1024
S = 2048
DFF = 2816
KC = D // 128
FCH = DFF // 128
DEPTH = 4
EPS = 1e-6
NCORES = 8
TT = 512
TG = 1024
SLOT = 6144
NSLOT = 3
ARENA = 104 * 1024 - 512
NCST = 474
HD = 64
CONV_K = 31
NEGBIG = -30000.0


def _esz(dt):
    return ESZ[dt]


class Sched:
    def __init__(self, nc, es):
        self.nc = nc
        self.es = es
        self.eng = {}
        for name, obj in (("pe", nc.tensor), ("act", nc.scalar), ("dve", nc.vector),
                          ("pool", nc.gpsimd), ("sp", nc.sync)):
            sem = es.enter_context(nc.semaphore("sem_" + name))
            self.eng[name] = dict(obj=obj, sem=sem, cnt=0, waited={})
        self.recs = {}
        self.nwaits = 0
        self.nops = 0

    @staticmethod
    def region(ap):
        name = ap.tensor.name
        a = ap.ap
        esz = _esz(ap.dtype)
        pstep, pcnt = a[0]
        off = ap.offset
        if pstep > 0:
            p0 = off // pstep
            lo = off % pstep
        else:
            p0 = 0
            lo = off
        hi = lo + 1
        for st, c in a[1:]:
            hi += (c - 1) * abs(st)
        return (name, p0, p0 + pcnt, lo * esz, hi * esz)

    @staticmethod
    def _ov(r, q):
        return r[1] < q[2] and q[1] < r[2] and r[3] < q[4] and q[3] < r[4]

    @staticmethod
    def _contains(outer, inner):
        return (outer[1] <= inner[1] and inner[2] <= outer[2]
                and outer[3] <= inner[3] and inner[4] <= outer[4])

    def _collect(self, reads, writes):
        deps = {}

        def add(tok):
            h, v = tok[0], tok[1]
            k = h.name
            if k not in deps or deps[k][1] < v:
                deps[k] = (h, v)

        rregs = [self.region(a) for a in reads]
        wregs = [self.region(a) for a in writes]
        for r in rregs:
            rec = self.recs.get(r[0])
            if rec:
                for q, tok in rec["w"]:
                    if self._ov(r, q):
                        add(tok)
        for r in wregs:
            rec = self.recs.get(r[0])
            if rec:
                for q, tok in rec["w"]:
                    if self._ov(r, q):
                        add(tok)
                for q, tok in rec["r"]:
                    if self._ov(r, q):
                        add(tok)
        return deps, rregs, wregs

    def _emit_waits(self, E, deps, skip_self):
        need = []
        for k, (h, v) in deps.items():
            if skip_self and h is E["sem"]:
                continue
            if E["waited"].get(k, 0) >= v:
                continue
            need.append((k, h, v))
        for (k, h, v) in need[:-1]:
            E["obj"].wait_ge(h, v)
            E["waited"][k] = v
            self.nwaits += 1
        if need:
            k, h, v = need[-1]
            E["waited"][k] = v
            return (h, v)
        return None

    def _record(self, rregs, wregs, tok):
        for r in rregs:
            rec = self.recs.setdefault(r[0], {"w": [], "r": []})
            found = False
            for i, (q, t) in enumerate(rec["r"]):
                if q == r and t[0] is tok[0]:
                    if t[1] < tok[1]:
                        rec["r"][i] = (q, tok)
                    found = True
                    break
            if not found:
                rec["r"].append((r, tok))
        for r in wregs:
            rec = self.recs.setdefault(r[0], {"w": [], "r": []})
            rec["w"] = [(q, t) for (q, t) in rec["w"] if not self._contains(r, q)]
            rec["r"] = [(q, t) for (q, t) in rec["r"] if not self._contains(r, q)]
            rec["w"].append((r, tok))

    def op(self, eng, fn, reads=(), writes=(), signal=True, extra=()):
        E = self.eng[eng]
        deps, rregs, wregs = self._collect(reads, writes)
        for tok in extra:
            k = tok[0].name
            if k not in deps or deps[k][1] < tok[1]:
                deps[k] = (tok[0], tok[1])
        w = self._emit_waits(E, deps, skip_self=(eng == "pe"))
        ins = fn(E["obj"])
        if w is not None:
            ins._wait_ge(w[0], w[1])
        self.nops += 1
        if signal:
            E["cnt"] += 1
            ins.then_inc(E["sem"], 1)
            tok = [E["sem"], E["cnt"]]
        else:
            tok = [E["sem"], E["cnt"] + 1]
        self._record(rregs, wregs, tok)
        return tok

    def new_dsem(self, name):
        sem = self.es.enter_context(self.nc.semaphore(name))
        return dict(sem=sem, cnt=0)

    def dma(self, queue, ds, out, in_, tok, sb_reads=(), sb_writes=(), extra=()):
        E = self.eng[queue]
        deps, rregs, wregs = self._collect(sb_reads, sb_writes)
        for t in extra:
            k = t[0].name
            if k not in deps or deps[k][1] < t[1]:
                deps[k] = (t[0], t[1])
        w = self._emit_waits(E, deps, skip_self=False)
        ins = E["obj"].dma_start(out=out, in_=in_)
        if w is not None:
            ins._wait_ge(w[0], w[1])
        ins.then_inc(ds["sem"], 16)
        ds["cnt"] += 16
        tok[0] = ds["sem"]
        tok[1] = ds["cnt"]
        self._record(rregs, wregs, tok)
        self.nops += 1


class WeightPool:
    def __init__(self, sch, ring, nslot, slot_elems):
        self.sch = sch
        self.ring = ring
        self.nslot = nslot
        self.slot = slot_elems
        self.plan = []
        self.next_load = 0
        self.next_get = 0
        self.free = list(range(nslot))
        self.slot_of = {}
        self.dsems = [sch.new_dsem(f"wsem{i}") for i in range(nslot)]

    def add(self, tile):
        self.plan.append(tile)

    def view(self, slot, off, shape):
        n = 1
        for s in shape:
            n *= s
        base = slot * self.slot + off
        v = self.ring[:, base:base + n]
        if len(shape) == 2:
            v = v.rearrange("p (a b) -> p a b", a=shape[0])
        elif len(shape) == 3:
            v = v.rearrange("p (a b c) -> p a b c", a=shape[0], b=shape[1])
        return v

    def prefetch(self):
        while self.free and self.next_load < len(self.plan):
            slot = self.free.pop(0)
            idx = self.next_load
            self.next_load += 1
            self.slot_of[idx] = slot
            tok = [None, 0]
            for (dst_fn, src) in self.plan[idx]:
                dst = dst_fn(slot)
                self.sch.dma("pool", self.dsems[slot], dst, src, tok, sb_writes=[dst])

    def get(self):
        self.prefetch()
        idx = self.next_get
        self.next_get += 1
        if idx not in self.slot_of:
            self.prefetch()
        assert idx in self.slot_of, "weight pool starved (release missing?)"
        return idx, self.slot_of[idx]

    def release(self, idx):
        self.free.append(self.slot_of[idx])
        self.prefetch()


class Prog:
    def __init__(self, layers, subs=("f1", "mix", "f2")):
        self.layers = list(layers)
        self.subs = subs
        self.nc = bass.Bass("TRN2", target_bir_lowering=False)
        self.es = contextlib.ExitStack()
        self.pp = 0
        import os
        self.stop = int(os.environ.get('KSTOP', '99'))
        self.kswa = int(os.environ.get('KSWA', '99'))
        self.kn = int(os.environ.get('KN', '99'))

    def dram_in(self, name, shape):
        return self.nc.dram_tensor(name, list(shape), F32, kind="ExternalInput").ap()

    def build(self):
        nc = self.nc
        with self.es as es:
            self.sch = Sched(nc, es)
            sch = self.sch
            self.xT = self.dram_in("xT", [D, S])
            self.cst_d = self.dram_in("cst", [128, NCST])
            self.mswa_d = self.dram_in("mswa", [128, 2048])
            self.mc_d = self.dram_in("mc", [128, 128])
            self.sel_d = self.dram_in("sel", [16, 2048])
            self.ident_d = self.dram_in("ident", [16, 16])
            self.identb_d = self.dram_in("identb", [128, 128])
            self.win, self.wout = {}, {}
            if "mix" in self.subs:
                for l in self.layers:
                    self.win[l] = self.dram_in(f"win{l}", [D, 1792 if l % 2 == 0 else 3088])
                    self.wout[l] = self.dram_in(f"wout{l}", [D, D])
            self.wg, self.wu, self.wd = {}, {}, {}
            for l in self.layers:
                for j, sub in ((0, "f1"), (1, "f2")):
                    if sub in self.subs:
                        self.wg[l, j] = self.dram_in(f"wg{l}{j}", [D, DFF])
                        self.wu[l, j] = self.dram_in(f"wu{l}{j}", [D, DFF])
                        self.wd[l, j] = self.dram_in(f"wd{l}{j}", [DFF, D])
            self.yT = nc.dram_tensor("yT", [D, S], F32, kind="ExternalOutput").ap()
            sb = lambda n, shp, dt: es.enter_context(nc.sbuf_tensor(n, shp, dt))
            self.xs = sb("xs", [128, KC, S], F32)
            self.ring = sb("ring", [128, NSLOT * SLOT], BF16)
            self.cst = sb("cst_sb", [128, NCST], F32)
            self.gT = self.cst[:, 0:DEPTH * 6 * KC]
            self.identb = sb("identb_sb", [128, 128], BF16)
            self.mc = sb("mc_sb", [128, 128], F32)
            self.ident = sb("ident_sb", [16, 16], F32)
            self.mcb = sb("mcb_sb", [128, 128], BF16)
            self.g32 = sb("g32", [128, DEPTH * 6 * KC], F32)
            self.ones = sb("ones", [128, 128], BF16)
            self.arena = sb("arena", [128, ARENA], mybir.dt.uint8)
            self.ps = [es.enter_context(nc.psum_tensor(f"ps{i}", [128, TT], F32)) for i in range(8)]
            self.wp = WeightPool(sch, self.ring, NSLOT, SLOT)
            self.ds_x = sch.new_dsem("ds_x")
            self.ds_c = sch.new_dsem("ds_c")
            self.ds_c2 = sch.new_dsem("ds_c2")
            self.ds_o = sch.new_dsem("ds_o")

            for l in self.layers:
                for sub in self.subs:
                    if sub == "f1":
                        self.plan_ffn(l, 0)
                    elif sub == "f2":
                        self.plan_ffn(l, 1)
                    elif sub == "mix":
                        (self.plan_even if l % 2 == 0 else self.plan_odd)(l)

            tok = [None, 0]
            for k in range(KC):
                sch.dma("sp", self.ds_x, self.xs[:, k, :], self.xT[k * 128:(k + 1) * 128, :], tok,
                        sb_writes=[self.xs[:, k, :]])
            tok = [None, 0]
            sch.dma("sp", self.ds_c, self.cst[:], self.cst_d, tok, sb_writes=[self.cst[:]])
            sch.dma("sp", self.ds_c, self.mc[:], self.mc_d, tok, sb_writes=[self.mc[:]])
            sch.dma("sp", self.ds_c, self.ident[:], self.ident_d, tok, sb_writes=[self.ident[:]])
            tok = [None, 0]
            sch.dma("pool", self.ds_c2, self.identb[:], self.identb_d, tok, sb_writes=[self.identb[:]])
            sch.op("dve", lambda e: e.memset(self.ones[:], 1.0), writes=[self.ones[:]])
            sch.op("dve", lambda e: e.tensor_scalar(out=self.mcb[:], in0=self.mc[:], scalar1=8.0, scalar2=None,
                                                    op0=ALU.mult), reads=[self.mc[:]], writes=[self.mcb[:]])
            self.wp.prefetch()
            gv = self.gT.rearrange("p (l n k) -> p l n k", l=DEPTH, n=6)
            g32v = self.g32[:].rearrange("p (l n k) -> p l n k", l=DEPTH, n=6)
            for n in range(6):
                f = 16.0 if n in (1, 5) else 32.0
                sch.op("dve", lambda e, n=n, f=f: e.tensor_scalar(
                    out=g32v[:, :, n, :], in0=gv[:, :, n, :], scalar1=f, scalar2=None, op0=ALU.mult),
                    reads=[gv[:, :, n, :]], writes=[g32v[:, :, n, :]])

            for l in self.layers:
                for sub in self.subs:
                    if sub == "f1":
                        self.ffn(l, 0)
                    elif sub == "f2":
                        self.ffn(l, 1)
                    elif sub == "mix":
                        (self.mixer_even if l % 2 == 0 else self.mixer_odd)(l)

            tok = [None, 0]
            for k in range(KC):
                sch.dma("sp", self.ds_o, self.yT[k * 128:(k + 1) * 128, :], self.xs[:, k, :], tok,
                        sb_reads=[self.xs[:, k, :]])
            nc.sync.wait_ge(self.ds_o["sem"], self.ds_o["cnt"])
        return nc

    def gcol(self, l, n, k):
        c = (l * 6 + n) * KC + k
        return self.g32[:, c:c + 1]

    def carve(self, off, shape, dt):
        n = 1
        for s in shape:
            n *= s
        nb = n * _esz(dt)
        v = self.arena[:, off:off + nb].bitcast(dt)
        if len(shape) == 2:
            v = v.rearrange("p (a b) -> p a b", a=shape[0])
        elif len(shape) == 3:
            v = v.rearrange("p (a b c) -> p a b c", a=shape[0], b=shape[1])
        return v, off + nb

    GU_STAGES = [(0, 3), (3, 6), (6, 9), (9, 12), (12, 15), (15, 18), (18, 21), (21, 22)]

    def plan_ffn(self, l, j):
        wg = self.wg[l, j].rearrange("(k p) c -> p k c", p=128)
        wu = self.wu[l, j].rearrange("(k p) c -> p k c", p=128)
        wd = self.wd[l, j].rearrange("(f p) c -> p f c", p=128)
        for g in range(S // TG):
            for (c0, c1) in self.GU_STAGES:
                n = (c1 - c0) * 128
                self.wp.add([(lambda s_, n=n: self.wp.view(s_, 0, [KC, n]), wg[:, :, c0 * 128:c1 * 128]),
                             (lambda s_, n=n: self.wp.view(s_, KC * n, [KC, n]), wu[:, :, c0 * 128:c1 * 128])])
            for tt in range(TG // TT):
                for dp in range(4):
                    self.wp.add([(lambda s_: self.wp.view(s_, 0, [FCH, 256]), wd[:, :, dp * 256:(dp + 1) * 256])])

    def rstd_from_stats(self, st_ps, rs):
        self.sch.op("act", lambda e: e.activation(out=rs, in_=st_ps, func=AF.Sqrt, bias=float(D * EPS), scale=1.0),
                    reads=[st_ps], writes=[rs])
        self.sch.op("dve", lambda e: e.reciprocal(out=rs, in_=rs), reads=[rs], writes=[rs])

    def prenorm_tile(self, l, n, t0, hdst, sqb, rs, st_ps):
        sch = self.sch
        xin = self.xs[:, :, t0:t0 + TT]
        sch.op("act", lambda e: e.activation(out=sqb, in_=xin, func=AF.Square), reads=[xin], writes=[sqb])
        for k in range(KC):
            sch.op("pe", lambda e, k=k: e.matmul(st_ps, lhsT=self.ones[:], rhs=sqb[:, k, :],
                                                 start=(k == 0), stop=(k == KC - 1)),
                   reads=[self.ones[:], sqb[:, k, :]], writes=[st_ps], signal=(k == KC - 1))
        self.rstd_from_stats(st_ps, rs)
        for k in range(KC):
            xi = self.xs[:, k, t0:t0 + TT]
            sch.op("dve", lambda e, k=k, xi=xi: e.scalar_tensor_tensor(
                out=hdst[:, k, :], in0=xi, scalar=self.gcol(l, n, k), in1=rs, op0=ALU.mult, op1=ALU.mult),
                reads=[xi, self.gcol(l, n, k), rs], writes=[hdst[:, k, :]])

    def postnorm_update(self, l, n, t0, hout, rs):
        sch = self.sch
        for k in range(KC):
            xi = self.xs[:, k, t0:t0 + TT]
            hk = hout[:, k, :]
            sch.op("dve", lambda e, k=k, hk=hk: e.scalar_tensor_tensor(
                out=hk, in0=hk, scalar=self.gcol(l, n, k), in1=rs, op0=ALU.mult, op1=ALU.mult),
                reads=[hk, self.gcol(l, n, k), rs], writes=[hk])
            sch.op("dve", lambda e, xi=xi, hk=hk: e.tensor_tensor(out=xi, in0=xi, in1=hk, op=ALU.add),
                   reads=[xi, hk], writes=[xi])

    def ffn(self, l, j):
        sch = self.sch
        wp = self.wp
        n_pre, n_post = (0, 1) if j == 0 else (4, 5)
        ntt = TG // TT
        ng = S // TG
        off = 0
        hTs, houts = [], []
        for _ in range(2):
            houts.append(self.arena[:, off:off + KC * TT * 4].bitcast(F32).rearrange("p (a b) -> p a b", a=KC))
            v, off = self.carve(off, [KC, TG], BF16)
            hTs.append(v)
        actT, off = self.carve(off, [FCH, TG], BF16)
        sqb, off = self.carve(off, [KC, TT], BF16)
        rs, off = self.carve(off, [TT], F32)
        rs2, off = self.carve(off, [TT], F32)
        sgt = []
        for i in range(2):
            v, off = self.carve(off, [TT], BF16)
            sgt.append(v)
        sqh = []
        for i in range(2):
            v, off = self.carve(off, [TT], BF16)
            sqh.append(v)
        assert off <= ARENA
        psG = [self.ps[0][:], self.ps[1][:]]
        psU = [self.ps[2][:], self.ps[3][:]]
        psD = [self.ps[4][:], self.ps[5][:]]
        stp = self.ps[6][:]
        stq = self.ps[7][:]
        it = 0

        def prenorm(g, tt):
            hT = hTs[g % 2]
            self.prenorm_tile(l, n_pre, g * TG + tt * TT, hT[:, :, tt * TT:(tt + 1) * TT], sqb, rs, stq)

        for tt in range(ntt):
            prenorm(0, tt)
        for g in range(ng):
            g0 = g * TG
            hT = hTs[g % 2]
            hout = houts[g % 2]
            for (c0, c1) in self.GU_STAGES:
                idx, slot = wp.get()
                n = (c1 - c0) * 128
                wgv = wp.view(slot, 0, [KC, n])
                wuv = wp.view(slot, KC * n, [KC, n])
                order = [(c, tt) for c in range(c0, c1) for tt in range(ntt)]
                if g == 0 and c0 == 0:
                    order = [(c, tt) for tt in range(ntt) for c in range(c0, c1)]
                for (c, tt) in order:
                    cl = (c - c0) * 128
                    if True:
                        b = it % 2
                        it += 1
                        hsl = hT[:, :, tt * TT:(tt + 1) * TT]
                        for (wv, pst) in ((wgv, psG[b]), (wuv, psU[b])):
                            for k in range(KC):
                                sch.op("pe", lambda e, k=k, wv=wv, pst=pst: e.matmul(
                                    pst, lhsT=wv[:, k, cl:cl + 128], rhs=hsl[:, k, :],
                                    start=(k == 0), stop=(k == KC - 1)),
                                    reads=[wv[:, k, cl:cl + 128], hsl[:, k, :]], writes=[pst],
                                    signal=(k == KC - 1))
                        sch.op("act", lambda e, b=b: e.activation(out=sgt[b], in_=psG[b], func=AF.Silu),
                               reads=[psG[b]], writes=[sgt[b]])
                        dst = actT[:, c, tt * TT:(tt + 1) * TT]
                        sch.op("dve", lambda e, b=b, dst=dst: e.tensor_tensor(
                            out=dst, in0=psU[b], in1=sgt[b], op=ALU.mult),
                            reads=[psU[b], sgt[b]], writes=[dst])
                wp.release(idx)
            for tt in range(ntt):
                t0 = g0 + tt * TT
                asl = actT[:, :, tt * TT:(tt + 1) * TT]
                pend = None
                for dp in range(4):
                    idx, slot = wp.get()
                    wdv = wp.view(slot, 0, [FCH, 256])
                    for dd in range(2):
                        dc = dp * 2 + dd
                        b = dc % 2
                        for f in range(FCH):
                            sch.op("pe", lambda e, f=f, b=b, dd=dd: e.matmul(
                                psD[b], lhsT=wdv[:, f, dd * 128:(dd + 1) * 128], rhs=asl[:, f, :],
                                start=(f == 0), stop=(f == FCH - 1)),
                                reads=[wdv[:, f, dd * 128:(dd + 1) * 128], asl[:, f, :]], writes=[psD[b]],
                                signal=(f == FCH - 1))
                        if pend is not None:
                            pend()
                        hk = hout[:, dc, :]
                        sch.op("act", lambda e, b=b, hk=hk: e.activation(out=hk, in_=psD[b], func=AF.Copy),
                               reads=[psD[b]], writes=[hk])
                        sch.op("act", lambda e, b=b: e.activation(out=sqh[b], in_=psD[b], func=AF.Square),
                               reads=[psD[b]], writes=[sqh[b]])
                        pend = (lambda b=b, dc=dc: sch.op("pe", lambda e: e.matmul(
                            stp, lhsT=self.ones[:], rhs=sqh[b], start=(dc == 0), stop=(dc == KC - 1)),
                            reads=[self.ones[:], sqh[b]], writes=[stp], signal=True))
                    wp.release(idx)
                    if g + 1 < ng and tt == 0 and dp < ntt:
                        prenorm(g + 1, dp)
                pend()
                self.rstd_from_stats(stp, rs2)
                self.postnorm_update(l, n_post, t0, hout, rs2)

    def plan_proj_out(self, w):
        wv = w.rearrange("(k p) c -> p k c", p=128)
        for hh in range(2):
            self.wp.add([(lambda s_: self.wp.view(s_, 0, [KC, 512]), wv[:, :, hh * 512:(hh + 1) * 512])])

    def fm_chunk(self, wv, hT, evac):
        sch = self.sch
        for tt in range(S // TT):
            pst = self.ps[self.pp % 2][:]
            self.pp += 1
            for k in range(KC):
                rhs = hT[:, k, tt * TT:(tt + 1) * TT]
                sch.op("pe", lambda e, k=k, rhs=rhs, pst=pst: e.matmul(
                    pst, lhsT=wv[:, k, :], rhs=rhs, start=(k == 0), stop=(k == KC - 1)),
                    reads=[wv[:, k, :], rhs], writes=[pst], signal=(k == KC - 1))
            evac(tt, pst)

    def prenorm_full(self, l, n, hT, sqb, rs):
        for tt in range(S // TT):
            self.prenorm_tile(l, n, tt * TT, hT[:, :, tt * TT:(tt + 1) * TT], sqb, rs, self.ps[6 + tt % 2][:])

    def proj_out_post(self, l, rhs_fn, hout, sqh, rs2):
        sch, wp = self.sch, self.wp
        t1 = wp.get()
        t2 = wp.get()
        psD = [self.ps[0][:], self.ps[1][:]]
        stp = [self.ps[2][:], self.ps[3][:]]
        for tt in range(S // TT):
            st = stp[tt % 2]
            pend = None
            for dc in range(KC):
                tile = t1 if dc < 4 else t2
                wv = wp.view(tile[1], 0, [KC, 512])
                b = dc % 2
                c0 = (dc % 4) * 128
                for k in range(KC):
                    rhs = rhs_fn(k, tt)
                    sch.op("pe", lambda e, k=k, rhs=rhs, wv=wv, b=b, c0=c0: e.matmul(
                        psD[b], lhsT=wv[:, k, c0:c0 + 128], rhs=rhs, start=(k == 0), stop=(k == KC - 1)),
                        reads=[wv[:, k, c0:c0 + 128], rhs], writes=[psD[b]], signal=(k == KC - 1))
                if pend is not None:
                    pend()
                hk = hout[:, dc, :]
                sch.op("act", lambda e, b=b, hk=hk: e.activation(out=hk, in_=psD[b], func=AF.Copy),
                       reads=[psD[b]], writes=[hk])
                sch.op("act", lambda e, b=b: e.activation(out=sqh[b], in_=psD[b], func=AF.Square),
                       reads=[psD[b]], writes=[sqh[b]])
                pend = (lambda b=b, dc=dc, st=st: sch.op("pe", lambda e: e.matmul(
                    st, lhsT=self.ones[:], rhs=sqh[b], start=(dc == 0), stop=(dc == KC - 1)),
                    reads=[self.ones[:], sqh[b]], writes=[st], signal=True))
            pend()
            self.rstd_from_stats(st, rs2)
            self.postnorm_update(l, 3, tt * TT, hout, rs2)
        wp.release(t1[0])
        wp.release(t2[0])

    def plan_even(self, l):
        w = self.win[l].rearrange("(k p) c -> p k c", p=128)
        V = self.wp.view
        self.wp.add([(lambda s_: V(s_, 0, [KC, 768]), w[:, :, 0:768])])
        self.wp.add([(lambda s_: V(s_, 0, [KC, 768]), w[:, :, 768:1536])])
        t3 = []
        for g in range(2):
            for hh in range(2):
                t3.append((lambda s_, g=g, hh=hh: V(s_, g * 1024, [KC, 128])[:, :, hh * 64:(hh + 1) * 64],
                           w[:, :, 1536 + g * 64:1536 + (g + 1) * 64]))
        t3.append((lambda s_: V(s_, 2048, [KC, 128]), w[:, :, 1664:1792]))
        self.wp.add(t3)
        self.plan_proj_out(self.wout[l])

    def mixer_even(self, l):
        sch, wp = self.sch, self.wp
        i = l // 2
        cb0 = DEPTH * 6 * KC + i * 140
        cst = self.cst
        cwv = cst[:, cb0:cb0 + 124].rearrange("p (c j) -> p c j", c=4)
        cbv = cst[:, cb0 + 124:cb0 + 128]
        lgv = cst[:, cb0 + 128:cb0 + 132]
        lbv = cst[:, cb0 + 132:cb0 + 136]
        skv = cst[:, cb0 + 136:cb0 + 140]
        K32 = 32 * 1024
        hT = self.carve(0, [KC, S], BF16)[0]
        y = self.carve(0, [4, S], F32)[0]
        boT = self.carve(0, [4, S], BF16)[0]
        hout = self.carve(16 * 1024, [KC, TT], F32)[0]
        aT, off = self.carve(K32, [4, S + 30], BF16)
        coT = self.carve(K32, [4, S], BF16)[0]
        off = (off + 63) // 64 * 64
        qT, off = self.carve(off, [4, S], BF16)
        kdT, off = self.carve(off, [2, S], BF16)
        vT, off = self.carve(off, [16, 128], BF16)
        mswa, off = self.carve(off, [2048], BF16)
        T0 = off
        assert T0 + 16 * 1024 <= ARENA, T0
        tokm = [None, 0]
        sch.dma("pool", self.ds_c2, mswa, self.mswa_d, tokm, sb_writes=[mswa])
        sqb, o = self.carve(T0, [KC, TT], BF16)
        rs, o = self.carve(o, [TT], F32)
        self.prenorm_full(l, 2, hT, sqb, rs)
        if self.stop <= 1:
            return
        sgs = []
        o = T0
        for _ in range(2):
            v, o = self.carve(o, [S], BF16)
            sgs.append(v)
        t1 = wp.get()
        t2 = wp.get()

        def wview(col):
            tile = t1 if col < 768 else t2
            v = wp.view(tile[1], 0, [KC, 768])
            lc = col % 768
            return v[:, :, lc:lc + 128]

        for c in range(4):
            sch.op("dve", lambda e, c=c: e.memset(aT[:, c, 0:30], 0.0), writes=[aT[:, c, 0:30]])
        for c in range(4):
            sg = sgs[c % 2]
            self.fm_chunk(wview(512 + 128 * c), hT, lambda tt, p, sg=sg: sch.op(
                "act", lambda e: e.activation(out=sg[:, tt * TT:(tt + 1) * TT], in_=p, func=AF.Sigmoid),
                reads=[p], writes=[sg[:, tt * TT:(tt + 1) * TT]]))
            self.fm_chunk(wview(128 * c), hT, lambda tt, p, sg=sg, c=c: sch.op(
                "dve", lambda e: e.tensor_tensor(out=aT[:, c, 30 + tt * TT:30 + (tt + 1) * TT], in0=p,
                                                 in1=sg[:, tt * TT:(tt + 1) * TT], op=ALU.mult),
                reads=[p, sg[:, tt * TT:(tt + 1) * TT]], writes=[aT[:, c, 30 + tt * TT:30 + (tt + 1) * TT]]))
        wp.release(t1[0])
        for pr in range(4):
            self.fm_chunk(wview(1024 + 128 * pr), hT, lambda tt, p, pr=pr: sch.op(
                "act", lambda e: e.activation(out=qT[:, pr, tt * TT:(tt + 1) * TT], in_=p, func=AF.Copy),
                reads=[p], writes=[qT[:, pr, tt * TT:(tt + 1) * TT]]))
        wp.release(t2[0])
        t3 = wp.get()
        for g in range(2):
            wv = wp.view(t3[1], g * 1024, [KC, 128])
            self.fm_chunk(wv, hT, lambda tt, p, g=g: sch.op(
                "dve", lambda e: e.tensor_copy(out=kdT[:, g, tt * TT:(tt + 1) * TT], in_=p),
                reads=[p], writes=[kdT[:, g, tt * TT:(tt + 1) * TT]]))
        wvv = wp.view(t3[1], 2048, [KC, 128])
        for tb4 in range(4):
            pst = self.ps[self.pp % 2][:]
            self.pp += 1
            for jb in range(4):
                tb = tb4 * 4 + jb
                po = pst[:, jb * 128:(jb + 1) * 128]
                for k in range(KC):
                    lh = hT[:, k, tb * 128:(tb + 1) * 128]
                    sch.op("pe", lambda e, k=k, lh=lh, po=po: e.matmul(
                        po, lhsT=lh, rhs=wvv[:, k, :], start=(k == 0), stop=(k == KC - 1)),
                        reads=[lh, wvv[:, k, :]], writes=[po], signal=(k == KC - 1))
            dst = vT[:, tb4 * 4:(tb4 + 1) * 4, :]
            src = pst.rearrange("p (a b) -> p a b", a=4)
            sch.op("act", lambda e, dst=dst, src=src: e.activation(out=dst, in_=src, func=AF.Copy),
                   reads=[pst], writes=[dst])
        wp.release(t3[0])
        if self.stop <= 2:
            return
        Dgs = []
        o = T0
        for _ in range(2):
            v, o = self.carve(o, [CONV_K, 128], BF16)
            Dgs.append(v)
        for c in range(4):
            Dg = Dgs[c % 2]
            idb = self.identb[:].unsqueeze(1).to_broadcast([128, CONV_K, 128])
            cwb = cwv[:, c, :].unsqueeze(2).to_broadcast([128, CONV_K, 128])
            sch.op("dve", lambda e, Dg=Dg, idb=idb, cwb=cwb: e.tensor_tensor(out=Dg, in0=idb, in1=cwb, op=ALU.mult),
                   reads=[self.identb[:], cwv[:, c, :]], writes=[Dg])
            for tt in range(S // TT):
                pst = self.ps[self.pp % 2][:]
                self.pp += 1
                for j in range(CONV_K):
                    rhs = aT[:, c, tt * TT + j:tt * TT + j + TT]
                    sch.op("pe", lambda e, j=j, rhs=rhs, pst=pst, Dg=Dg: e.matmul(
                        pst, lhsT=Dg[:, j, :], rhs=rhs, start=(j == 0), stop=(j == CONV_K - 1)),
                        reads=[Dg[:, j, :], rhs], writes=[pst], signal=(j == CONV_K - 1))
                yc = y[:, c, tt * TT:(tt + 1) * TT]
                sch.op("act", lambda e, yc=yc, pst=pst, c=c: e.activation(
                    out=yc, in_=pst, func=AF.Identity, bias=cbv[:, c:c + 1], scale=1.0),
                    reads=[pst, cbv[:, c:c + 1]], writes=[yc])
        if self.stop <= 3:
            return
        o = T0
        yb, o = self.carve(o, [4, TT], BF16)
        ysq, o = self.carve(o, [4, TT], BF16)
        mu, o = self.carve(o, [TT], F32)
        var, o = self.carve(o, [TT], F32)
        for tt in range(S // TT):
            ysl = y[:, :, tt * TT:(tt + 1) * TT]
            s0 = self.ps[2 + 2 * (tt % 2)][:]
            s1 = self.ps[3 + 2 * (tt % 2)][:]
            sch.op("act", lambda e, ysl=ysl: e.activation(out=yb, in_=ysl, func=AF.Copy), reads=[ysl], writes=[yb])
            sch.op("act", lambda e, ysl=ysl: e.activation(out=ysq, in_=ysl, func=AF.Square), reads=[ysl], writes=[ysq])
            for (src, st) in ((yb, s0), (ysq, s1)):
                for c in range(4):
                    sch.op("pe", lambda e, c=c, src=src, st=st: e.matmul(
                        st, lhsT=self.ones[:], rhs=src[:, c, :], start=(c == 0), stop=(c == 3)),
                        reads=[self.ones[:], src[:, c, :]], writes=[st], signal=(c == 3))
            sch.op("act", lambda e, s0=s0: e.activation(out=mu, in_=s0, func=AF.Copy, scale=1.0 / 512.0),
                   reads=[s0], writes=[mu])
            sch.op("dve", lambda e: e.tensor_tensor(out=var, in0=mu, in1=mu, op=ALU.mult), reads=[mu], writes=[var])
            sch.op("dve", lambda e, s1=s1: e.scalar_tensor_tensor(
                out=var, in0=s1, scalar=1.0 / 512.0, in1=var, op0=ALU.mult, op1=ALU.subtract),
                reads=[s1, var], writes=[var])
            sch.op("act", lambda e: e.activation(out=var, in_=var, func=AF.Sqrt, bias=float(EPS), scale=1.0),
                   reads=[var], writes=[var])
            sch.op("dve", lambda e: e.reciprocal(out=var, in_=var), reads=[var], writes=[var])
            for c in range(4):
                yc = y[:, c, tt * TT:(tt + 1) * TT]
                sch.op("dve", lambda e, yc=yc: e.tensor_tensor(out=yc, in0=yc, in1=mu, op=ALU.subtract),
                       reads=[yc, mu], writes=[yc])
                sch.op("dve", lambda e, yc=yc: e.tensor_tensor(out=yc, in0=yc, in1=var, op=ALU.mult),
                       reads=[yc, var], writes=[yc])
                dst = coT[:, c, tt * TT:(tt + 1) * TT]
                sch.op("act", lambda e, yc=yc, dst=dst, c=c: e.activation(
                    out=dst, in_=yc, func=AF.Silu, bias=lbv[:, c:c + 1], scale=lgv[:, c:c + 1]),
                    reads=[yc, lbv[:, c:c + 1], lgv[:, c:c + 1]], writes=[dst])
        if self.stop <= 4:
            return
        o = T0
        ets, pTs = [], []
        for _ in range(4):
            v, o = self.carve(o, [TT], F32)
            ets.append(v)
        for _ in range(4):
            v, o = self.carve(o, [TT], BF16)
            pTs.append(v)
        den, o = self.carve(o, [4, 128], F32)
        esk, o = self.carve(o, [4], F32)
        assert o <= ARENA
        sch.op("act", lambda e: e.activation(out=esk, in_=skv, func=AF.Exp), reads=[skv], writes=[esk])
        Mv = mswa.rearrange("p (r h b q) -> p r h b q", r=4, h=2, b=2)
        it = 0
        pend_pv = None
        for n in range(min(S // 128, self.kn)):
            A = self.ps[4 + (n % 2) * 2][:]
            B = self.ps[5 + (n % 2) * 2][:]
            kbs = [1] if n == 0 else [0, 1]
            b0 = kbs[0]
            for g in range(2):
                par = it % 2
                it += 1
                Sb = [self.ps[2 * par + hs][:].rearrange("p (r b q) -> p r b q", r=2, b=2) for hs in range(2)]
                etv = [ets[2 * par + hs].rearrange("p (r b q) -> p r b q", r=2, b=2) for hs in range(2)]
                pTv = [pTs[2 * par + hs].rearrange("p (r b q) -> p r b q", r=2, b=2) for hs in range(2)]
                for prl in range(2):
                    pr = 2 * g + prl
                    for bs in kbs:
                        kb = n - 1 + bs
                        for hs in range(2):
                            r0 = hs * 64
                            lh = kdT[r0:r0 + 64, g, kb * 128:(kb + 1) * 128]
                            rh = qT[r0:r0 + 64, pr, n * 128:(n + 1) * 128]
                            po = Sb[hs][:, prl, bs, :]
                            sch.op("pe", lambda e, lh=lh, rh=rh, po=po: e.matmul(po, lhsT=lh, rhs=rh, start=True, stop=True),
                                   reads=[lh, rh], writes=[po], signal=(prl == 1 and bs == kbs[-1]))
                if self.kswa <= 1:
                    continue
                for hs in range(2):
                    si, eo, po_ = Sb[hs][:, :, b0:, :], etv[hs][:, :, b0:, :], pTv[hs][:, :, b0:, :]
                    mi = Mv[:, 2 * g:2 * g + 2, hs, b0:, :]
                    sch.op("act", lambda e, si=si, eo=eo: e.activation(out=eo, in_=si, func=AF.Exp, scale=0.125),
                           reads=[si], writes=[eo])
                    if self.kswa <= 2:
                        continue
                    sch.op("dve", lambda e, eo=eo, po_=po_, mi=mi: e.tensor_tensor(out=po_, in0=eo, in1=mi, op=ALU.mult),
                           reads=[eo, mi], writes=[po_])
                def pv_part(n=n, g=g, kbs=kbs, pTv=pTv, A=A, B=B):
                    for prl in range(2):
                        pr = 2 * g + prl
                        for hs in range(2):
                            r0 = hs * 64
                            for bs in kbs:
                                kb = n - 1 + bs
                                rh = pTv[hs][:, prl, bs, :]
                                lv = vT[:, kb, g * 64:(g + 1) * 64]
                                pa = A[r0:r0 + 64, pr * 128:(pr + 1) * 128]
                                pb = B[r0:r0 + 64, pr * 128:(pr + 1) * 128]
                                last = (prl == 1 and hs == 1 and bs == kbs[-1])
                                sch.op("pe", lambda e, lv=lv, rh=rh, pa=pa, bs=bs: e.matmul(
                                    pa, lhsT=lv, rhs=rh, start=(bs == kbs[0]), stop=(bs == kbs[-1])),
                                    reads=[lv, rh], writes=[pa], signal=False)
                                sch.op("pe", lambda e, rh=rh, pb=pb, bs=bs: e.matmul(
                                    pb, lhsT=self.ones[:, 0:64], rhs=rh, start=(bs == kbs[0]), stop=(bs == kbs[-1])),
                                    reads=[self.ones[:, 0:64], rh], writes=[pb], signal=last)
                    if g == 1:
                        Bv = B.rearrange("p (r q) -> p r q", r=4)
                        Av = A.rearrange("p (r q) -> p r q", r=4)
                        ebc = esk.unsqueeze(2).to_broadcast([128, 4, 128])
                        sch.op("dve", lambda e: e.tensor_tensor(out=den, in0=Bv, in1=ebc, op=ALU.add),
                               reads=[B, esk], writes=[den])
                        sch.op("dve", lambda e: e.reciprocal(out=den, in_=den), reads=[den], writes=[den])
                        dst = boT[:, :, n * 128:(n + 1) * 128]
                        sch.op("dve", lambda e: e.tensor_tensor(out=dst, in0=Av, in1=den, op=ALU.mult),
                               reads=[A, den], writes=[dst])

                if pend_pv is not None:
                    pend_pv()
                pend_pv = pv_part
        if pend_pv is not None:
            pend_pv()
        if self.stop <= 5:
            return
        o = T0
        sqh = []
        for _ in range(2):
            v, o = self.carve(o, [TT], BF16)
            sqh.append(v)
        rs2, o = self.carve(o, [TT], F32)
        self.proj_out_post(l, lambda k, tt: (coT if k < 4 else boT)[:, k % 4, tt * TT:(tt + 1) * TT],
                           hout, sqh, rs2)

    def plan_odd(self, l):
        w = self.win[l].rearrange("(k p) c -> p k c", p=128)
        V = self.wp.view
        self.wp.add([(lambda s_: V(s_, 0, [KC, 16]), w[:, :, 3072:3088])])
        for pr in range(8):
            self.wp.add([
                (lambda s_: V(s_, 0, [KC, 128]), w[:, :, pr * 128:(pr + 1) * 128]),
                (lambda s_: V(s_, 1024, [KC, 128]), w[:, :, 1024 + pr * 128:1024 + (pr + 1) * 128]),
                (lambda s_: V(s_, 2048, [KC, 128]), w[:, :, 2048 + pr * 128:2048 + (pr + 1) * 128])])
        self.plan_proj_out(self.wout[l])

    def mixer_odd(self, l):
        sch, wp = self.sch, self.wp
        i = l // 2
        bfv = self.cst[0:16, DEPTH * 6 * KC + 280 + i:DEPTH * 6 * KC + 280 + i + 1]
        K32 = 32 * 1024
        hT = self.carve(0, [KC, S], BF16)[0]
        hout = self.carve(0, [KC, TT], F32)[0]
        OT, off = self.carve(K32, [KC, S], BF16)
        P0 = off
        qTp, off = self.carve(off, [S], BF16)
        kTp, off = self.carve(off, [S], BF16)
        vaug, off = self.carve(off, [16, 192], BF16)
        negc, off = self.carve(off, [16, 16], F32)
        cTb, off = self.carve(off, [S], BF16)
        pTs = []
        for _ in range(10):
            v, off = self.carve(off, [TT], BF16)
            pTs.append(v)
        rec, off = self.carve(off, [TT], F32)
        sel, off = self.carve(off, [2048], BF16)
        assert off <= ARENA, off
        toks = [None, 0]
        sch.dma("pool", self.ds_c2, sel[0:16, :], self.sel_d, toks, sb_writes=[sel[0:16, :]])
        sch.dma("pool", self.ds_c2, sel[64:80, :], self.sel_d, toks, sb_writes=[sel[64:80, :]])
        sqb, o = self.carve(P0, [KC, TT], BF16)
        rs, o = self.carve(o, [TT], F32)
        lf = self.carve(P0, [S], F32)[0]
        cpT = self.carve(K32, [S], F32)[0]
        nbf, _ = self.carve(K32 + S * 4, [1], F32)
        self.prenorm_full(l, 2, hT, sqb, rs)
        tF = wp.get()
        wf = wp.view(tF[1], 0, [KC, 16])
        sch.op("dve", lambda e: e.tensor_scalar(out=nbf[0:16, :], in0=bfv, scalar1=-1.0, scalar2=None, op0=ALU.mult),
               reads=[bfv], writes=[nbf[0:16, :]])
        for tt in range(S // TT):
            pst = self.ps[self.pp % 2][:]
            self.pp += 1
            for k in range(KC):
                rhs = hT[:, k, tt * TT:(tt + 1) * TT]
                sch.op("pe", lambda e, k=k, rhs=rhs, pst=pst: e.matmul(
                    pst[0:16, :], lhsT=wf[:, k, :], rhs=rhs, start=(k == 0), stop=(k == KC - 1)),
                    reads=[wf[:, k, :], rhs], writes=[pst[0:16, :]], signal=(k == KC - 1))
            dst = lf[0:16, tt * TT:(tt + 1) * TT]
            sch.op("act", lambda e, pst=pst, dst=dst: e.activation(
                out=dst, in_=pst[0:16, :], func=AF.Exp, bias=nbf[0:16, :], scale=-1.0),
                reads=[pst[0:16, :], nbf[0:16, :]], writes=[dst])
        wp.release(tF[0])
        sch.op("act", lambda e: e.activation(out=lf[0:16, :], in_=lf[0:16, :], func=AF.Ln, bias=1.0, scale=1.0),
               reads=[lf[0:16, :]], writes=[lf[0:16, :]])
        sch.op("dve", lambda e: e.tensor_tensor_scan(out=cpT[0:16, :], data0=lf[0:16, :], data1=lf[0:16, :],
                                                     initial=0.0, op0=ALU.add, op1=ALU.bypass),
               reads=[lf[0:16, :]], writes=[cpT[0:16, :]])
        pst = self.ps[7][:]
        for kb in range(16):
            po = pst[:, kb * 16:(kb + 1) * 16]
            src = cpT[0:16, kb * 128:(kb + 1) * 128]
            sch.op("pe", lambda e, po=po, src=src: e.transpose(po, src, self.ident[:]),
                   reads=[src, self.ident[:]], writes=[po], signal=(kb == 15))
        sch.op("act", lambda e: e.activation(out=negc, in_=pst[:, 0:256].rearrange("p (a b) -> p a b", a=16), func=AF.Copy),
               reads=[pst[:, 0:256]], writes=[negc])
        sch.op("dve", lambda e: e.tensor_scalar(out=cTb[0:16, :], in0=cpT[0:16, :], scalar1=-8.0, scalar2=None, op0=ALU.mult),
               reads=[cpT[0:16, :]], writes=[cTb[0:16, :]])
        sch.op("dve", lambda e: e.tensor_copy(out=cTb[64:80, :], in_=cTb[0:16, :]),
               reads=[cTb[0:16, :]], writes=[cTb[64:80, :]])
        sch.op("dve", lambda e: e.memset(vaug[:, :, 64:128], 1.0), writes=[vaug[:, :, 64:128]])
        selv = sel.rearrange("p (h c) -> p h c", h=16)
        it = 0
        for pr in range(8):
            tp = wp.get()
            wq = wp.view(tp[1], 0, [KC, 128])
            wk = wp.view(tp[1], 1024, [KC, 128])
            wv_ = wp.view(tp[1], 2048, [KC, 128])
            self.fm_chunk(wq, hT, lambda tt, p: sch.op(
                "act", lambda e: e.activation(out=qTp[:, tt * TT:(tt + 1) * TT], in_=p, func=AF.Copy),
                reads=[p], writes=[qTp[:, tt * TT:(tt + 1) * TT]]))
            self.fm_chunk(wk, hT, lambda tt, p: sch.op(
                "dve", lambda e: e.tensor_copy(out=kTp[:, tt * TT:(tt + 1) * TT], in_=p),
                reads=[p], writes=[kTp[:, tt * TT:(tt + 1) * TT]]))
            for tb4 in range(4):
                pst = self.ps[self.pp % 2][:]
                self.pp += 1
                for jb in range(4):
                    tb = tb4 * 4 + jb
                    po = pst[:, jb * 128:(jb + 1) * 128]
                    for k in range(KC):
                        lh = hT[:, k, tb * 128:(tb + 1) * 128]
                        sch.op("pe", lambda e, k=k, lh=lh, po=po: e.matmul(
                            po, lhsT=lh, rhs=wv_[:, k, :], start=(k == 0), stop=(k == KC - 1)),
                            reads=[lh, wv_[:, k, :]], writes=[po], signal=(k == KC - 1))
                srcv = pst.rearrange("p (a b) -> p a b", a=4)
                d0 = vaug[:, tb4 * 4:(tb4 + 1) * 4, 0:64]
                d1 = vaug[:, tb4 * 4:(tb4 + 1) * 4, 128:192]
                sch.op("act", lambda e, d0=d0, srcv=srcv: e.activation(out=d0, in_=srcv[:, :, 0:64], func=AF.Copy),
                       reads=[pst], writes=[d0])
                sch.op("dve", lambda e, d1=d1, srcv=srcv: e.tensor_copy(out=d1, in_=srcv[:, :, 64:128]),
                       reads=[pst], writes=[d1])
            wp.release(tp[0])
            tiles = []
            for qc in range(4):
                nkb = 4 * qc + 4
                for kb in range(nkb):
                    tiles.append((qc, kb, nkb))
            LA = 4
            slots = {}
            for i in range(len(tiles) + LA):
                if i < len(tiles):
                    qc, kb, nkb = tiles[i]
                    j = kb - 4 * qc
                    q0 = max(0, j) * 128
                    cur = []
                    for hs in range(2):
                        Sb = self.ps[it % 4][:]
                        pT = pTs[it % 10]
                        it += 1
                        cur.append((Sb, pT))
                    slots[i] = (cur, q0)
                    for hs in range(2):
                        r0 = hs * 64
                        Sb = cur[hs][0]
                        lh = kTp[r0:r0 + 64, kb * 128:(kb + 1) * 128]
                        rh = qTp[r0:r0 + 64, qc * TT + q0:(qc + 1) * TT]
                        sch.op("pe", lambda e, lh=lh, rh=rh, Sb=Sb, q0=q0: e.matmul(
                            Sb[:, q0:TT], lhsT=lh, rhs=rh, start=True, stop=False),
                            reads=[lh, rh], writes=[Sb[:, q0:TT]], signal=False)
                    for hs in range(2):
                        r0 = hs * 64
                        h = 2 * pr + hs
                        Sb = cur[hs][0]
                        ls = selv[r0:r0 + 16, h, :]
                        rc = cTb[r0:r0 + 16, qc * TT + q0:(qc + 1) * TT]
                        sch.op("pe", lambda e, ls=ls, rc=rc, Sb=Sb, q0=q0, j=j: e.matmul(
                            Sb[:, q0:TT], lhsT=ls, rhs=rc, start=False, stop=(j < 0)),
                            reads=[ls, rc], writes=[Sb[:, q0:TT]], signal=(j < 0))
                    if j >= 0:
                        for hs in range(2):
                            Sb = cur[hs][0]
                            sch.op("pe", lambda e, Sb=Sb, q0=q0: e.matmul(
                                Sb[:, q0:q0 + 128], lhsT=self.identb[:], rhs=self.mcb[:], start=False, stop=True),
                                reads=[self.identb[:], self.mcb[:]], writes=[Sb[:, q0:q0 + 128]], signal=True)
                    for hs in range(2):
                        h = 2 * pr + hs
                        Sb, pT = cur[hs]
                        nb = negc[:, kb, h:h + 1]
                        sch.op("act", lambda e, Sb=Sb, pT=pT, nb=nb, q0=q0: e.activation(
                            out=pT[:, q0:TT], in_=Sb[:, q0:TT], func=AF.Exp, bias=nb, scale=0.125),
                            reads=[Sb[:, q0:TT], nb], writes=[pT[:, q0:TT]])
                if i - LA >= 0:
                    qc, kb, nkb = tiles[i - LA]
                    cur, q0 = slots.pop(i - LA)
                    for hs in range(2):
                        pT = cur[hs][1]
                        O = self.ps[4 + 2 * (qc % 2) + hs][:]
                        lv = vaug[:, kb, hs * 64:hs * 64 + 128]
                        sch.op("pe", lambda e, lv=lv, pT=pT, O=O, q0=q0, kb=kb, nkb=nkb: e.matmul(
                            O[:, q0:TT], lhsT=lv, rhs=pT[:, q0:TT], start=(kb == 0), stop=(kb == nkb - 1)),
                            reads=[lv, pT[:, q0:TT]], writes=[O[:, q0:TT]], signal=(kb == nkb - 1))
                    if kb == nkb - 1:
                        for hs in range(2):
                            O = self.ps[4 + 2 * (qc % 2) + hs][:]
                            nr = hs * 64
                            dr = 64 - nr
                            sch.op("dve", lambda e, O=O, nr=nr, dr=dr: e.reciprocal(out=rec[nr:nr + 64, :], in_=O[dr:dr + 64, :]),
                                   reads=[O[dr:dr + 64, :]], writes=[rec[nr:nr + 64, :]])
                            dst = OT[nr:nr + 64, pr, qc * TT:(qc + 1) * TT]
                            sch.op("dve", lambda e, O=O, nr=nr, dst=dst: e.tensor_tensor(
                                out=dst, in0=O[nr:nr + 64, :], in1=rec[nr:nr + 64, :], op=ALU.mult),
                                reads=[O[nr:nr + 64, :], rec[nr:nr + 64, :]], writes=[dst])
        sqh = []
        o = P0
        for _ in range(2):
            v, o = self.carve(o, [TT], BF16)
            sqh.append(v)
        rs2, o = self.carve(o, [TT], F32)
        self.proj_out_post(l, lambda k, tt: OT[:, k, tt * TT:(tt + 1) * TT], hout, sqh, rs2)


_CACHE = {}


def _get_prog(layers, subs):
    key = (tuple(layers), tuple(subs))
    if key not in _CACHE:
        p = Prog(layers, subs)
        p.build()
        _CACHE[key] = p
    return _CACHE[key]


def _consts(inputs):
    ng = np.asarray(inputs["norm_g"], dtype=np.float32)
    cst = np.zeros((128, NCST), np.float32)
    cst[:, 0:DEPTH * 6 * KC] = ng.reshape(DEPTH, 6, KC, 128).transpose(3, 0, 1, 2).reshape(128, -1)
    base = DEPTH * 6 * KC
    for i in range(2):
        b0 = base + i * 140
        cw = np.asarray(inputs["conv_w"][i], np.float32)
        cst[:, b0:b0 + 124] = cw.reshape(CONV_K, 4, 128).transpose(2, 1, 0).reshape(128, 124)
        cst[:, b0 + 124:b0 + 128] = np.asarray(inputs["conv_b"][i], np.float32).reshape(4, 128).T
        cst[:, b0 + 128:b0 + 132] = np.asarray(inputs["conv_ln_g"][i], np.float32).reshape(4, 128).T
        cst[:, b0 + 132:b0 + 136] = np.asarray(inputs["conv_ln_b"][i], np.float32).reshape(4, 128).T
        sk = np.asarray(inputs["swa_sinks"][i], np.float32)
        cst[0:64, b0 + 136:b0 + 140] = sk[0::2][None, :]
        cst[64:128, b0 + 136:b0 + 140] = sk[1::2][None, :]
        cst[0:16, base + 280 + i] = np.asarray(inputs["fox_b_f"][i], np.float32)
    k = np.arange(128)[:, None].astype(np.float64)
    q = np.arange(128)[None, :].astype(np.float64)
    mswa = np.zeros((128, 8, 2, 128), np.float64)
    for h in range(8):
        slope = 2.0 ** (-(h + 1))
        d0 = 128 + q - k
        mswa[:, h, 0, :] = np.exp(-slope * d0) * (q < k)
        d1 = q - k
        mswa[:, h, 1, :] = np.exp(-slope * d1) * (q >= k)
    mc = np.where(q >= k, 0.0, NEGBIG).astype(np.float32)
    sel = np.zeros((16, 16, 128), np.float32)
    for h in range(16):
        sel[h, h, :] = 1.0
    return {"cst": cst, "mswa": mswa.reshape(128, 2048).astype(np.float32), "mc": mc,
            "sel": sel.reshape(16, 2048), "ident": np.eye(16, dtype=np.float32),
            "identb": np.eye(128, dtype=np.float32)}


def _run(inputs, layers, subs=("f1", "mix", "f2"), x_override=None):
    x = np.asarray(inputs["x"], dtype=np.float32) if x_override is None else x_override
    shared = _consts(inputs)
    f32c = lambda a: np.ascontiguousarray(a, dtype=np.float32)
    for l in layers:
        for j, sub in ((0, "f1"), (1, "f2")):
            if sub in subs:
                shared[f"wg{l}{j}"] = f32c(inputs["ffn_w_gate"][l, j])
                shared[f"wu{l}{j}"] = f32c(inputs["ffn_w_up"][l, j])
                shared[f"wd{l}{j}"] = f32c(inputs["ffn_w_down"][l, j])
        if "mix" in subs:
            if l % 2 == 0:
                shared[f"win{l}"] = f32c(inputs["ab_w_in"][l // 2])
                shared[f"wout{l}"] = f32c(inputs["ab_w_out"][l // 2])
            else:
                shared[f"win{l}"] = f32c(inputs["fox_w_in"][l // 2])
                shared[f"wout{l}"] = f32c(inputs["fox_w_out"][l // 2])
    p = _get_prog(layers, subs)
    in_maps = []
    for b in range(NCORES):
        m = dict(shared)
        m["xT"] = np.ascontiguousarray(x[b].T)
        in_maps.append(m)
    import time as _t
    _t0 = _t.time()
    import os as _os
    if _os.environ.get("KTRACE"):
        res = run_bass_kernel_spmd(p.nc, in_maps, core_ids=list(range(NCORES)), trace=True)
        print("[kernel] exec_time_ns", res.exec_time_ns, flush=True)
    else:
        res = run_bass_kernel_spmd(p.nc, in_maps, core_ids=list(range(NCORES)))
    print("[kernel] launch wall s", round(_t.time() - _t0, 1), flush=True)
    out = np.stack([np.ascontiguousarray(r["yT"].T) for r in res.results], axis=0)
    return out.astype(np.float32)


LAUNCH_GROUPS = [[0, 1, 2, 3]]


def kernel(**inputs):
    x = None
    for grp in LAUNCH_GROUPS:
        x = _run(inputs, grp, x_override=x)
    return x
```
